# Optimizing a Trainium2 kernel written in Bass

```python
import math
import jax, jax.numpy as jnp
from jax import lax
import numpy as np

D_MODEL = 1024
BATCH = 8
SEQ = 2048
DEPTH = 2
DEC_BATCH = 32
DEC_SEQ = 4
PAST_LEN = 16384
PAGE_SIZE = 128

GLA_HEADS = 4
GLA_DK = 64
GLA_DV = 128
GLA_K = GLA_HEADS * GLA_DK
GLA_V = GLA_HEADS * GLA_DV
GLA_LORA = 16
GLA_GATE_NORM = 16.0
GLA_CHUNK = 64
RWKV_HEADS = 8
RWKV_HD = 64
RWKV_W = RWKV_HEADS * RWKV_HD
RWKV_LORA_W = 64
RWKV_LORA_A = 64
RWKV_SPLITS = [RWKV_W, RWKV_W, RWKV_W, RWKV_LORA_W, RWKV_LORA_A, RWKV_W]
RWKV_IN = sum(RWKV_SPLITS)
RWKV_OFFS = [int(v) for v in np.cumsum(RWKV_SPLITS)[:-1]]
RWKV_LN_EPS = 64e-5
NSA_HEADS = 8
NSA_KV_HEADS = 2
NSA_GROUP = NSA_HEADS // NSA_KV_HEADS
NSA_HD = 64
NSA_Q = NSA_HEADS * NSA_HD
NSA_KVW = NSA_KV_HEADS * NSA_HD
CMP_BLOCK = 32
SEL_BLOCK = 64
CMP_PER_SEL = SEL_BLOCK // CMP_BLOCK
N_SELECT = 16
WINDOW = 512
Q_BLOCK = 64
N_KV_KINDS = 4
FORCE_SCORE = 1e9
BRANCH_W = 512
N_BRANCH = 3
DN_ALPHA = (2 * DEPTH) ** 0.25
DN_BETA = (8 * DEPTH) ** -0.25
LN_EPS = 1e-5
NORM_EPS = 1e-6

IN_SPLITS = [GLA_K, GLA_K, GLA_V, GLA_LORA, GLA_V,
             RWKV_IN,
             NSA_Q, 6 * NSA_KVW, 3 * NSA_HEADS, NSA_Q,
             N_BRANCH * D_MODEL]
IN_WIDTH = sum(IN_SPLITS)
IN_OFFS = [int(v) for v in np.cumsum(IN_SPLITS)[:-1]]

kernel_name = "gla_rwkv7_nsa_parallel_deepnorm_step"


def _layernorm(x, g, b):
    xf = x.astype(jnp.float32)
    mu = jnp.mean(xf, -1, keepdims=True)
    var = jnp.mean(jnp.square(xf - mu), -1, keepdims=True)
    return ((xf - mu) * lax.rsqrt(var + LN_EPS) * g + b).astype(x.dtype)


def _masked_softmax(s, mask, axis):
    s = jnp.where(mask, s.astype(jnp.float32), -jnp.inf)
    m = jnp.max(s, axis=axis, keepdims=True)
    m = jnp.where(jnp.isfinite(m), m, 0.0)
    p = jnp.exp(s - m)
    den = jnp.sum(p, axis=axis, keepdims=True)
    return p / jnp.where(den > 0, den, 1.0)


def _alibi_slopes():
    h = np.arange(1, NSA_HEADS + 1, dtype=np.float32)
    return jnp.asarray(2.0 ** (-8.0 * h / NSA_HEADS), jnp.float32).reshape(NSA_KV_HEADS, NSA_GROUP)


def gla_mixer(q, k, v, a_low, g, a_up, a_bias, norm_g, s0):
    f32 = jnp.float32
    B, T, _ = q.shape
    C = GLA_CHUNK if T % GLA_CHUNK == 0 else T
    n = T // C

    def heads(t, d):
        return t.astype(f32).reshape(B, n, C, GLA_HEADS, d).transpose(1, 0, 3, 2, 4)

    log_a = jax.nn.log_sigmoid((a_low @ a_up + a_bias).astype(f32)) / GLA_GATE_NORM
    xs = (heads(q, GLA_DK) * GLA_DK ** -0.5, heads(k, GLA_DK), heads(v, GLA_DV), heads(log_a, GLA_DK))
    causal = jnp.tril(jnp.ones((C, C), bool))

    def chunk_step(S, inp):
        qc, kc, vc, lc = inp
        cum = jnp.cumsum(lc, axis=2)
        o_inter = jnp.einsum('bhcd,bhde->bhce', qc * jnp.exp(cum), S)
        rel = jnp.where(causal[:, :, None], cum[:, :, :, None, :] - cum[:, :, None, :, :], -jnp.inf)
        att = jnp.einsum('bhid,bhijd->bhij', qc, jnp.exp(rel) * kc[:, :, None, :, :])
        o = o_inter + jnp.einsum('bhij,bhje->bhie', att, vc)
        last = cum[:, :, -1, :]
        S = jnp.exp(last)[..., None] * S + jnp.einsum('bhjd,bhje->bhde', kc * jnp.exp(last[:, :, None, :] - cum), vc)
        return S, o

    S, o = lax.scan(chunk_step, s0.astype(f32), xs)
    o = o.transpose(1, 0, 3, 2, 4).reshape(B, T, GLA_HEADS, GLA_DV)
    o = o * lax.rsqrt(jnp.mean(o * o, -1, keepdims=True) + NORM_EPS)
    o = o.reshape(B, T, GLA_V) * norm_g * jax.nn.silu(g.astype(f32))
    return o.astype(q.dtype), S.astype(s0.dtype)


def rwkv_mixer(z, z_prev0, mu, w0, w_up, a0, a_up, k_k, k_a, r_k, ln_w, ln_b, s0):
    f32 = jnp.float32
    B, T, _ = z.shape
    zf = z.astype(f32)
    z_prev = jnp.concatenate([z_prev0[:, None].astype(f32), zf[:, :-1]], axis=1)
    zs = zf + (z_prev - zf) * mu
    r, k, v, wl, al, g = jnp.split(zs, RWKV_OFFS, axis=-1)
    w = -jax.nn.softplus(-(w0 + jnp.tanh(wl) @ w_up)) - 0.5
    decay = jnp.exp(-jnp.exp(w))
    a = jax.nn.sigmoid(a0 + al @ a_up)
    hs = lambda t: t.reshape(B, T, RWKV_HEADS, RWKV_HD)
    kk = hs(k * k_k)
    kk = kk * lax.rsqrt(jnp.sum(kk * kk, -1, keepdims=True) + NORM_EPS)
    k = k * (1.0 + (a - 1.0) * k_a)
    r, k, v, a, decay = hs(r), hs(k), hs(v), hs(a), hs(decay)

    def step(S, inp):
        r_t, w_t, k_t, v_t, kk_t, a_t = inp
        sa = jnp.einsum('bhij,bhj->bhi', S, -kk_t)
        S = S * w_t[:, :, None, :] + sa[..., None] * (kk_t * a_t)[:, :, None, :] + v_t[..., None] * k_t[:, :, None, :]
        return S, jnp.einsum('bhij,bhj->bhi', S, r_t)

    tm = lambda t: jnp.swapaxes(t, 0, 1)
    S, y = lax.scan(step, s0.astype(f32), (tm(r), tm(decay), tm(k), tm(v), tm(kk), tm(a)))
    y = tm(y)
    mean = jnp.mean(y, -1, keepdims=True)
    var = jnp.mean(jnp.square(y - mean), -1, keepdims=True)
    y = ((y - mean) * lax.rsqrt(var + RWKV_LN_EPS)).reshape(B, T, RWKV_W) * ln_w + ln_b
    bonus = jnp.sum(r * k * r_k, -1, keepdims=True) * v
    y = (y + bonus.reshape(B, T, RWKV_W)) * jax.nn.silu(g)
    return y.astype(z.dtype), S.astype(s0.dtype), z[:, -1]


def nsa_compress(k_full):
    B, L = k_full.shape[:2]
    kf = k_full.astype(jnp.float32).reshape(B, L // CMP_BLOCK, CMP_BLOCK, NSA_KV_HEADS, NSA_HD)
    return jnp.mean(kf, 2).astype(k_full.dtype)


def nsa_blocks(k_full):
    B, L = k_full.shape[:2]
    return k_full.reshape(B, L // SEL_BLOCK, SEL_BLOCK, NSA_KV_HEADS, NSA_HD).transpose(0, 3, 1, 2, 4)


def nsa_attend(q, t_pos, kc, vc, c_end, ks_b, vs_b, kw, vw, w_pos, gates, slopes):
    f32 = jnp.float32
    B = q.shape[0]
    NS = ks_b.shape[2]
    tq = t_pos[:, None]
    dist_c = (tq - c_end[None, :]).astype(f32)
    s_c = jnp.einsum('bqkgd,bnkd->bqkgn', q, kc).astype(f32) - slopes[None, None, :, :, None] * dist_c[None, :, None, None, :]
    p_c = _masked_softmax(s_c, (c_end[None, :] <= tq)[None, :, None, None, :], -1)
    o_c = jnp.einsum('bqkgn,bnkd->bqkgd', p_c.astype(vc.dtype), vc)
    imp = jnp.sum(p_c, 3).reshape(p_c.shape[0], p_c.shape[1], NSA_KV_HEADS, NS, CMP_PER_SEL).sum(-1)
    blk = jnp.arange(NS)
    forced = (blk[None, :] == tq // SEL_BLOCK) | (blk[None, :] == 0)
    valid = blk[None, :] * SEL_BLOCK <= tq
    score = jnp.where(forced[None, :, None, :], FORCE_SCORE, jnp.where(valid[None, :, None, :], imp, -FORCE_SCORE))
    _, idx = lax.top_k(score, min(N_SELECT, NS))
    bi = jnp.arange(B)[:, None, None, None]
    ki = jnp.arange(NSA_KV_HEADS)[None, None, :, None]
    ks = ks_b[bi, ki, idx]
    vs = vs_b[bi, ki, idx]
    pos_s = idx[..., None] * SEL_BLOCK + jnp.arange(SEL_BLOCK)
    dist_s = t_pos[None, :, None, None, None] - pos_s
    s_s = jnp.einsum('bqkgd,bqkjsd->bqkgjs', q, ks).astype(f32) - slopes[None, None, :, :, None, None] * dist_s[:, :, :, None].astype(f32)
    p_s = _masked_softmax(s_s, (dist_s >= 0)[:, :, :, None], (-2, -1))
    o_s = jnp.einsum('bqkgjs,bqkjsd->bqkgd', p_s.astype(vs.dtype), vs)
    dist_w = tq - w_pos[None, :]
    mask_w = (dist_w >= 0) & (dist_w <= WINDOW) & (w_pos[None, :] >= 0)
    s_w = jnp.einsum('bqkgd,bwkd->bqkgw', q, kw).astype(f32) - slopes[None, None, :, :, None] * dist_w[None, :, None, None, :].astype(f32)
    p_w = _masked_softmax(s_w, mask_w[None, :, None, None, :], -1)
    o_w = jnp.einsum('bqkgw,bwkd->bqkgd', p_w.astype(vw.dtype), vw)
    return gates[..., 0:1] * o_c + gates[..., 1:2] * o_s + gates[..., 2:3] * o_w


def nsa_mixer(qr, kvr, bgr, g, slopes, t0, kv_past, win_buf, win_len):
    B, T, _ = qr.shape
    q = qr.reshape(B, T, NSA_KV_HEADS, NSA_GROUP, NSA_HD) * NSA_HD ** -0.5
    kv = kvr.reshape(B, T, 6, NSA_KV_HEADS, NSA_HD)
    gates = jax.nn.sigmoid(bgr.reshape(B, T, NSA_KV_HEADS, NSA_GROUP, 3))
    new_rows = kv[:, :, :N_KV_KINDS]
    new_win = kv[:, :, N_KV_KINDS:]
    parts = ([] if kv_past is None else [kv_past]) + [new_rows]
    L = sum(p.shape[1] for p in parts)
    Lp = -(-L // SEL_BLOCK) * SEL_BLOCK
    if Lp > L:
        parts.append(jnp.zeros((B, Lp - L) + new_rows.shape[2:], new_rows.dtype))
    full = jnp.concatenate(parts, axis=1) if len(parts) > 1 else parts[0]
    kc = nsa_compress(full[:, :, 0])
    vc = nsa_compress(full[:, :, 1])
    c_end = jnp.arange(Lp // CMP_BLOCK) * CMP_BLOCK + (CMP_BLOCK - 1)
    ks_b = nsa_blocks(full[:, :, 2])
    vs_b = nsa_blocks(full[:, :, 3])
    if win_buf is None:
        nb = T // Q_BLOCK
        wpad = jnp.pad(new_win, ((0, 0), (WINDOW, 0), (0, 0), (0, 0), (0, 0)))

        def block_fn(i):
            s = i * Q_BLOCK
            wk = lax.dynamic_slice_in_dim(wpad, s, Q_BLOCK + WINDOW, axis=1)
            w_pos = s - WINDOW + jnp.arange(Q_BLOCK + WINDOW)
            return nsa_attend(lax.dynamic_slice_in_dim(q, s, Q_BLOCK, 1), t0 + s + jnp.arange(Q_BLOCK),
                              kc, vc, c_end, ks_b, vs_b, wk[:, :, 0], wk[:, :, 1], w_pos,
                              lax.dynamic_slice_in_dim(gates, s, Q_BLOCK, 1), slopes)

        o = lax.map(block_fn, jnp.arange(nb))
        o = o.transpose(1, 0, 2, 3, 4, 5).reshape(B, T, NSA_Q)
        win_state = jnp.pad(new_win, ((0, 0), (max(win_len - T, 0), 0), (0, 0), (0, 0), (0, 0)))[:, -win_len:]
    else:
        wb = win_buf.shape[1]
        win_all = jnp.concatenate([win_buf, new_win], axis=1)
        w_pos = t0 - wb + jnp.arange(wb + T)
        o = nsa_attend(q, t0 + jnp.arange(T), kc, vc, c_end, ks_b, vs_b,
                       win_all[:, :, 0], win_all[:, :, 1], w_pos, gates, slopes).reshape(B, T, NSA_Q)
        win_state = win_all[:, -wb:]
    return o * jax.nn.silu(g), new_rows, win_state


def trunk_layer(x, t0, kv_past, win_buf, gla_s0, rwkv_s0, shift0, win_len,
                w_in, b_in, gla_a_up, gla_a_bias, gla_norm, rwkv_mu, rwkv_w0, rwkv_w_up, rwkv_a0,
                rwkv_a_up, rwkv_k_k, rwkv_k_a, rwkv_r_k, rwkv_ln_w, rwkv_ln_b, w_br, w_out, ln_g, ln_b):
    B, T, _ = x.shape
    z = x @ w_in + b_in
    gq, gk, gv, ga, gg, rz, nq, nkv, nbg, ng, mg = jnp.split(z, IN_OFFS, axis=-1)
    o_gla, gla_s = gla_mixer(gq, gk, gv, ga, gg, gla_a_up, gla_a_bias, gla_norm, gla_s0)
    o_rwkv, rwkv_s, shift = rwkv_mixer(rz, shift0, rwkv_mu, rwkv_w0, rwkv_w_up, rwkv_a0, rwkv_a_up,
                                       rwkv_k_k, rwkv_k_a, rwkv_r_k, rwkv_ln_w, rwkv_ln_b, rwkv_s0)
    o_nsa, kv_rows, win_state = nsa_mixer(nq, nkv, nbg, ng, _alibi_slopes(), t0, kv_past, win_buf, win_len)
    br = jnp.stack([o_gla, o_rwkv, o_nsa.astype(o_gla.dtype)], axis=2)
    proj = jnp.einsum('btmc,mcd->btmd', br, w_br)
    gate = jax.nn.sigmoid(mg.reshape(B, T, N_BRANCH, D_MODEL))
    y = jnp.sum(gate * proj, axis=2) @ w_out
    x = _layernorm(DN_ALPHA * x + y, ln_g, ln_b)
    return x, kv_rows, win_state, gla_s, rwkv_s, shift


def setup_inputs(seed: int = 0) -> dict:
    key = jax.random.key(seed)
    ks = jax.random.split(key, 32)
    f32 = jnp.float32
    n_pages = PAST_LEN // PAGE_SIZE
    n_pool = (DEC_BATCH * n_pages * 5) // 4
    win_len = min(WINDOW, PAST_LEN)
    nrm = lambda k, shape, s: jax.random.normal(k, shape, f32) * s
    page_table = jax.random.permutation(ks[0], n_pool)[:DEC_BATCH * n_pages].reshape(DEC_BATCH, n_pages).astype(jnp.int32)
    return {
        "x_prompt": nrm(ks[1], (BATCH, SEQ, D_MODEL), 1.0),
        "x_sample": nrm(ks[2], (DEC_BATCH, DEC_SEQ, D_MODEL), 1.0),
        "cache_nsa_kv": nrm(ks[3], (DEPTH, n_pool, PAGE_SIZE, N_KV_KINDS, NSA_KV_HEADS, NSA_HD), 1.0),
        "state_nsa_win": nrm(ks[4], (DEPTH, DEC_BATCH, win_len, 2, NSA_KV_HEADS, NSA_HD), 1.0),
        "state_gla": nrm(ks[5], (DEPTH, DEC_BATCH, GLA_HEADS, GLA_DK, GLA_DV), 0.5),
        "state_rwkv": nrm(ks[6], (DEPTH, DEC_BATCH, RWKV_HEADS, RWKV_HD, RWKV_HD), 0.5),
        "state_rwkv_shift": nrm(ks[7], (DEPTH, DEC_BATCH, RWKV_IN), 1.0),
        "page_table": page_table,
        "w_in": nrm(ks[8], (DEPTH, D_MODEL, IN_WIDTH), D_MODEL ** -0.5),
        "b_in": nrm(ks[9], (DEPTH, IN_WIDTH), 0.01),
        "gla_a_up": nrm(ks[10], (DEPTH, GLA_LORA, GLA_K), GLA_LORA ** -0.5),
        "gla_a_bias": nrm(ks[11], (DEPTH, GLA_K), 0.1),
        "gla_norm": 1.0 + nrm(ks[12], (DEPTH, GLA_V), 0.01),
        "rwkv_mu": jax.random.uniform(ks[13], (DEPTH, RWKV_IN), f32),
        "rwkv_w0": jax.random.uniform(ks[14], (DEPTH, RWKV_W), f32, -3.0, 1.0),
        "rwkv_w_up": nrm(ks[15], (DEPTH, RWKV_LORA_W, RWKV_W), 0.5 * RWKV_LORA_W ** -0.5),
        "rwkv_a0": nrm(ks[16], (DEPTH, RWKV_W), 0.1),
        "rwkv_a_up": nrm(ks[17], (DEPTH, RWKV_LORA_A, RWKV_W), RWKV_LORA_A ** -0.5),
        "rwkv_k_k": 0.85 + nrm(ks[18], (DEPTH, RWKV_W), 0.01),
        "rwkv_k_a": 1.0 + nrm(ks[19], (DEPTH, RWKV_W), 0.01),
        "rwkv_r_k": nrm(ks[20], (DEPTH, RWKV_HEADS, RWKV_HD), 0.1),
        "rwkv_ln_w": 1.0 + nrm(ks[21], (DEPTH, RWKV_W), 0.01),
        "rwkv_ln_b": nrm(ks[22], (DEPTH, RWKV_W), 0.01),
        "w_br": nrm(ks[23], (DEPTH, N_BRANCH, BRANCH_W, D_MODEL), DN_BETA * BRANCH_W ** -0.5),
        "w_out": nrm(ks[24], (DEPTH, D_MODEL, D_MODEL), DN_BETA * D_MODEL ** -0.5),
        "ln_g": 1.0 + nrm(ks[25], (DEPTH, D_MODEL), 0.01),
        "ln_b": nrm(ks[26], (DEPTH, D_MODEL), 0.01),
    }


def reference(x_prompt, x_sample, cache_nsa_kv, state_nsa_win, state_gla, state_rwkv, state_rwkv_shift,
              page_table, w_in, b_in, gla_a_up, gla_a_bias, gla_norm, rwkv_mu, rwkv_w0, rwkv_w_up,
              rwkv_a0, rwkv_a_up, rwkv_k_k, rwkv_k_a, rwkv_r_k, rwkv_ln_w, rwkv_ln_b, w_br, w_out, ln_g, ln_b):
    n_pages = PAST_LEN // PAGE_SIZE
    win_len = min(WINDOW, PAST_LEN)
    bp = x_prompt.shape[0]
    bs = x_sample.shape[0]
    dt = x_prompt.dtype
    xp, xs = x_prompt, x_sample
    kv_p, kv_s, win_p, win_s, gla_p, gla_s, rwkv_p, rwkv_s, sh_p, sh_s = ([] for _ in range(10))
    weights = (w_in, b_in, gla_a_up, gla_a_bias, gla_norm, rwkv_mu, rwkv_w0, rwkv_w_up, rwkv_a0,
               rwkv_a_up, rwkv_k_k, rwkv_k_a, rwkv_r_k, rwkv_ln_w, rwkv_ln_b, w_br, w_out, ln_g, ln_b)
    for l in range(DEPTH):
        lw = [w[l] for w in weights]
        xp, a, b, c, d, e = trunk_layer(
            xp, 0, None, None,
            jnp.zeros((bp, GLA_HEADS, GLA_DK, GLA_DV), dt),
            jnp.zeros((bp, RWKV_HEADS, RWKV_HD, RWKV_HD), dt),
            jnp.zeros((bp, RWKV_IN), dt), win_len, *lw)
        kv_p.append(a); win_p.append(b); gla_p.append(c); rwkv_p.append(d); sh_p.append(e)
        past = cache_nsa_kv[l][page_table].reshape(bs, n_pages * PAGE_SIZE, N_KV_KINDS, NSA_KV_HEADS, NSA_HD)
        xs, a, b, c, d, e = trunk_layer(
            xs, PAST_LEN, past, state_nsa_win[l], state_gla[l], state_rwkv[l], state_rwkv_shift[l],
            win_len, *lw)
        kv_s.append(a); win_s.append(b); gla_s.append(c); rwkv_s.append(d); sh_s.append(e)
    return (xp, xs, jnp.stack(kv_p), jnp.stack(kv_s), jnp.stack(win_p), jnp.stack(win_s),
            jnp.stack(gla_p), jnp.stack(gla_s), jnp.stack(rwkv_p), jnp.stack(rwkv_s),
            jnp.stack(sh_p), jnp.stack(sh_s))
```

```python
import contextlib
import numpy as np
import concourse.bass as bass
import concourse.mybir as mybir
from concourse.bass_utils import run_bass_kernel_spmd

F32 = mybir.dt.float32
BF16 = mybir.dt.bfloat16
I32 = mybir.dt.int32
U32 = mybir.dt.uint32
AF = mybir.ActivationFunctionType
ALU = mybir.AluOpType
AX = mybir.AxisListType

NCORES = 8
D = 1024
DEPTH = 2
T = 2048
NS_B = 4
NS_T = 4
NSAMP = NS_B * NS_T
NTOK = T + NSAMP
INW = 8616
O_GQ, O_GK, O_GV, O_GA, O_GG = 0, 256, 512, 1024, 1040
O_RZ = 1552
RWIN = 2176
O_NQ = 3728
O_NKV = 4240
O_NBG = 5008
O_NG = 5032
O_MG = 5544
WINB = 512
NPAGES = 128
PAGE = 128
NPOOL = 5120


class Sched:
    def __init__(self, nc, n_dma_sems=40, same_eng_wait=True):
        self.nc = nc
        self.e = {'pe': nc.tensor, 'dve': nc.vector, 'act': nc.scalar,
                  'pool': nc.gpsimd, 'sp': nc.sync}
        self.sem = {k: nc.alloc_semaphore(name="s_" + k) for k in self.e}
        self.cnt = {k: 0 for k in self.e}
        self.dsem = [nc.alloc_semaphore(name="d%d" % i) for i in range(n_dma_sems)]
        self.dcnt = [0] * n_dma_sems
        self.dnext = 0
        self.seen = {k: {} for k in self.e}
        self.lastw = {}
        self.readers = {}
        self.same_eng_wait = same_eng_wait
        self.out_events = []
        self.n_inst = 0

    def _wait(self, eng, ev, same_ok=False):
        semkey, semh, val, src = ev
        if src == eng:
            if eng == 'pe' or same_ok or not self.same_eng_wait:
                return
        if self.seen[eng].get(semkey, 0) >= val:
            return
        self.e[eng].wait_ge(semh, val)
        self.seen[eng][semkey] = val
        self.n_inst += 1

    def _deps(self, eng, reads, writes):
        for k in reads:
            ev = self.lastw.get(k)
            if ev is not None:
                self._wait(eng, ev)
        for k in writes:
            ev = self.lastw.get(k)
            if ev is not None:
                self._wait(eng, ev)
            for ev in self.readers.get(k, ()):
                self._wait(eng, ev, same_ok=True)

    def _record(self, ev, reads, writes):
        for k in writes:
            self.lastw[k] = ev
            self.readers[k] = []
        for k in reads:
            if k in writes:
                continue
            lst = self.readers.setdefault(k, [])
            lst.append(ev)
            if len(lst) > 48:
                d = {}
                for e2 in lst:
                    if e2[0] not in d or d[e2[0]][2] < e2[2]:
                        d[e2[0]] = e2
                self.readers[k] = list(d.values())

    def op(self, eng, fn, reads=(), writes=()):
        self._deps(eng, reads, writes)
        inst = fn(self.e[eng])
        self.cnt[eng] += 1
        inst.then_inc(self.sem[eng], 1)
        ev = (eng, self.sem[eng], self.cnt[eng], eng)
        self._record(ev, reads, writes)
        self.n_inst += 1
        return ev

    def dma(self, q, out, in_, reads=(), writes=(), is_output=False, fn=None, **kw):
        self._deps(q, reads, writes)
        i = self.dnext
        self.dnext = (self.dnext + 1) % len(self.dsem)
        if self.dcnt[i] > 0:
            self._wait(q, (('d', i), self.dsem[i], 16 * self.dcnt[i], None))
        if fn is not None:
            inst = fn(self.e[q])
        else:
            inst = self.e[q].dma_start(out=out, in_=in_, **kw)
        inst.then_inc(self.dsem[i], 16)
        self.dcnt[i] += 1
        ev = (('d', i), self.dsem[i], 16 * self.dcnt[i], None)
        self._record(ev, reads, writes)
        if is_output:
            self.out_events.append(ev)
        self.n_inst += 1
        return ev

    def barrier(self):
        evs = [(k, self.sem[k], self.cnt[k], k) for k in self.e if self.cnt[k] > 0]
        evs += [(('d', i), self.dsem[i], 16 * self.dcnt[i], None)
                for i in range(len(self.dsem)) if self.dcnt[i] > 0]
        for eng in self.e:
            for ev in evs:
                if ev[3] == eng:
                    continue
                self._wait(eng, ev)
        self.lastw = {}
        self.readers = {}

    def finish(self):
        for ev in self.out_events:
            self._wait('sp', ev)
        self.barrier()


def token_tiles():
    tl = [(i * 128, 128) for i in range(T // 128)]
    tl.append((T, NSAMP))
    return tl


class Ctx:
    pass


class Phase:
    def __init__(self, C):
        self.C = C
        self.st = contextlib.ExitStack()

    def __enter__(self):
        self.st.__enter__()
        return self

    def __exit__(self, *a):
        if a[0] is None:
            self.C.S.barrier()
        return self.st.__exit__(*a)

    def T(self, name, shape, dt=F32):
        self.C.uid += 1
        return self.st.enter_context(self.C.nc.sbuf_tensor("%s_%d" % (name, self.C.uid), shape, dt))


def MM(S, out, lhsT, rhs, start=True, stop=True, r=(), w=()):
    return S.op('pe', lambda e: e.matmul(out, lhsT=lhsT, rhs=rhs, start=start, stop=stop), r, w)


def TR(S, out, in_, ident, r=(), w=()):
    n = in_.shape[0]
    return S.op('pe', lambda e: e.transpose(out, in_, ident[:n, :n]), list(r) + ['ident'], w)


def TT(S, eng, out, a, b, op, r=(), w=()):
    return S.op(eng, lambda e: e.tensor_tensor(out=out, in0=a, in1=b, op=op), r, w)


def STT(S, out, a, sc, b, op0, op1, r=(), w=()):
    return S.op('dve', lambda e: e.scalar_tensor_tensor(out=out, in0=a, scalar=sc, in1=b, op0=op0, op1=op1), r, w)


def TS(S, eng, out, a, s1, s2, op0, op1, r=(), w=()):
    return S.op(eng, lambda e: e.tensor_scalar(out=out, in0=a, scalar1=s1, scalar2=s2, op0=op0, op1=op1), r, w)


def ACT(S, out, in_, func, r=(), w=(), **kw):
    return S.op('act', lambda e: e.activation(out=out, in_=in_, func=func, **kw), r, w)


def CP(S, eng, out, in_, r=(), w=()):
    if eng == 'act':
        return S.op('act', lambda e: e.copy(out, in_), r, w)
    return S.op(eng, lambda e: e.tensor_copy(out, in_), r, w)


def MEMSET(S, eng, ap, val, w=()):
    return S.op(eng, lambda e: e.memset(ap, val), (), w)


def ASEL(S, ap, pattern, base, cm, w, op=None, fill=0.0):
    op = ALU.is_ge if op is None else op
    return S.op('pool', lambda e: e.affine_select(out=ap, in_=ap, pattern=pattern, compare_op=op, fill=fill,
                                                  base=base, channel_multiplier=cm), w, w)


def RED(S, out, in_, op, r=(), w=()):
    return S.op('dve', lambda e: e.tensor_reduce(out=out, in_=in_, axis=AX.X, op=op), r, w)


def phase_inproj(C, l, xsrc):
    S, nc, I, ps, ident = C.S, C.nc, C.I, C.ps, C.ident
    tiles = token_tiles()
    z = C.z
    with Phase(C) as ph:
        xT = ph.T("xT", [128, 8, NTOK], BF16)
        xin = [ph.T("xin%d" % i, [128, D]) for i in range(2)]
        ones_bf = ph.T("ones_bf", [1, 128], BF16)
        MEMSET(S, 'dve', ones_bf[:], 1.0, ['ones_bf'])
        for ti, (r0, nr) in enumerate(tiles):
            xb = xin[ti % 2]
            kx = 'xin%d' % (ti % 2)
            S.dma('sp', xb[:nr, :], xsrc[r0:r0 + nr, :], reads=['xs'], writes=[kx])
            for half in range(2):
                pi = (ti * 2 + half) % 2
                pb, kp = ps[pi], 'ps%d' % pi
                for c4 in range(4):
                    c = half * 4 + c4
                    TR(S, pb[:, c4 * 128:c4 * 128 + nr], xb[:nr, c * 128:(c + 1) * 128], ident, [kx], [kp])
                src = pb[:].rearrange("p (c t) -> p c t", c=4)[:, :, :nr]
                dst = xT[:, half * 4:half * 4 + 4, r0:r0 + nr]
                CP(S, 'dve' if half == 0 else 'act', dst, src, [kp], [('xT', ti)])
        wbuf = [ph.T("wbuf%d" % i, [128, 8, 512], BF16) for i in range(2)]
        bbuf = [ph.T("bbuf%d" % i, [1, 512], BF16) for i in range(2)]
        ost = [ph.T("ost%d" % i, [128, 512]) for i in range(4)]
        ncol = (INW + 511) // 512
        k_ev = 0
        for j in range(ncol):
            c0 = j * 512
            cw = min(512, INW - c0)
            wb, bb, kw_ = wbuf[j % 2], bbuf[j % 2], 'wbuf%d' % (j % 2)
            S.dma('pool', wb[:, :, :cw], I['w_in'][l, :, c0:c0 + cw].rearrange("(c p) n -> p c n", p=128),
                  writes=[kw_])
            S.dma('pool', bb[:, :cw], I['b_in'][l:l + 1, c0:c0 + cw], writes=[kw_ + 'b'])
            for ti, (r0, nr) in enumerate(tiles):
                pi = 2 + (k_ev % 4)
                pb, kp = ps[pi], 'ps%d' % pi
                for c in range(8):
                    MM(S, pb[:nr, :cw], xT[:, c, r0:r0 + nr], wb[:, c, :cw], c == 0, False, [('xT', ti), kw_], [kp])
                MM(S, pb[:nr, :cw], ones_bf[:, :nr], bb[:, :cw], False, True, ['ones_bf', kw_ + 'b'], [kp])
                ob, ko = ost[k_ev % 4], 'ost%d' % (k_ev % 4)
                CP(S, 'dve' if k_ev % 2 == 0 else 'act', ob[:nr, :cw], pb[:nr, :cw], [kp], [ko])
                S.dma('sp', z[r0:r0 + nr, c0:c0 + cw], ob[:nr, :cw], reads=[ko], writes=['z'])
                k_ev += 1


def direct_outputs(C, l):
    S, I, O, z = C.S, C.I, C.O, C.z
    S.dma('sp', O['kv'][l], z[:, O_NKV:O_NKV + 512], is_output=True)
    S.dma('sp', O['winp'][l], z[T - WINB:T, O_NKV + 512:O_NKV + 768], is_output=True)
    for b in range(NS_B):
        S.dma('sp', O['wins'][l, b, 0:WINB - NS_T, :], I['win'][l, b, NS_T:WINB, :], is_output=True)
        S.dma('sp', O['wins'][l, b, WINB - NS_T:WINB, :],
              z[T + b * NS_T:T + (b + 1) * NS_T, O_NKV + 512:O_NKV + 768], is_output=True)
        S.dma('sp', O['shs'][l, b:b + 1, :], z[T + b * NS_T + NS_T - 1:T + (b + 1) * NS_T, O_RZ:O_RZ + RWIN],
              is_output=True)
    S.dma('sp', O['shp'][l:l + 1, :], z[T - 1:T, O_RZ:O_RZ + RWIN], is_output=True)


def phase_gla(C, l):
    S, nc, I, O, ps, ident, z, br = C.S, C.nc, C.I, C.O, C.ps, C.ident, C.z, C.br
    with Phase(C) as ph:
        tri = ph.T('tri', [64, 64]); slow = ph.T('slow', [64, 64]); cmask = ph.T('cmask', [64, 64])
        MEMSET(S, 'pool', tri[:], -1.0 / 16, ['tri'])
        ASEL(S, tri[:], [[1, 64]], 0, -1, ['tri'])
        MEMSET(S, 'pool', slow[:], -1.0 / 16, ['slow'])
        ASEL(S, slow[:], [[-1, 64]], -1, 1, ['slow'])
        MEMSET(S, 'pool', cmask[:], 1.0, ['cmask'])
        ASEL(S, cmask[:], [[1, 64]], 0, -1, ['cmask'])
        aup = ph.T('aup', [16, 256]); abias = ph.T('abias', [1, 256]); ones1 = ph.T('ones1', [1, 64])
        normg = ph.T('normg', [64, 512])
        S.dma('sp', aup[:], I['gla_a_up'][l], writes=['aup'])
        S.dma('sp', abias[:], I['gla_a_bias'][l:l + 1, :], writes=['abias'])
        MEMSET(S, 'dve', ones1[:], 1.0, ['ones1'])
        S.dma('sp', normg[:], I['gla_norm'][l:l + 1, :].partition_broadcast(64), writes=['normg'])
        Sst = ph.T('Sst', [64, 512])
        zb = [ph.T('gz%d' % i, [64, 1552]) for i in range(2)]
        gaT = ph.T('gaT', [16, 64]); la = ph.T('la', [64, 256])
        ecum = ph.T('ecum', [64, 4, 64]); encum = ph.T('encum', [64, 4, 64]); edl = ph.T('edl', [64, 256])
        qdT = ph.T('qdT', [64, 4, 64]); kdT = ph.T('kdT', [64, 4, 64]); kl = ph.T('kl', [64, 256])
        attT = ph.T('attT', [64, 4, 64])
        ssq = ph.T('ssq', [64, 4]); rstd = ph.T('rstd', [64, 4]); junk = ph.T('junk', [64, 128])
        sg = ph.T('sg', [64, 512]); bro = [ph.T('bro%d' % i, [64, 512]) for i in range(2)]
        seqs = [('p', 0, T, 64, None)] + [('s', T + b * NS_T, NS_T, NS_T, b) for b in range(NS_B)]
        kc = 0
        for (kind, r0, L, n, b) in seqs:
            if kind == 'p':
                MEMSET(S, 'dve', Sst[:], 0.0, ['Sst'])
            else:
                S.dma('sp', Sst[:].rearrange("d (h e) -> d h e", h=4), I['gla_st'][l, b].rearrange("h d e -> d h e"),
                      writes=['Sst'])
            for c0 in range(0, L, n):
                row = r0 + c0
                zt, kz = zb[kc % 2], 'gz%d' % (kc % 2)
                S.dma('sp', zt[:n, :], z[row:row + n, 0:1552], writes=[kz])
                TR(S, ps[0][:16, :n], zt[:n, O_GA:O_GA + 16], ident, [kz], ['ps0'])
                CP(S, 'act', gaT[:, :n], ps[0][:16, :n], ['ps0'], ['gaT'])
                MM(S, ps[1][:n, :256], gaT[:, :n], aup[:], True, False, ['gaT', 'aup'], ['ps1'])
                MM(S, ps[1][:n, :256], ones1[:, :n], abias[:], False, True, ['ones1', 'abias'], ['ps1'])
                ACT(S, la[:n, :], ps[1][:n, :256], AF.Exp, ['ps1'], ['la'], scale=-1.0)
                ACT(S, la[:n, :], la[:n, :], AF.Ln, ['la'], ['la'], bias=1.0)
                for h in range(4):
                    MM(S, ps[2][:64, h * 64:h * 64 + n], la[:n, h * 64:(h + 1) * 64], tri[:n, :n], True, True,
                       ['la', 'tri'], ['ps2'])
                MM(S, ps[3][:n, :256], slow[:n, :n], la[:n, :], True, True, ['la', 'slow'], ['ps3'])
                pc = ps[2][:64, :256].rearrange("p (h t) -> p h t", h=4)[:, :, :n]
                ACT(S, ecum[:, :, :n], pc, AF.Exp, ['ps2'], ['ecum'])
                ACT(S, encum[:, :, :n], pc, AF.Exp, ['ps2'], ['encum'], scale=-1.0)
                ACT(S, edl[:n, :], ps[3][:n, :256], AF.Exp, ['ps3'], ['edl'])
                for hh in range(8):
                    TR(S, ps[4][:64, hh * 64:hh * 64 + n], zt[:n, hh * 64:(hh + 1) * 64], ident, [kz], ['ps4'])
                pq = ps[4][:64, :].rearrange("p (h t) -> p h t", h=8)
                STT(S, qdT[:, :, :n], pq[:, 0:4, :n], 0.125, ecum[:, :, :n], ALU.mult, ALU.mult, ['ps4', 'ecum'], ['qdT'])
                TT(S, 'dve', kdT[:, :, :n], pq[:, 4:8, :n], encum[:, :, :n], ALU.mult, ['ps4', 'encum'], ['kdT'])
                TT(S, 'pool', kl[:n, :], zt[:n, O_GK:O_GK + 256], edl[:n, :], ALU.mult, [kz, 'edl'], ['kl'])
                for h in range(4):
                    MM(S, ps[5][:n, h * 64:h * 64 + n], kdT[:, h, :n], qdT[:, h, :n], True, True, ['kdT', 'qdT'], ['ps5'])
                pe_ = ps[5][:n, :256].rearrange("p (h t) -> p h t", h=4)[:, :, :n]
                TT(S, 'dve', attT[:n, :, :n], pe_, cmask[:n, :n].unsqueeze(1).to_broadcast([n, 4, n]), ALU.mult,
                   ['ps5', 'cmask'], ['attT'])
                for h in range(4):
                    vh = zt[:n, O_GV + h * 128:O_GV + (h + 1) * 128]
                    MM(S, ps[6][:n, h * 128:(h + 1) * 128], attT[:n, h, :n], vh, True, False, ['attT', kz], ['ps6'])
                    MM(S, ps[6][:n, h * 128:(h + 1) * 128], qdT[:, h, :n], Sst[:, h * 128:(h + 1) * 128], False, True,
                       ['qdT', 'Sst'], ['ps6'])
                for h in range(4):
                    vh = zt[:n, O_GV + h * 128:O_GV + (h + 1) * 128]
                    MM(S, ps[7][:64, h * 128:(h + 1) * 128], kl[:n, h * 64:(h + 1) * 64], vh, True, True, ['kl', kz], ['ps7'])
                for h in range(4):
                    STT(S, Sst[:, h * 128:(h + 1) * 128], Sst[:, h * 128:(h + 1) * 128], ecum[:, h, n - 1:n],
                        ps[7][:64, h * 128:(h + 1) * 128], ALU.mult, ALU.add, ['Sst', 'ecum', 'ps7'], ['Sst'])
                for h in range(4):
                    ACT(S, junk[:n, :], ps[6][:n, h * 128:(h + 1) * 128], AF.Square, ['ps6'], ['junk', 'ssq'],
                        accum_out=ssq[:n, h:h + 1])
                ACT(S, rstd[:n, :], ssq[:n, :], AF.Ln, ['ssq'], ['rstd'], scale=1.0 / 128, bias=1e-6)
                ACT(S, rstd[:n, :], rstd[:n, :], AF.Exp, ['rstd'], ['rstd'], scale=-0.5)
                ACT(S, sg[:n, :], zt[:n, O_GG:O_GG + 512], AF.Silu, [kz], ['sg'])
                TT(S, 'pool', sg[:n, :], sg[:n, :], normg[:n, :], ALU.mult, ['sg', 'normg'], ['sg'])
                bo, kb = bro[kc % 2], 'bro%d' % (kc % 2)
                for h in range(4):
                    STT(S, bo[:n, h * 128:(h + 1) * 128], ps[6][:n, h * 128:(h + 1) * 128], rstd[:n, h:h + 1],
                        sg[:n, h * 128:(h + 1) * 128], ALU.mult, ALU.mult, ['ps6', 'rstd', 'sg'], [kb])
                S.dma('sp', br[row:row + n, 0:512], bo[:n, :], reads=[kb], writes=[('br', 0)])
                kc += 1
            dst = O['glap'][l] if kind == 'p' else O['glas'][l, b]
            S.dma('sp', dst.rearrange("h d e -> d h e"), Sst[:].rearrange("d (h e) -> d h e", h=4), reads=['Sst'],
                  is_output=True)
def phase_rwkv_pre(C, l):
    S, nc, I, O, ps, ident, z = C.S, C.nc, C.I, C.O, C.ps, C.ident, C.z
    tiles = token_tiles()
    with Phase(C) as ph:
        def bc(name, src_row, width):
            t = ph.T(name, [128, width])
            S.dma('sp', t[:], src_row.partition_broadcast(128), writes=[name])
            return t
        mu_b = bc('mu_b', I['rwkv_mu'][l:l + 1, :], RWIN)
        kk_b = bc('kk_b', I['rwkv_k_k'][l:l + 1, :], 512)
        ka_b = bc('ka_b', I['rwkv_k_a'][l:l + 1, :], 512)
        rk_b = bc('rk_b', I['rwkv_r_k'][l:l + 1, :], 512)
        wup = ph.T('wup', [64, 512]); aup = ph.T('raup', [64, 512])
        w0 = ph.T('w0', [1, 512]); a0 = ph.T('a0', [1, 512]); ones1 = ph.T('rones', [1, 128])
        S.dma('sp', wup[:], I['rwkv_w_up'][l], writes=['wup'])
        S.dma('sp', aup[:], I['rwkv_a_up'][l], writes=['raup'])
        S.dma('sp', w0[:], I['rwkv_w0'][l:l + 1, :], writes=['w0'])
        S.dma('sp', a0[:], I['rwkv_a0'][l:l + 1, :], writes=['a0'])
        MEMSET(S, 'dve', ones1[:], 1.0, ['rones'])
        zc = ph.T('zc', [128, RWIN]); zp = ph.T('zp', [128, RWIN]); zs = ph.T('zs', [128, RWIN])
        twl = ph.T('twl', [128, 64]); lT = ph.T('lT', [64, 2, 128])
        dec = ph.T('dec', [128, 512]); av = ph.T('av', [128, 512]); kk = ph.T('kk', [128, 512])
        kk2 = ph.T('kk2', [128, 512]); s8 = ph.T('s8', [128, 8]); rn = ph.T('rn', [128, 8])
        nkk = ph.T('nkk', [128, 512]); t1 = ph.T('t1', [128, 512]); kmod = ph.T('kmod', [128, 512])
        rk3 = ph.T('rk3', [128, 3, 512], BF16); rkt = ph.T('rkt', [128, 512]); b8 = ph.T('b8', [128, 8])
        bv = ph.T('bv', [128, 512]); sgt = ph.T('sgt', [128, 512])
        fm = [ph.T('fm%d' % i, [64, 8, 128]) for i in range(3)]
        for ti, (r0, nr) in enumerate(tiles):
            S.dma('sp', zc[:nr, :], z[r0:r0 + nr, O_RZ:O_RZ + RWIN], writes=['zc'])
            if ti == 0:
                MEMSET(S, 'dve', zp[0:1, :], 0.0, ['zp'])
                S.dma('sp', zp[1:nr, :], z[0:nr - 1, O_RZ:O_RZ + RWIN], reads=['zp'], writes=['zp1'])
            elif nr == 128:
                S.dma('sp', zp[:nr, :], z[r0 - 1:r0 + nr - 1, O_RZ:O_RZ + RWIN], writes=['zp', 'zp1'])
            else:
                S.dma('sp', zp[:nr, :], z[r0 - 1:r0 + nr - 1, O_RZ:O_RZ + RWIN], writes=['zp'])
                kws = ['zp']
                for b in range(NS_B):
                    S.dma('sp', zp[b * NS_T:b * NS_T + 1, :], I['shift'][l, b:b + 1, :], reads=kws, writes=['zp1'])
            rd = ['zp', 'zp1', 'zc']
            TT(S, 'dve', zs[:nr, :], zp[:nr, :], zc[:nr, :], ALU.subtract, rd, ['zs'])
            TT(S, 'pool', zs[:nr, :], zs[:nr, :], mu_b[:nr, :], ALU.mult, ['zs', 'mu_b'], ['zs'])
            TT(S, 'dve', zs[:nr, :], zs[:nr, :], zc[:nr, :], ALU.add, ['zs', 'zc'], ['zs'])
            r_, k_, v_ = zs[:nr, 0:512], zs[:nr, 512:1024], zs[:nr, 1024:1536]
            wl_, al_, g_ = zs[:nr, 1536:1600], zs[:nr, 1600:1664], zs[:nr, 1664:2176]
            ACT(S, twl[:nr, :], wl_, AF.Tanh, ['zs'], ['twl'])
            TR(S, ps[0][:64, 0:nr], twl[:nr, :], ident, ['twl'], ['ps0'])
            TR(S, ps[0][:64, 128:128 + nr], al_, ident, ['zs'], ['ps0'])
            CP(S, 'dve', lT[:, :, :nr], ps[0][:64, :256].rearrange("p (a t) -> p a t", a=2)[:, :, :nr], ['ps0'], ['lT'])
            MM(S, ps[1][:nr, :], lT[:, 0, :nr], wup[:], True, False, ['lT', 'wup'], ['ps1'])
            MM(S, ps[1][:nr, :], ones1[:, :nr], w0[:], False, True, ['rones', 'w0'], ['ps1'])
            MM(S, ps[2][:nr, :], lT[:, 1, :nr], aup[:], True, False, ['lT', 'raup'], ['ps2'])
            MM(S, ps[2][:nr, :], ones1[:, :nr], a0[:], False, True, ['rones', 'a0'], ['ps2'])
            ACT(S, dec[:nr, :], ps[1][:nr, :], AF.Sigmoid, ['ps1'], ['dec'])
            ACT(S, av[:nr, :], ps[2][:nr, :], AF.Sigmoid, ['ps2'], ['av'])
            ACT(S, dec[:nr, :], dec[:nr, :], AF.Exp, ['dec'], ['dec'], scale=-float(np.exp(-0.5)))
            ACT(S, sgt[:nr, :], g_, AF.Silu, ['zs'], ['sgt'])
            S.dma('act', C.sg_scr[r0:r0 + nr, :], sgt[:nr, :], reads=['sgt'], writes=[('sgs', ti)])
            TT(S, 'dve', kk[:nr, :], k_, kk_b[:nr, :], ALU.mult, ['zs', 'kk_b'], ['kk'])
            TT(S, 'pool', kk2[:nr, :], kk[:nr, :], kk[:nr, :], ALU.mult, ['kk'], ['kk2'])
            RED(S, s8[:nr, :], kk2[:nr, :].rearrange("p (h j) -> p h j", h=8), ALU.add, ['kk2'], ['s8'])
            ACT(S, rn[:nr, :], s8[:nr, :], AF.Ln, ['s8'], ['rn'], bias=1e-6)
            ACT(S, rn[:nr, :], rn[:nr, :], AF.Exp, ['rn'], ['rn'], scale=-0.5)
            v3 = lambda t: t.rearrange("p (h j) -> p h j", h=8)
            STT(S, v3(nkk[:nr, :]), v3(kk[:nr, :]), -1.0, rn[:nr, :].unsqueeze(2).to_broadcast([nr, 8, 64]),
                ALU.mult, ALU.mult, ['kk', 'rn'], ['nkk'])
            STT(S, t1[:nr, :], av[:nr, :], -1.0, ka_b[:nr, :], ALU.add, ALU.mult, ['av', 'ka_b'], ['t1'])
            STT(S, kmod[:nr, :], t1[:nr, :], 1.0, k_, ALU.add, ALU.mult, ['t1', 'zs'], ['kmod'])
            STT(S, rk3[:nr, 1, :], nkk[:nr, :], -1.0, av[:nr, :], ALU.mult, ALU.mult, ['nkk', 'av'], [('rk3', 1)])
            CP(S, 'act', rk3[:nr, 0, :], kmod[:nr, :], ['kmod'], [('rk3', 0)])
            CP(S, 'act', rk3[:nr, 2, :], v_, ['zs'], [('rk3', 2)])
            S.dma('act', C.rk_scr[r0:r0 + nr, :, :], rk3[:nr, :, :], reads=[('rk3', 0), ('rk3', 1), ('rk3', 2)],
                  writes=[('rks', ti)])
            TT(S, 'pool', rkt[:nr, :], r_, kmod[:nr, :], ALU.mult, ['zs', 'kmod'], ['rkt'])
            TT(S, 'dve', rkt[:nr, :], rkt[:nr, :], rk_b[:nr, :], ALU.mult, ['rkt', 'rk_b'], ['rkt'])
            RED(S, b8[:nr, :], v3(rkt[:nr, :]), ALU.add, ['rkt'], ['b8'])
            TT(S, 'dve', v3(bv[:nr, :]), v3(v_), b8[:nr, :].unsqueeze(2).to_broadcast([nr, 8, 64]), ALU.mult,
               ['zs', 'b8'], ['bv'])
            S.dma('act', C.bv_scr[r0:r0 + nr, :], bv[:nr, :], reads=['bv'], writes=[('bvs', ti)])
            for qi, src in enumerate((nkk[:nr, :], r_, dec[:nr, :])):
                rkey = ['nkk', 'zs', 'dec'][qi]
                for hb in range(2):
                    pb, kp = ps[3 + hb], 'ps%d' % (3 + hb)
                    for h4 in range(4):
                        h = hb * 4 + h4
                        TR(S, pb[:64, h4 * 128:h4 * 128 + nr], src[:, h * 64:(h + 1) * 64], ident, [rkey], [kp])
                    CP(S, 'dve' if hb == 0 else 'act', fm[qi][:, hb * 4:hb * 4 + 4, :nr],
                       pb[:64, :].rearrange("p (h t) -> p h t", h=4)[:, :, :nr], [kp], [('fm', qi)])
                S.dma('sp', C.fm_scr[qi, :, :, r0:r0 + nr], fm[qi][:, :, :nr], reads=[('fm', qi)], writes=[('fms', ti)])


def phase_rwkv_scan(C, l):
    S, nc, I, O, ps, ident = C.S, C.nc, C.I, C.O, C.ps, C.ident
    SUB = 8
    with Phase(C) as ph:
        maskbd = ph.T('maskbd', [8, 512]); maskbf = ph.T('maskbf', [8, 512], BF16)
        MEMSET(S, 'pool', maskbd[:], 1.0, ['maskbd'])
        ASEL(S, maskbd[:], [[1, 512]], 0, -64, ['maskbd'])
        ASEL(S, maskbd[:], [[-1, 512]], 63, 64, ['maskbd'])
        CP(S, 'dve', maskbf[:], maskbd[:], ['maskbd'], ['maskbf'])
        ST = [ph.T('ST%d' % i, [64, 512]) for i in range(2)]
        tmp = ph.T('sttmp', [64, 512])
        fmt = [[ph.T('fmt%d_%d' % (q, i), [64, 8, 128]) for i in range(2)] for q in range(3)]
        kmr = [ph.T('kmr%d' % i, [8, SUB, 64], BF16) for i in range(2)]
        kar = [ph.T('kar%d' % i, [8, SUB, 64], BF16) for i in range(2)]
        vb = [ph.T('vb%d' % i, [8, SUB, 512], BF16) for i in range(2)]
        sab = [ph.T('sab%d' % i, [8, 512], BF16) for i in range(2)]
        yT = ph.T('yT', [128, 4, 128]); ytm = ph.T('ytm', [128, 512])
        sin = ph.T('sin', [64, 8, 64]); sout = ph.T('sout', [64, 8, 64])
        seqs = [('p', 0, T, None)] + [('s', T + b * NS_T, NS_T, b) for b in range(NS_B)]
        cur = 0
        gt = 0
        gs = 0
        for (kind, r0, L, b) in seqs:
            if kind == 'p':
                MEMSET(S, 'dve', ST[cur][:], 0.0, ['ST%d' % cur])
            else:
                S.dma('sp', sin[:], I['rw_st'][l, b].rearrange("h i j -> i h j"), writes=['sin'])
                for h in range(8):
                    TR(S, ps[0][:64, h * 64:(h + 1) * 64], sin[:, h, :], ident, ['sin'], ['ps0'])
                CP(S, 'dve', ST[cur][:], ps[0][:64, :], ['ps0'], ['ST%d' % cur])
            tiles_ = [(t0, min(128, L - t0)) for t0 in range(0, L, 128)]
            subs = []
            for k, (t0, nt) in enumerate(tiles_):
                for s0 in range(0, nt, SUB):
                    subs.append((k, t0, s0, min(SUB, nt - s0)))
            steps = []
            for si, (k, t0, s0, ns) in enumerate(subs):
                for tt_ in range(ns):
                    steps.append((si, k, t0, s0, ns, tt_))

            def load_tile(k):
                t0, nt = tiles_[k]
                fb = (gt + k) % 2
                for q in range(3):
                    S.dma('sp' if q != 1 else 'act', fmt[q][fb][:, :, :nt], C.fm_scr[q, :, :, r0 + t0:r0 + t0 + nt],
                          writes=[('fmt', q, fb)])

            def load_sub(si):
                k, t0, s0, ns = subs[si]
                sb = (gs + si) % 2
                rows = slice(r0 + t0 + s0, r0 + t0 + s0 + ns)
                S.dma('sp', kmr[sb][:, :ns, :], C.rk_scr[rows, 0, :].rearrange("t (h j) -> h t j", h=8),
                      writes=[('kmr', sb)])
                S.dma('act', kar[sb][:, :ns, :], C.rk_scr[rows, 1, :].rearrange("t (h j) -> h t j", h=8),
                      writes=[('kar', sb)])
                S.dma('sp', vb[sb][:, :ns, :], C.rk_scr[rows, 2, :].partition_broadcast(8), writes=[('vb', sb)])
                TT(S, 'pool', vb[sb][:, :ns, :], vb[sb][:, :ns, :],
                   maskbf[:].unsqueeze(1).to_broadcast([8, ns, 512]), ALU.mult, [('vb', sb), 'maskbf'], [('vb', sb)])

            def front(i, cur_):
                si, k, t0, s0, ns, tt_ = steps[i]
                sb = (gs + si) % 2
                fb = (gt + k) % 2
                t = s0 + tt_
                MM(S, ps[2][:64, :], kmr[sb][:, tt_, :], vb[sb][:, tt_, :], True, False, [('kmr', sb), ('vb', sb)], ['ps2'])
                MM(S, ps[1][:8, :], fmt[0][fb][:, :, t], ST[cur_][:], True, True, [('fmt', 0, fb), 'ST%d' % cur_], ['ps1'])

            load_tile(0)
            load_sub(0)
            front(0, cur)
            slot = 0
            slot_t0 = 0
            for i, (si, k, t0, s0, ns, tt_) in enumerate(steps):
                sb = (gs + si) % 2
                fb = (gt + k) % 2
                t = s0 + tt_
                nt = tiles_[k][1]
                if tt_ == 0 and si + 1 < len(subs):
                    if subs[si + 1][0] != k:
                        load_tile(subs[si + 1][0])
                    load_sub(si + 1)
                nxt = 1 - cur
                kc_, kn_ = 'ST%d' % cur, 'ST%d' % nxt
                sa, ksa = sab[i % 2], 'sab%d' % (i % 2)
                TT(S, 'dve', sa[:], ps[1][:8, :], maskbd[:], ALU.mult, ['ps1', 'maskbd'], [ksa])
                MM(S, ps[2][:64, :], kar[sb][:, tt_, :], sa[:], False, True, [('kar', sb), ksa], ['ps2'])
                TT(S, 'pool', tmp[:].rearrange("p (h i) -> p h i", h=8), ST[cur][:].rearrange("p (h i) -> p h i", h=8),
                   fmt[2][fb][:, :, t].unsqueeze(2).to_broadcast([64, 8, 64]), ALU.mult, [kc_, ('fmt', 2, fb)], ['sttmp'])
                TT(S, 'dve', ST[nxt][:], tmp[:], ps[2][:64, :], ALU.add, ['sttmp', 'ps2'], [kn_])
                if i + 1 < len(steps):
                    front(i + 1, nxt)
                for c in range(4):
                    MM(S, ps[3][:, slot * 32 + c * 8:slot * 32 + c * 8 + 8], ST[nxt][:, c * 128:(c + 1) * 128],
                       fmt[1][fb][:, :, t], True, True, [kn_, ('fmt', 1, fb)], ['ps3'])
                cur = nxt
                slot += 1
                if slot == 16 or t == nt - 1:
                    for hf in range(2):
                        src = ps[3][hf * 64:(hf + 1) * 64, :].rearrange("p (s x) -> p s x", x=32)[:, :slot, hf:hf + 31:10]
                        dst = yT[hf * 64:(hf + 1) * 64, :, slot_t0:slot_t0 + slot].rearrange("p c t -> p t c")
                        CP(S, 'act', dst, src, ['ps3'], [('yT', hf)])
                    slot_t0 += slot
                    slot = 0
                if t == nt - 1:
                    for c in range(4):
                        TR(S, ps[4][:nt, c * 128:(c + 1) * 128], yT[:, c, :nt], ident, [('yT', 0), ('yT', 1)], ['ps4'])
                    CP(S, 'act', ytm[:nt, :], ps[4][:nt, :], ['ps4'], ['ytm'])
                    S.dma('sp', C.y_scr[r0 + t0:r0 + t0 + nt, :], ytm[:nt, :], reads=['ytm'], writes=[('ys', gt + k)])
                    slot_t0 = 0
            gt += len(tiles_)
            gs += len(subs)
            for h in range(8):
                TR(S, ps[0][:64, h * 64:(h + 1) * 64], ST[cur][:, h * 64:(h + 1) * 64], ident, ['ST%d' % cur], ['ps0'])
            CP(S, 'dve', sout[:].rearrange("p h j -> p (h j)"), ps[0][:64, :], ['ps0'], ['sout'])
            dst = O['rwp'][l] if kind == 'p' else O['rws'][l, b]
            S.dma('sp', dst.rearrange("h i j -> i h j"), sout[:], reads=['sout'], is_output=True)


def phase_rwkv_post(C, l):
    S, nc, I, O, ps, ident, br = C.S, C.nc, C.I, C.O, C.ps, C.ident, C.br
    tiles = token_tiles()
    with Phase(C) as ph:
        lnw = ph.T('lnw', [128, 512]); lnb = ph.T('lnb', [128, 512])
        S.dma('sp', lnw[:], I['rwkv_ln_w'][l:l + 1, :].partition_broadcast(128), writes=['lnw'])
        S.dma('sp', lnb[:], I['rwkv_ln_b'][l:l + 1, :].partition_broadcast(128), writes=['lnb'])
        v3 = lambda t: t.rearrange("p (h j) -> p h j", h=8)
        for ti, (r0, nr) in enumerate(tiles):
            i2 = ti % 2
            y = ph.T('py', [128, 512]) if ti < 2 else None
            if ti < 2:
                C._rwp = getattr(C, '_rwp', {})
                C._rwp[i2] = dict(y=y, bvt=ph.T('pbv', [128, 512]), sgt=ph.T('psg', [128, 512]),
                                  yc=ph.T('pyc', [128, 512]), sq=ph.T('psq', [128, 512]),
                                  m8=ph.T('pm8', [128, 8]), v8=ph.T('pv8', [128, 8]), o=ph.T('po', [128, 512]))
            d = C._rwp[i2]
            k = lambda s: '%s%d' % (s, i2)
            S.dma('sp', d['y'][:nr, :], C.y_scr[r0:r0 + nr, :], writes=[k('y')])
            S.dma('act', d['bvt'][:nr, :], C.bv_scr[r0:r0 + nr, :], writes=[k('bv')])
            S.dma('act', d['sgt'][:nr, :], C.sg_scr[r0:r0 + nr, :], writes=[k('sg')])
            RED(S, d['m8'][:nr, :], v3(d['y'][:nr, :]), ALU.add, [k('y')], [k('m8')])
            STT(S, v3(d['yc'][:nr, :]), d['m8'][:nr, :].unsqueeze(2).to_broadcast([nr, 8, 64]), -1.0 / 64,
                v3(d['y'][:nr, :]), ALU.mult, ALU.add, [k('m8'), k('y')], [k('yc')])
            TT(S, 'pool', d['sq'][:nr, :], d['yc'][:nr, :], d['yc'][:nr, :], ALU.mult, [k('yc')], [k('sq')])
            RED(S, d['v8'][:nr, :], v3(d['sq'][:nr, :]), ALU.add, [k('sq')], [k('v8')])
            ACT(S, d['v8'][:nr, :], d['v8'][:nr, :], AF.Ln, [k('v8')], [k('v8')], scale=1.0 / 64, bias=64e-5)
            ACT(S, d['v8'][:nr, :], d['v8'][:nr, :], AF.Exp, [k('v8')], [k('v8')], scale=-0.5)
            TT(S, 'dve', v3(d['yc'][:nr, :]), v3(d['yc'][:nr, :]), d['v8'][:nr, :].unsqueeze(2).to_broadcast([nr, 8, 64]),
               ALU.mult, [k('yc'), k('v8')], [k('yc')])
            TT(S, 'pool', d['yc'][:nr, :], d['yc'][:nr, :], lnw[:nr, :], ALU.mult, [k('yc'), 'lnw'], [k('yc')])
            TT(S, 'dve', d['yc'][:nr, :], d['yc'][:nr, :], lnb[:nr, :], ALU.add, [k('yc'), 'lnb'], [k('yc')])
            TT(S, 'pool', d['yc'][:nr, :], d['yc'][:nr, :], d['bvt'][:nr, :], ALU.add, [k('yc'), k('bv')], [k('yc')])
            TT(S, 'dve', d['o'][:nr, :], d['yc'][:nr, :], d['sgt'][:nr, :], ALU.mult, [k('yc'), k('sg')], [k('o')])
            S.dma('sp', br[r0:r0 + nr, 512:1024], d['o'][:nr, :], reads=[k('o')], writes=[('br', 1, ti)])


def phase_merge(C, l, xsrc, xdst):
    S, nc, I, O, ps, ident, br, z = C.S, C.nc, C.I, C.O, C.ps, C.ident, C.br, C.z
    tiles = token_tiles()
    alpha = float((2 * DEPTH) ** 0.25)
    with Phase(C) as ph:
        wbr = ph.T('wbr', [128, 12, 1024], BF16); wout = ph.T('wout', [128, 8, 1024], BF16)
        S.dma('pool', wbr[:], I['w_br'][l].rearrange("m (c p) n -> p (m c) n", p=128), writes=['wbr'])
        S.dma('pool', wout[:], I['w_out'][l].rearrange("(c p) n -> p c n", p=128), writes=['wout'])
        lng = ph.T('lng', [128, D]); lnb = ph.T('lnbb', [128, D])
        S.dma('sp', lng[:], I['ln_g'][l:l + 1, :].partition_broadcast(128), writes=['lng'])
        S.dma('sp', lnb[:], I['ln_b'][l:l + 1, :].partition_broadcast(128), writes=['lnbb'])
        bt = ph.T('mbt', [128, 1536]); gt = ph.T('mgt', [128, 3072]); xt = ph.T('mxt', [128, D])
        brT = ph.T('brT', [128, 12, 128], BF16); mg = ph.T('mmg', [128, D]); tmp = ph.T('mtmp', [128, 512])
        mT = ph.T('mT', [128, 8, 128], BF16); res = ph.T('mres', [128, D]); st6 = ph.T('mst6', [128, 2, 6])
        mv = ph.T('mmv', [128, 2]); rs = ph.T('mrs', [128, 1]); xo = ph.T('mxo', [128, D])
        for ti, (r0, nr) in enumerate(tiles):
            S.dma('sp', bt[:nr, :], br[r0:r0 + nr, :], writes=['mbt'])
            S.dma('act', gt[:nr, :], z[r0:r0 + nr, O_MG:O_MG + 3072], writes=['mgt'])
            S.dma('sp', xt[:nr, :], xsrc[r0:r0 + nr, :], writes=['mxt'])
            ACT(S, gt[:nr, :], gt[:nr, :], AF.Sigmoid, ['mgt'], ['mgt'])
            for q in range(3):
                pb, kp = ps[q % 2], 'ps%d' % (q % 2)
                for c4 in range(4):
                    TR(S, pb[:, c4 * 128:c4 * 128 + nr], bt[:nr, (q * 4 + c4) * 128:(q * 4 + c4 + 1) * 128], ident,
                       ['mbt'], [kp])
                CP(S, 'dve' if q % 2 == 0 else 'act', brT[:, q * 4:q * 4 + 4, :nr],
                   pb[:].rearrange("p (c t) -> p c t", c=4)[:, :, :nr], [kp], [('brT', q)])
            for hf in range(2):
                for m in range(3):
                    pi = 2 + (hf * 3 + m) % 3
                    pb, kp = ps[pi], 'ps%d' % pi
                    for c in range(4):
                        MM(S, pb[:nr, :], brT[:, m * 4 + c, :nr], wbr[:, m * 4 + c, hf * 512:(hf + 1) * 512], c == 0, c == 3,
                           [('brT', m), 'wbr'], [kp])
                    gsl = gt[:nr, m * 1024 + hf * 512:m * 1024 + (hf + 1) * 512]
                    if m == 0:
                        TT(S, 'dve', mg[:nr, hf * 512:(hf + 1) * 512], pb[:nr, :], gsl, ALU.mult, [kp, 'mgt'], [('mmg', hf)])
                    else:
                        TT(S, 'dve', tmp[:nr, :], pb[:nr, :], gsl, ALU.mult, [kp, 'mgt'], ['mtmp'])
                        TT(S, 'pool', mg[:nr, hf * 512:(hf + 1) * 512], mg[:nr, hf * 512:(hf + 1) * 512], tmp[:nr, :], ALU.add,
                           [('mmg', hf), 'mtmp'], [('mmg', hf)])
            for hf in range(2):
                pb, kp = ps[5 + hf], 'ps%d' % (5 + hf)
                for c4 in range(4):
                    c = hf * 4 + c4
                    TR(S, pb[:, c4 * 128:c4 * 128 + nr], mg[:nr, c * 128:(c + 1) * 128], ident, [('mmg', hf)], [kp])
                CP(S, 'dve' if hf == 0 else 'act', mT[:, hf * 4:hf * 4 + 4, :nr],
                   pb[:].rearrange("p (c t) -> p c t", c=4)[:, :, :nr], [kp], [('mT', hf)])
            for hf in range(2):
                pb, kp = ps[2 + hf], 'ps%d' % (2 + hf)
                for c in range(8):
                    MM(S, pb[:nr, :], mT[:, c, :nr], wout[:, c, hf * 512:(hf + 1) * 512], c == 0, c == 7,
                       [('mT', 0), ('mT', 1), 'wout'], [kp])
                STT(S, res[:nr, hf * 512:(hf + 1) * 512], xt[:nr, hf * 512:(hf + 1) * 512], alpha, pb[:nr, :], ALU.mult, ALU.add,
                    ['mxt', kp], [('mres', hf)])
                S.op('dve', lambda e: e.bn_stats(out=st6[:nr, hf, :], in_=res[:nr, hf * 512:(hf + 1) * 512]),
                     [('mres', hf)], [('mst6', hf)])
            S.op('dve', lambda e: e.bn_aggr(out=mv[:nr, :], in_=st6[:nr, :, :].rearrange("p a s -> p (a s)")),
                 [('mst6', 0), ('mst6', 1)], ['mmv'])
            ACT(S, rs[:nr, :], mv[:nr, 1:2], AF.Ln, ['mmv'], ['mrs'], bias=1e-5)
            ACT(S, rs[:nr, :], rs[:nr, :], AF.Exp, ['mrs'], ['mrs'], scale=-0.5)
            TS(S, 'dve', xo[:nr, :], res[:nr, :], mv[:nr, 0:1], rs[:nr, 0:1], ALU.subtract, ALU.mult,
               [('mres', 0), ('mres', 1), 'mmv', 'mrs'], ['mxo'])
            TT(S, 'pool', xo[:nr, :], xo[:nr, :], lng[:nr, :], ALU.mult, ['mxo', 'lng'], ['mxo'])
            TT(S, 'dve', xo[:nr, :], xo[:nr, :], lnb[:nr, :], ALU.add, ['mxo', 'lnbb'], ['mxo'])
            S.dma('sp', xdst[r0:r0 + nr, :], xo[:nr, :], reads=['mxo'], writes=[('xd', ti)],
                  is_output=(l == DEPTH - 1))
SLOPES = [2.0 ** (-(h + 1)) for h in range(8)]
NEG = -1.0e30


def IOTA(S, ap, pattern, base, cm, w):
    return S.op('pool', lambda e: e.iota(ap, pattern=pattern, base=base, channel_multiplier=cm,
                                        allow_small_or_imprecise_dtypes=True), (), w)


def phase_nsa_prompt(C, l):
    S, nc, I, O, ps, ident, z, br = C.S, C.nc, C.I, C.O, C.ps, C.ident, C.z, C.br
    NTI = T // 128
    with Phase(C) as ph:
        pool4 = ph.T('pool4', [128, 4]); pband = ph.T('pband', [128, 124])
        MEMSET(S, 'pool', pool4[:], 1.0 / 32, ['pool4'])
        ASEL(S, pool4[:], [[-32, 4]], 0, 1, ['pool4'])
        ASEL(S, pool4[:], [[32, 4]], 31, -1, ['pool4'])
        MEMSET(S, 'pool', pband[:], 1.0 / 32, ['pband'])
        ASEL(S, pband[:], [[-32, 124]], 1920, 1, ['pband'])
        ASEL(S, pband[:], [[32, 124]], -1889, -1, ['pband'])
        d0 = ph.T('d0', [128, 64]); B0 = ph.T('B0', [128, 8, 64])
        IOTA(S, d0[:], [[-32, 64]], -31, 1, ['d0'])
        for hd in range(8):
            TS(S, 'dve', B0[:, hd, :], d0[:], -SLOPES[hd], None, ALU.mult, ALU.bypass, ['d0'], ['B0'])
        penband = ph.T('penband', [128, 124]); m01band = ph.T('m01band', [128, 124])
        MEMSET(S, 'pool', penband[:], 0.0, ['penband'])
        ASEL(S, penband[:], [[-32, 124]], 1889, 1, ['penband'], fill=NEG)
        MEMSET(S, 'pool', m01band[:], 1.0, ['m01band'])
        ASEL(S, m01band[:], [[-32, 124]], 1889, 1, ['m01band'], fill=0.0)
        posb = ph.T('posb', [128, T])
        IOTA(S, posb[:], [[1, T]], 0, 0, ['posb'])
        causalpen = ph.T('causalpen', [128, 128])
        MEMSET(S, 'pool', causalpen[:], 0.0, ['causalpen'])
        ASEL(S, causalpen[:], [[-1, 128]], 0, 1, ['causalpen'], fill=NEG)
        penW = ph.T('penW', [128, 640])
        MEMSET(S, 'pool', penW[:], 0.0, ['penW'])
        ASEL(S, penW[:], [[1, 640]], 0, -1, ['penW'], fill=NEG)
        ASEL(S, penW[:], [[-1, 640]], 512, 1, ['penW'], fill=NEG)
        adj = ph.T('adj', [128, 62])
        MEMSET(S, 'pool', adj[:], 0.0, ['adj'])
        for hf in range(2):
            sl = adj[hf * 64:(hf + 1) * 64, :]
            ASEL(S, sl, [[-1, 62]], 30 + hf, 0, ['adj'], fill=-1.0e9)
            ASEL(S, sl, [[1, 62]], -(30 + hf), 0, ['adj'], op=ALU.not_equal, fill=1.0e9)
        KT = ph.T('KT', [64, 4, T], BF16)
        V = ph.T('Vsw', [128, NTI, 2, 128], BF16)
        kcT = ph.T('kcT', [64, 2, 64], BF16); vc = ph.T('vc', [64, 128], BF16)
        kvt = [ph.T('kvt%d' % i, [128, 768]) for i in range(2)]
        for ti in range(NTI):
            kt_, kk_ = kvt[ti % 2], 'kvt%d' % (ti % 2)
            S.dma('sp', kt_[:], z[ti * 128:(ti + 1) * 128, O_NKV:O_NKV + 768], writes=[kk_])
            for kv in range(2):
                MM(S, ps[6][:64, kv * 64 + 4 * ti:kv * 64 + 4 * ti + 4], kt_[:, kv * 64:(kv + 1) * 64], pool4[:], True, True,
                   [kk_, 'pool4'], ['ps6'])
            MM(S, ps[7][:64, :128], pband[:, 60 - 4 * ti:124 - 4 * ti], kt_[:, 128:256], ti == 0, ti == NTI - 1,
               [kk_, 'pband'], ['ps7'])
            pb, kp = ps[ti % 2], 'ps%d' % (ti % 2)
            for a in range(4):
                col = (256 if a < 2 else 512) + (a % 2) * 64
                TR(S, pb[:64, a * 128:(a + 1) * 128], kt_[:, col:col + 64], ident, [kk_], [kp])
            CP(S, 'dve', KT[:, :, ti * 128:(ti + 1) * 128], pb[:64, :].rearrange("p (a t) -> p a t", a=4), [kp], ['KT'])
            CP(S, 'act', V[:, ti, 0, :], kt_[:, 384:512], [kk_], ['Vsw'])
            CP(S, 'pool', V[:, ti, 1, :], kt_[:, 640:768], [kk_], ['Vsw'])
        CP(S, 'dve', kcT[:].rearrange("p k b -> p (k b)"), ps[6][:64, :128], ['ps6'], ['kcT'])
        CP(S, 'act', vc[:], ps[7][:64, :128], ['ps7'], ['vc'])
        qin = [ph.T('qin%d' % i, [128, 512]) for i in range(2)]
        gin = [ph.T('gin%d' % i, [128, 536]) for i in range(2)]
        qT = ph.T('qT', [64, 8, 128], BF16)
        gates = ph.T('gates', [128, 24]); sgate = ph.T('sgate', [128, 512])
        s1 = ph.T('s1', [128, 8, 64]); mx8 = ph.T('mx8', [128, 8]); den8 = ph.T('den8', [128, 8])
        t8 = ph.T('t8', [128, 8, 32]); imp = ph.T('imp', [128, 2, 32]); sc2 = ph.T('sc2', [128, 2, 32])
        m8a = ph.T('m8a', [128, 8]); m8b = ph.T('m8b', [128, 8]); sel01 = ph.T('sel01', [128, 2, 32])
        pcT = ph.T('pcT', [64, 8, 128], BF16)
        pens = [ph.T('pens%d' % i, [128, T]) for i in range(2)]
        ssb = ph.T('ssb', [128, T]); pex = ph.T('pex', [128, T]); pT = ph.T('pT', [128, 16, 128], BF16)
        mx1 = ph.T('mx1', [128, 1]); dens = ph.T('dens', [128, 8]); denw = ph.T('denw', [128, 8])
        g2 = ph.T('g2', [128, 8]); acc = ph.T('nacc', [128, 512]); tmpo = ph.T('ntmpo', [128, 512])
        v8 = lambda t: t.rearrange("p (h d) -> p h d", h=8)

        def attend(hd, kt0, kt1, aidx, vidx, pen_ap, pen_keys, obank, okey, dent):
            kv = hd // 4
            nkt = kt1 - kt0 + 1
            nk = nkt * 128
            k0 = kt0 * 128
            for ci, c in enumerate(range(0, nk, 512)):
                cw = min(512, nk - c)
                pb, kp = ps[ci % 2], 'ps%d' % (ci % 2)
                MM(S, pb[:, :cw], qT[:, hd, :], KT[:, aidx * 2 + kv, k0 + c:k0 + c + cw], True, True, ['qT', 'KT'], [kp])
                STT(S, ssb[:, c:c + cw], posb[:, k0 + c:k0 + c + cw], SLOPES[hd], pb[:, :cw], ALU.mult, ALU.add,
                    ['posb', kp], ['ssb'])
            TT(S, 'pool', ssb[:, :nk], ssb[:, :nk], pen_ap, ALU.add, ['ssb'] + pen_keys, ['ssb'])
            RED(S, mx1[:], ssb[:, :nk], ALU.max, ['ssb'], ['mx1'])
            TS(S, 'dve', mx1[:], mx1[:], -1.0, None, ALU.mult, ALU.bypass, ['mx1'], ['mx1'])
            ACT(S, pex[:, :nk], ssb[:, :nk], AF.Exp, ['ssb', 'mx1'], ['pex', dent[1]], bias=mx1[:, 0:1],
                accum_out=dent[0][:, hd:hd + 1])
            for g0 in range(0, nkt, 4):
                gn = min(4, nkt - g0)
                pi = 2 + (g0 // 4) % 2
                pb, kp = ps[pi], 'ps%d' % pi
                for j in range(gn):
                    TR(S, pb[:, j * 128:(j + 1) * 128], pex[:, (g0 + j) * 128:(g0 + j + 1) * 128], ident, ['pex'], [kp])
                CP(S, 'act' if (g0 // 4) % 2 == 0 else 'dve', pT[:, g0:g0 + gn, :],
                   pb[:, :gn * 128].rearrange("p (a t) -> p a t", a=gn), [kp], [('pT', g0)])
            for j in range(nkt):
                MM(S, obank[:, hd * 64:(hd + 1) * 64], pT[:, j, :], V[:, kt0 + j, vidx, kv * 64:(kv + 1) * 64],
                   j == 0, j == nkt - 1, [('pT', (j // 4) * 4), 'Vsw'], [okey])

        for ti in range(NTI):
            qb, kq = qin[ti % 2], 'qin%d' % (ti % 2)
            gb, kg = gin[ti % 2], 'gin%d' % (ti % 2)
            r0 = ti * 128
            S.dma('sp', qb[:], z[r0:r0 + 128, O_NQ:O_NQ + 512], writes=[kq])
            S.dma('act', gb[:], z[r0:r0 + 128, O_NBG:O_NBG + 536], writes=[kg])
            for hb in range(2):
                pb, kp = ps[2 + hb], 'ps%d' % (2 + hb)
                for h4 in range(4):
                    hd = hb * 4 + h4
                    TR(S, pb[:64, h4 * 128:(h4 + 1) * 128], qb[:, hd * 64:(hd + 1) * 64], ident, [kq], [kp])
                TS(S, 'dve', qT[:, hb * 4:hb * 4 + 4, :], pb[:64, :].rearrange("p (h t) -> p h t", h=4), 0.125, None,
                   ALU.mult, ALU.bypass, [kp], ['qT'])
            ACT(S, gates[:], gb[:, 0:24], AF.Sigmoid, [kg], ['gates'])
            ACT(S, sgate[:], gb[:, 24:536], AF.Silu, [kg], ['sgate'])
            g3 = gates[:].rearrange("p (h j) -> p h j", j=3)
            for hd in range(8):
                MM(S, ps[6][:, hd * 64:(hd + 1) * 64], qT[:, hd, :], kcT[:, hd // 4, :], True, True, ['qT', 'kcT'], ['ps6'])
            TT(S, 'dve', s1[:], ps[6][:].rearrange("p (h b) -> p h b", h=8), B0[:], ALU.add, ['ps6', 'B0'], ['s1'])
            TT(S, 'pool', s1[:], s1[:], penband[:, 60 - 4 * ti:124 - 4 * ti].unsqueeze(1).to_broadcast([128, 8, 64]), ALU.add,
               ['s1', 'penband'], ['s1'])
            RED(S, mx8[:], s1[:], ALU.max, ['s1'], ['mx8'])
            TT(S, 'dve', s1[:], s1[:], mx8[:].unsqueeze(2).to_broadcast([128, 8, 64]), ALU.subtract, ['s1', 'mx8'], ['s1'])
            ACT(S, s1[:], s1[:], AF.Exp, ['s1'], ['s1'])
            TT(S, 'pool', s1[:], s1[:], m01band[:, 60 - 4 * ti:124 - 4 * ti].unsqueeze(1).to_broadcast([128, 8, 64]), ALU.mult,
               ['s1', 'm01band'], ['s1'])
            RED(S, den8[:], s1[:], ALU.add, ['s1'], ['den8'])
            TS(S, 'dve', den8[:], den8[:], 1e-30, None, ALU.max, ALU.bypass, ['den8'], ['den8'])
            S.op('dve', lambda e: e.reciprocal(den8[:], den8[:]), ['den8'], ['den8'])
            TT(S, 'dve', s1[:], s1[:], den8[:].unsqueeze(2).to_broadcast([128, 8, 64]), ALU.mult, ['s1', 'den8'], ['s1'])
            for hb in range(2):
                pb, kp = ps[2 + hb], 'ps%d' % (2 + hb)
                for h4 in range(4):
                    TR(S, pb[:64, h4 * 128:(h4 + 1) * 128], s1[:, hb * 4 + h4, :], ident, ['s1'], [kp])
                CP(S, 'act', pcT[:, hb * 4:hb * 4 + 4, :], pb[:64, :].rearrange("p (h t) -> p h t", h=4), [kp], ['pcT'])
            for hd in range(8):
                kv = hd // 4
                MM(S, ps[7][:, hd * 64:(hd + 1) * 64], pcT[:, hd, :], vc[:, kv * 64:(kv + 1) * 64], True, True,
                   ['pcT', 'vc'], ['ps7'])
            TT(S, 'dve', v8(acc[:]), v8(ps[7][:]), g3[:, :, 0:1].to_broadcast([128, 8, 64]), ALU.mult,
               ['ps7', 'gates'], ['nacc'])
            nk = (ti + 1) * 128
            if ti >= 8:
                RED(S, t8[:], s1[:].rearrange("p h (s w) -> p h s w", w=2), ALU.add, ['s1'], ['t8'])
                RED(S, imp[:], t8[:].rearrange("p (k g) s -> p k s g", g=4), ALU.add, ['t8'], ['imp'])
                TT(S, 'dve', imp[:], imp[:], adj[:, 30 - 2 * ti:62 - 2 * ti].unsqueeze(1).to_broadcast([128, 2, 32]), ALU.add,
                   ['imp', 'adj'], ['imp'])
                MEMSET(S, 'dve', imp[:, :, 0:1], 1.0e9, ['imp'])
                for kv in range(2):
                    S.op('dve', lambda e: e.max(out=m8a[:], in_=imp[:, kv, :]), ['imp'], ['m8a'])
                    S.op('dve', lambda e: e.match_replace(out=sc2[:, kv, :], in_to_replace=m8a[:], in_values=imp[:, kv, :],
                                                          imm_value=-3.0e9), ['imp', 'm8a'], ['sc2'])
                    S.op('dve', lambda e: e.max(out=m8b[:], in_=sc2[:, kv, :]), ['sc2'], ['m8b'])
                    TS(S, 'dve', sel01[:, kv, :], imp[:, kv, :], m8b[:, 7:8], None, ALU.is_ge, ALU.bypass, ['imp', 'm8b'],
                       ['sel01'])
                    pk = pens[kv]
                    nb = 2 * (ti + 1)
                    TS(S, 'pool' if kv == 0 else 'dve', pk[:, :nk].rearrange("p (b s) -> p b s", s=64),
                       sel01[:, kv, :nb].unsqueeze(2).to_broadcast([128, nb, 64]), 1.0e30, NEG, ALU.mult, ALU.add,
                       ['sel01'], ['pens%d' % kv])
                    TT(S, 'dve', pk[:, ti * 128:nk], pk[:, ti * 128:nk], causalpen[:], ALU.add,
                       ['pens%d' % kv, 'causalpen'], ['pens%d' % kv])
                penk = lambda kv: (pens[kv][:, :nk], ['pens%d' % kv])
            else:
                MEMSET(S, 'pool', pens[0][:, :nk], 0.0, ['pens0'])
                CP(S, 'pool', pens[0][:, ti * 128:nk], causalpen[:], ['causalpen', 'pens0'], ['pens0'])
                penk = lambda kv: (pens[0][:, :nk], ['pens0'])
            wt0 = max(0, ti - 4)
            wc0 = (wt0 - (ti - 4)) * 128
            wnk = (ti - wt0 + 1) * 128
            for hd in range(8):
                pa, pkeys = penk(hd // 4)
                attend(hd, 0, ti, 0, 0, pa, pkeys, ps[4], 'ps4', (dens, 'dens'))
                attend(hd, wt0, ti, 1, 1, penW[:, wc0:wc0 + wnk], ['penW'], ps[5], 'ps5', (denw, 'denw'))
            for (dn, kd, bank, kb, j) in ((dens, 'dens', ps[4], 'ps4', 1), (denw, 'denw', ps[5], 'ps5', 2)):
                S.op('dve', lambda e: e.reciprocal(dn[:], dn[:]), [kd], [kd])
                TT(S, 'dve', g2[:], dn[:], g3[:, :, j], ALU.mult, [kd, 'gates'], ['g2'])
                TT(S, 'dve', v8(tmpo[:]), v8(bank[:]), g2[:].unsqueeze(2).to_broadcast([128, 8, 64]), ALU.mult,
                   [kb, 'g2'], ['ntmpo'])
                TT(S, 'pool', acc[:], acc[:], tmpo[:], ALU.add, ['nacc', 'ntmpo'], ['nacc'])
            TT(S, 'dve', acc[:], acc[:], sgate[:], ALU.mult, ['nacc', 'sgate'], ['nacc'])
            S.dma('sp', br[r0:r0 + 128, 1024:1536], acc[:], reads=['nacc'], writes=[('br', 2, ti)])
def phase_nsa_sample(C, l):
    S, nc, I, O, ps, ident, z, br = C.S, C.nc, C.I, C.O, C.ps, C.ident, C.z, C.br
    cache = I['cache%d' % l]
    PAST = NPAGES * PAGE
    with Phase(C) as ph:
        pool4 = ph.T('spool4', [128, 4]); pband = ph.T('spband', [128, 252])
        MEMSET(S, 'pool', pool4[:], 1.0 / 32, ['pool4'])
        ASEL(S, pool4[:], [[-32, 4]], 0, 1, ['pool4'])
        ASEL(S, pool4[:], [[32, 4]], 31, -1, ['pool4'])
        MEMSET(S, 'pool', pband[:], 1.0 / 32, ['pband'])
        ASEL(S, pband[:], [[-32, 252]], 3968, 1, ['pband'])
        ASEL(S, pband[:], [[32, 252]], -3937, -1, ['pband'])
        pidx = ph.T('pidx', [128, 1])
        IOTA(S, pidx[:], [[0, 1]], 0, 1, ['pidx'])
        G = ph.T('Gm', [16, 4]); GT = ph.T('GTm', [4, 16])
        MEMSET(S, 'pool', G[:], 0.0, ['G']); MEMSET(S, 'pool', GT[:], 0.0, ['GT'])
        for g in range(4):
            ASEL(S, G[:], [[-1, 4]], -4 * g, 1, ['G'], op=ALU.not_equal, fill=1.0)
            ASEL(S, GT[:], [[1, 16]], -4 * g, -1, ['GT'], op=ALU.not_equal, fill=1.0)
        slrow = ph.T('slrow', [1, 2, 16]); one11 = ph.T('one11', [1, 1]); slr = ph.T('slr', [16, 2])
        for kv in range(2):
            for g in range(4):
                MEMSET(S, 'dve', slrow[:, kv, g * 4:(g + 1) * 4], SLOPES[kv * 4 + g], ['slrow'])
        MEMSET(S, 'dve', one11[:], 1.0, ['one11'])
        for kv in range(2):
            MM(S, ps[0][:16, kv:kv + 1], slrow[:, kv, :], one11[:], True, True, ['slrow', 'one11'], ['ps0'])
        CP(S, 'dve', slr[:], ps[0][:16, 0:2], ['ps0'], ['slr'])
        CM = ph.T('CMm', [4, 4]); CMW = ph.T('CMW', [4, 512]); penw = ph.T('penw', [16, 516]); pennew = ph.T('pennew', [16, 4])
        MEMSET(S, 'pool', CM[:], 0.0, ['CM']); ASEL(S, CM[:], [[-1, 4]], 0, 1, ['CM'], fill=NEG)
        MEMSET(S, 'pool', CMW[:], 0.0, ['CMW']); ASEL(S, CMW[:], [[1, 512]], 0, -1, ['CMW'], fill=NEG)
        MM(S, ps[1][:16, 0:512], GT[:], CMW[:], True, True, ['GT', 'CMW'], ['ps1'])
        MM(S, ps[2][:16, 0:4], GT[:], CM[:], True, True, ['GT', 'CM'], ['ps2'])
        CP(S, 'dve', penw[:, 0:512], ps[1][:16, 0:512], ['ps1'], ['penw'])
        CP(S, 'dve', penw[:, 512:516], ps[2][:16, 0:4], ['ps2'], ['penw'])
        CP(S, 'dve', pennew[:], ps[2][:16, 0:4], ['ps2'], ['pennew'])
        posl = ph.T('posl', [16, 2048]); blk32 = ph.T('blk32', [16, 512])
        IOTA(S, posl[:], [[1, 2048]], 0, 0, ['posl'])
        IOTA(S, blk32[:], [[32, 512]], 0, 0, ['blk32'])
        KsT = ph.T('KsT', [128, PAST], BF16); Vs = ph.T('Vs', [128, NPAGES, 128], BF16)
        kcT = ph.T('skcT', [128, 512], BF16); vcs = ph.T('svc', [128, 4, 128], BF16)
        pgb = [ph.T('pgb%d' % i, [128, 512]) for i in range(3)]
        pti = ph.T('pti', [128, NPAGES], I32); ptf = ph.T('ptf', [128, NPAGES]); idx = ph.T('idx', [128, NPAGES], I32)
        qz = ph.T('sqz', [4, 512]); gz = ph.T('sgz', [4, 536]); kvz = ph.T('skvz', [4, 768]); q2 = ph.T('sq2', [4, 4, 128])
        qT2 = ph.T('sqT2', [128, 16], BF16); knT = ph.T('sknT', [128, 2, 4], BF16); vn = ph.T('svn', [4, 2, 128], BF16)
        wbuf = ph.T('swbuf', [128, 4, 256]); KwT = ph.T('sKwT', [128, 516], BF16); Vw = ph.T('sVw', [128, 4, 128], BF16)
        gsig = ph.T('sgsig', [4, 24]); sgate = ph.T('ssgate', [4, 512])
        sc = ph.T('ssc', [16, 2048]); bias_c = ph.T('sbias', [16, 2048]); pex = ph.T('spex', [16, 2048])
        mx = ph.T('smx', [16, 1]); rmx = ph.T('srmx', [16, 1]); den = ph.T('sden', [16, 1]); dc = ph.T('sdc', [16, 1])
        pn = ph.T('spn', [16, 512]); imp = ph.T('simp', [4, 256]); isc = ph.T('sisc', [4, 256])
        m8a = ph.T('sm8a', [4, 8]); m8b = ph.T('sm8b', [4, 8]); sel = ph.T('ssel', [4, 256]); sel16 = ph.T('ssel16', [16, 256])
        pT = ph.T('spT', [128, 16, 16], BF16); pnT = ph.T('spnT', [4, 16], BF16)
        osb = [ph.T('sosb%d' % j, [16, 64]) for j in range(3)]
        snew = ph.T('ssnew', [16, 4]); pnew = ph.T('spnew', [16, 4]); biasw = ph.T('sbiasw', [16, 512])
        acc = ph.T('sacc', [4, 512]); tmpo = ph.T('stmpo', [4, 512])
        kpg = 0
        for b in range(NS_B):
            S.dma('sp', pti[:], I['ptab'][b:b + 1, :].partition_broadcast(128), writes=['pti'])
            CP(S, 'dve', ptf[:], pti[:], ['pti'], ['ptf'])
            STT(S, ptf[:], ptf[:], float(PAGE), pidx[:, 0:1].to_broadcast([128, NPAGES]), ALU.mult, ALU.add,
                ['ptf', 'pidx'], ['ptf'])
            CP(S, 'dve', idx[:], ptf[:], ['ptf'], ['idx'])
            for pg in range(NPAGES):
                pb_, kpb = pgb[kpg % 3], 'pgb%d' % (kpg % 3)
                S.dma('pool', None, None, reads=['idx'], writes=[kpb],
                      fn=lambda e: e.indirect_dma_start(out=pb_[:], out_offset=None, in_=cache,
                                                        in_offset=bass.IndirectOffsetOnAxis(ap=idx[:, pg:pg + 1], axis=0)))
                MM(S, ps[0][:, pg * 4:pg * 4 + 4], pb_[:, 0:128], pool4[:], True, True, [kpb, 'pool4'], ['ps0'])
                j32 = pg % 32
                MM(S, ps[1][:, :128], pband[:, 124 - 4 * j32:252 - 4 * j32], pb_[:, 128:256], j32 == 0, j32 == 31,
                   [kpb, 'pband'], ['ps1'])
                if j32 == 31:
                    CP(S, 'act', vcs[:, pg // 32, :], ps[1][:, :128], ['ps1'], ['vcs'])
                pi = 2 + (pg // 4) % 2
                TR(S, ps[pi][:, (pg % 4) * 128:(pg % 4 + 1) * 128], pb_[:, 256:384], ident, [kpb], ['ps%d' % pi])
                if pg % 4 == 3:
                    CP(S, 'dve', KsT[:, (pg - 3) * 128:(pg + 1) * 128], ps[pi][:, :], ['ps%d' % pi], ['KsT'])
                CP(S, 'act' if pg % 2 == 0 else 'pool', Vs[:, pg, :], pb_[:, 384:512], [kpb], ['Vs'])
                kpg += 1
            CP(S, 'dve', kcT[:], ps[0][:, :], ['ps0'], ['kcT'])
            rows = slice(T + b * NS_T, T + (b + 1) * NS_T)
            S.dma('sp', qz[:], z[rows, O_NQ:O_NQ + 512], writes=['qz'])
            S.dma('act', gz[:], z[rows, O_NBG:O_NBG + 536], writes=['gz'])
            S.dma('sp', kvz[:], z[rows, O_NKV:O_NKV + 768], writes=['kvz'])
            S.dma('act', wbuf[:], I['win'][l, b].rearrange("(a p) c -> p a c", p=128), writes=['wbuf'])
            CP(S, 'dve', q2[:].rearrange("p g (k d) -> p g k d", k=2), qz[:].rearrange("p (k g d) -> p g k d", k=2, g=4),
               ['qz'], ['q2'])
            for g in range(4):
                TR(S, ps[2][:, g * 4:(g + 1) * 4], q2[:, g, :], ident, ['q2'], ['ps2'])
            TS(S, 'dve', qT2[:], ps[2][:, 0:16], 0.125, None, ALU.mult, ALU.bypass, ['ps2'], ['qT2'])
            TR(S, ps[3][:, 0:4], kvz[:, 256:384], ident, ['kvz'], ['ps3'])
            TR(S, ps[3][:, 4:8], kvz[:, 512:640], ident, ['kvz'], ['ps3'])
            CP(S, 'dve', knT[:].rearrange("p a t -> p (a t)"), ps[3][:, 0:8], ['ps3'], ['knT'])
            CP(S, 'act', vn[:, 0, :], kvz[:, 384:512], ['kvz'], ['vn'])
            CP(S, 'act', vn[:, 1, :], kvz[:, 640:768], ['kvz'], ['vn'])
            for a in range(4):
                TR(S, ps[2][:, a * 128:(a + 1) * 128], wbuf[:, a, 0:128], ident, ['wbuf', 'qT2'], ['ps2'])
            CP(S, 'dve', KwT[:, 0:512], ps[2][:, :], ['ps2'], ['KwT'])
            CP(S, 'dve', KwT[:, 512:516], knT[:, 1, :], ['knT', 'KwT'], ['KwT'])
            CP(S, 'pool', Vw[:], wbuf[:, :, 128:256], ['wbuf'], ['Vw'])
            ACT(S, gsig[:], gz[:, 0:24], AF.Sigmoid, ['gz'], ['gsig'])
            ACT(S, sgate[:], gz[:, 24:536], AF.Silu, ['gz'], ['sgate'])
            for kv in range(2):
                p0 = kv * 64
                lq = qT2[p0:p0 + 64, :]
                sl = slr[:, kv:kv + 1]
                MM(S, ps[1][:16, :], lq, kcT[p0:p0 + 64, :], True, True, ['qT2', 'kcT'], ['ps1'])
                STT(S, sc[:, :512], blk32[:], sl, ps[1][:16, :], ALU.mult, ALU.add, ['blk32', 'slr', 'ps1'], ['sc'])
                RED(S, mx[:], sc[:, :512], ALU.max, ['sc'], ['mx'])
                TS(S, 'dve', mx[:], mx[:], -1.0, None, ALU.mult, ALU.bypass, ['mx'], ['mx'])
                ACT(S, pn[:], sc[:, :512], AF.Exp, ['sc', 'mx'], ['pn', 'den'], bias=mx[:, 0:1], accum_out=den[:, 0:1])
                S.op('dve', lambda e: e.reciprocal(den[:], den[:]), ['den'], ['den'])
                TS(S, 'dve', pn[:], pn[:], den[:, 0:1], None, ALU.mult, ALU.bypass, ['pn', 'den'], ['pn'])
                MM(S, ps[2][:4, :], G[:], pn[:], True, True, ['G', 'pn'], ['ps2'])
                RED(S, imp[:], ps[2][:4, :].rearrange("p (s w) -> p s w", w=2), ALU.add, ['ps2'], ['imp'])
                for a in range(4):
                    TR(S, ps[3][:, a * 16:(a + 1) * 16], pn[:, a * 128:(a + 1) * 128], ident, ['pn'], ['ps3'])
                CP(S, 'act', pT[:, 0:4, :].rearrange("p a r -> p (a r)"), ps[3][:, 0:64], ['ps3'], ['pT'])
                for a in range(4):
                    MM(S, ps[4][:16, 0:64], pT[:, a, :], vcs[:, a, p0:p0 + 64], a == 0, a == 3, ['pT', 'vcs'], ['ps4'])
                CP(S, 'dve', osb[0][:], ps[4][:16, 0:64], ['ps4'], ['osb0'])
                MEMSET(S, 'dve', imp[:, 0:1], -1.0, ['imp'])
                S.op('dve', lambda e: e.max(out=m8a[:], in_=imp[:]), ['imp'], ['m8a'])
                S.op('dve', lambda e: e.match_replace(out=isc[:], in_to_replace=m8a[:], in_values=imp[:], imm_value=-3.0),
                     ['imp', 'm8a'], ['isc'])
                S.op('dve', lambda e: e.max(out=m8b[:], in_=isc[:]), ['isc'], ['m8b'])
                TS(S, 'dve', sel[:], imp[:], m8b[:, 5:6], None, ALU.is_ge, ALU.bypass, ['imp', 'm8b'], ['sel'])
                MEMSET(S, 'dve', sel[:, 0:1], 1.0, ['sel'])
                MM(S, ps[2][:16, 0:256], GT[:], sel[:], True, True, ['GT', 'sel'], ['ps2'])
                CP(S, 'dve', sel16[:], ps[2][:16, 0:256], ['ps2'], ['sel16'])
                MM(S, ps[1][:16, 0:4], lq, knT[p0:p0 + 64, 0, :], True, True, ['qT2', 'knT'], ['ps1'])
                STT(S, snew[:], posl[:, 0:4], sl, ps[1][:16, 0:4], ALU.mult, ALU.add, ['posl', 'slr', 'ps1'], ['snew'])
                TT(S, 'dve', snew[:], snew[:], pennew[:], ALU.add, ['snew', 'pennew'], ['snew'])
                RED(S, rmx[:], snew[:], ALU.max, ['snew'], ['rmx'])

                def scores(c):
                    TS(S, 'dve', bias_c[:], posl[:], float(2048 * c - PAST), sl, ALU.add, ALU.mult, ['posl', 'slr'], ['bias_c'])
                    TS(S, 'pool', pex[:].rearrange("p (b s) -> p b s", s=64),
                       sel16[:, c * 32:(c + 1) * 32].unsqueeze(2).to_broadcast([16, 32, 64]), 1.0e30, NEG, ALU.mult, ALU.add,
                       ['sel16'], ['pex'])
                    TT(S, 'dve', bias_c[:], bias_c[:], pex[:], ALU.add, ['bias_c', 'pex'], ['bias_c'])
                    for c4 in range(4):
                        pb2, kp2 = ps[1 + c4 % 2], 'ps%d' % (1 + c4 % 2)
                        MM(S, pb2[:16, :], lq, KsT[p0:p0 + 64, c * 2048 + c4 * 512:c * 2048 + (c4 + 1) * 512], True, True,
                           ['qT2', 'KsT'], [kp2])
                        TT(S, 'dve', sc[:, c4 * 512:(c4 + 1) * 512], pb2[:16, :], bias_c[:, c4 * 512:(c4 + 1) * 512], ALU.add,
                           [kp2, 'bias_c'], ['sc'])

                for c in range(8):
                    scores(c)
                    RED(S, mx[:], sc[:], ALU.max, ['sc'], ['mx'])
                    TT(S, 'dve', rmx[:], rmx[:], mx[:], ALU.max, ['rmx', 'mx'], ['rmx'])
                TS(S, 'dve', rmx[:], rmx[:], -1.0, None, ALU.mult, ALU.bypass, ['rmx'], ['rmx'])
                ACT(S, pnew[:], snew[:], AF.Exp, ['snew', 'rmx'], ['pnew', 'den'], bias=rmx[:, 0:1], accum_out=den[:, 0:1])
                for c in range(8):
                    scores(c)
                    ACT(S, pex[:], sc[:], AF.Exp, ['sc', 'rmx'], ['pex', 'dc'], bias=rmx[:, 0:1], accum_out=dc[:, 0:1])
                    TT(S, 'dve', den[:], den[:], dc[:], ALU.add, ['den', 'dc'], ['den'])
                    for j in range(16):
                        TR(S, ps[3][:, j * 16:(j + 1) * 16], pex[:, j * 128:(j + 1) * 128], ident, ['pex'], ['ps3'])
                    CP(S, 'act', pT[:].rearrange("p a r -> p (a r)"), ps[3][:, 0:256], ['ps3'], ['pT'])
                    for j in range(16):
                        MM(S, ps[4][:16, 0:64], pT[:, j, :], Vs[:, c * 16 + j, p0:p0 + 64], c == 0 and j == 0, False,
                           ['pT', 'Vs'], ['ps4'])
                TR(S, ps[3][:4, 0:16], pnew[:], ident, ['pnew'], ['ps3'])
                CP(S, 'act', pnT[:], ps[3][:4, 0:16], ['ps3'], ['pnT'])
                MM(S, ps[4][:16, 0:64], pnT[:], vn[:, 0, p0:p0 + 64], False, True, ['pnT', 'vn'], ['ps4'])
                S.op('dve', lambda e: e.reciprocal(den[:], den[:]), ['den'], ['den'])
                TS(S, 'dve', osb[1][:], ps[4][:16, 0:64], den[:, 0:1], None, ALU.mult, ALU.bypass, ['ps4', 'den'], ['osb1'])
                TS(S, 'dve', biasw[:], posl[:, 0:512], -512.0, sl, ALU.add, ALU.mult, ['posl', 'slr'], ['biasw'])
                MM(S, ps[1][:16, :], lq, KwT[p0:p0 + 64, 0:512], True, True, ['qT2', 'KwT'], ['ps1'])
                MM(S, ps[2][:16, 0:4], lq, KwT[p0:p0 + 64, 512:516], True, True, ['qT2', 'KwT'], ['ps2'])
                TT(S, 'dve', sc[:, 0:512], ps[1][:16, :], biasw[:], ALU.add, ['ps1', 'biasw'], ['sc'])
                STT(S, sc[:, 512:516], posl[:, 0:4], sl, ps[2][:16, 0:4], ALU.mult, ALU.add, ['posl', 'slr', 'ps2', 'sc'], ['sc'])
                TT(S, 'dve', sc[:, 0:516], sc[:, 0:516], penw[:], ALU.add, ['sc', 'penw'], ['sc'])
                RED(S, mx[:], sc[:, 0:516], ALU.max, ['sc'], ['mx'])
                TS(S, 'dve', mx[:], mx[:], -1.0, None, ALU.mult, ALU.bypass, ['mx'], ['mx'])
                ACT(S, pex[:, 0:516], sc[:, 0:516], AF.Exp, ['sc', 'mx'], ['pex', 'den'], bias=mx[:, 0:1], accum_out=den[:, 0:1])
                for a in range(4):
                    TR(S, ps[3][:, a * 16:(a + 1) * 16], pex[:, a * 128:(a + 1) * 128], ident, ['pex'], ['ps3'])
                TR(S, ps[3][:4, 64:80], pex[:, 512:516], ident, ['pex'], ['ps3'])
                CP(S, 'act', pT[:, 0:4, :].rearrange("p a r -> p (a r)"), ps[3][:, 0:64], ['ps3'], ['pT'])
                CP(S, 'act', pnT[:], ps[3][:4, 64:80], ['ps3'], ['pnT'])
                for a in range(4):
                    MM(S, ps[4][:16, 0:64], pT[:, a, :], Vw[:, a, p0:p0 + 64], a == 0, False, ['pT', 'Vw'], ['ps4'])
                MM(S, ps[4][:16, 0:64], pnT[:], vn[:, 1, p0:p0 + 64], False, True, ['pnT', 'vn'], ['ps4'])
                S.op('dve', lambda e: e.reciprocal(den[:], den[:]), ['den'], ['den'])
                TS(S, 'dve', osb[2][:], ps[4][:16, 0:64], den[:, 0:1], None, ALU.mult, ALU.bypass, ['ps4', 'den'], ['osb2'])
                for j in range(3):
                    for g in range(4):
                        hd = kv * 4 + g
                        MM(S, ps[5 + j][:4, hd * 64:(hd + 1) * 64], ident[:16, g * 4:(g + 1) * 4], osb[j][:], True, True,
                           ['ident', 'osb%d' % j], ['ps%d' % (5 + j)])
            g3 = gsig[:].rearrange("p (h j) -> p h j", j=3)
            v8 = lambda t: t.rearrange("p (h d) -> p h d", h=8)
            for j in range(3):
                dst = acc if j == 0 else tmpo
                TT(S, 'dve', v8(dst[:]), v8(ps[5 + j][:4, :]), g3[:, :, j:j + 1].to_broadcast([4, 8, 64]), ALU.mult,
                   ['ps%d' % (5 + j), 'gsig'], ['sacc' if j == 0 else 'stmpo'])
                if j > 0:
                    TT(S, 'dve', acc[:], acc[:], tmpo[:], ALU.add, ['sacc', 'stmpo'], ['sacc'])
            TT(S, 'dve', acc[:], acc[:], sgate[:], ALU.mult, ['sacc', 'sgate'], ['sacc'])
            S.dma('sp', br[rows, 1024:1536], acc[:], reads=['sacc'], writes=[('br', 3, b)])


HAVE_NSA_SAMPLE = True


def build_program(debug=False):
    nc = bass.Bass("TRN2", target_bir_lowering=False)
    C = Ctx()
    C.nc = nc
    C.uid = 0
    dt_in = lambda name, shape, dt=F32: nc.dram_tensor(name, shape, dt, kind="ExternalInput").ap()
    dt_out = lambda name, shape, dt=F32: nc.dram_tensor(name, shape, dt, kind="ExternalOutput").ap()
    dt_scr = lambda name, shape, dt=F32: nc.dram_tensor(name, shape, dt, kind="Internal").ap()
    I = {}
    I['x'] = dt_in("x", [NTOK, D])
    I['w_in'] = dt_in("w_in", [DEPTH, D, INW])
    I['b_in'] = dt_in("b_in", [DEPTH, INW])
    I['win'] = dt_in("win", [DEPTH, NS_B, WINB, 256])
    I['shift'] = dt_in("shift", [DEPTH, NS_B, RWIN])
    I['gla_st'] = dt_in("gla_st", [DEPTH, NS_B, 4, 64, 128])
    I['rw_st'] = dt_in("rw_st", [DEPTH, NS_B, 8, 64, 64])
    I['gla_a_up'] = dt_in("gla_a_up", [DEPTH, 16, 256])
    I['gla_a_bias'] = dt_in("gla_a_bias", [DEPTH, 256])
    I['gla_norm'] = dt_in("gla_norm", [DEPTH, 512])
    I['rwkv_mu'] = dt_in("rwkv_mu", [DEPTH, RWIN])
    for nm in ('rwkv_w0', 'rwkv_a0', 'rwkv_k_k', 'rwkv_k_a', 'rwkv_r_k', 'rwkv_ln_w', 'rwkv_ln_b'):
        I[nm] = dt_in(nm, [DEPTH, 512])
    I['rwkv_w_up'] = dt_in("rwkv_w_up", [DEPTH, 64, 512])
    I['rwkv_a_up'] = dt_in("rwkv_a_up", [DEPTH, 64, 512])
    I['w_br'] = dt_in("w_br", [DEPTH, 3, 512, D])
    I['w_out'] = dt_in("w_out", [DEPTH, D, D])
    I['ln_g'] = dt_in("ln_g", [DEPTH, D])
    I['ln_b'] = dt_in("ln_b", [DEPTH, D])
    I['ptab'] = dt_in("ptab", [NS_B, NPAGES], I32)
    if HAVE_NSA_SAMPLE:
        for ll in range(DEPTH):
            I['cache%d' % ll] = dt_in("cache%d" % ll, [NPOOL * PAGE, 512])
    O = {}
    O['y'] = dt_out("y", [NTOK, D])
    O['kv'] = dt_out("kv", [DEPTH, NTOK, 512])
    O['winp'] = dt_out("winp", [DEPTH, WINB, 256])
    O['wins'] = dt_out("wins", [DEPTH, NS_B, WINB, 256])
    O['shp'] = dt_out("shp", [DEPTH, RWIN])
    O['shs'] = dt_out("shs", [DEPTH, NS_B, RWIN])
    O['glap'] = dt_out("glap", [DEPTH, 4, 64, 128])
    O['glas'] = dt_out("glas", [DEPTH, NS_B, 4, 64, 128])
    O['rwp'] = dt_out("rwp", [DEPTH, 8, 64, 64])
    O['rws'] = dt_out("rws", [DEPTH, NS_B, 8, 64, 64])
    C.I, C.O = I, O
    C.z = dt_scr("z", [NTOK, INW])
    C.xs = dt_scr("xs", [NTOK, D])
    if debug:
        C.br = dt_out("br", [NTOK, 1536])
    else:
        C.br = dt_scr("br", [NTOK, 1536])
    C.sg_scr = dt_scr("sg_scr", [NTOK, 512])
    C.bv_scr = dt_scr("bv_scr", [NTOK, 512])
    C.y_scr = dt_scr("y_scr", [NTOK, 512])
    C.rk_scr = dt_scr("rk_scr", [NTOK, 3, 512], BF16)
    C.fm_scr = dt_scr("fm_scr", [3, 64, 8, NTOK])
    S = Sched(nc)
    C.S = S
    with contextlib.ExitStack() as gst:
        C.ps = [gst.enter_context(nc.psum_tensor("ps%d" % i, [128, 512], F32)) for i in range(8)]
        C.ident = gst.enter_context(nc.sbuf_tensor("ident", [128, 128], F32))
        zt = gst.enter_context(nc.sbuf_tensor("zerot", [128, 512], F32))
        MEMSET(S, 'pool', C.ident[:], 1.0, ['ident'])
        ASEL(S, C.ident[:], [[-1, 128]], 0, 1, ['ident'], op=ALU.is_equal)
        MEMSET(S, 'dve', zt[:], 0.0, ['zerot'])
        nl = 1 if debug else DEPTH
        for l in range(nl):
            xsrc = I['x'] if l == 0 else C.xs
            xdst = O['y'] if l == DEPTH - 1 else C.xs
            phase_inproj(C, l, xsrc)
            direct_outputs(C, l)
            phase_gla(C, l)
            phase_rwkv_pre(C, l)
            phase_rwkv_scan(C, l)
            phase_rwkv_post(C, l)
            phase_nsa_prompt(C, l)
            if not HAVE_NSA_SAMPLE:
                S.dma('sp', C.br[T:NTOK, 1024:1536], zt[:NSAMP, :], reads=['zerot'])
                S.barrier()
            else:
                phase_nsa_sample(C, l)
            phase_merge(C, l, xsrc, xdst)
        S.finish()
    print("program built: n_inst", S.n_inst, flush=True)
    return nc


_NC_CACHE = {}


def kernel(x_prompt, x_sample, cache_nsa_kv, state_nsa_win, state_gla, state_rwkv, state_rwkv_shift,
           page_table, w_in, b_in, gla_a_up, gla_a_bias, gla_norm, rwkv_mu, rwkv_w0, rwkv_w_up,
           rwkv_a0, rwkv_a_up, rwkv_k_k, rwkv_k_a, rwkv_r_k, rwkv_ln_w, rwkv_ln_b, w_br, w_out, ln_g, ln_b,
           _debug=False, _cores=NCORES):
    f = lambda a: np.ascontiguousarray(np.asarray(a, dtype=np.float32))
    key = 'nc_dbg' if _debug else 'nc'
    if key not in _NC_CACHE:
        _NC_CACHE[key] = build_program(debug=_debug)
    nc = _NC_CACHE[key]
    shared = dict(w_in=f(w_in), b_in=f(b_in), gla_a_up=f(gla_a_up), gla_a_bias=f(gla_a_bias), gla_norm=f(gla_norm),
                  rwkv_mu=f(rwkv_mu), rwkv_w0=f(rwkv_w0), rwkv_a0=f(rwkv_a0), rwkv_k_k=f(rwkv_k_k), rwkv_k_a=f(rwkv_k_a),
                  rwkv_r_k=f(rwkv_r_k).reshape(DEPTH, 512), rwkv_ln_w=f(rwkv_ln_w), rwkv_ln_b=f(rwkv_ln_b),
                  rwkv_w_up=f(rwkv_w_up), rwkv_a_up=f(rwkv_a_up), w_br=f(w_br), w_out=f(w_out), ln_g=f(ln_g), ln_b=f(ln_b))
    if HAVE_NSA_SAMPLE:
        cache = np.asarray(cache_nsa_kv)
        for ll in range(DEPTH):
            shared['cache%d' % ll] = np.ascontiguousarray(cache[ll], dtype=np.float32).reshape(NPOOL * PAGE, 512)
    ptab = np.ascontiguousarray(np.asarray(page_table), dtype=np.int32)
    in_maps = []
    for c in range(_cores):
        bs = slice(c * NS_B, (c + 1) * NS_B)
        xx = np.concatenate([f(x_prompt[c]), f(x_sample[bs]).reshape(NSAMP, D)], axis=0)
        m = dict(shared)
        m.update({
            'x': xx,
            'win': f(state_nsa_win[:, bs]).reshape(DEPTH, NS_B, WINB, 256),
            'shift': f(state_rwkv_shift[:, bs]),
            'gla_st': f(state_gla[:, bs]),
            'rw_st': f(state_rwkv[:, bs]),
            'ptab': np.ascontiguousarray(ptab[bs]),
        })
        in_maps.append(m)
    res = run_bass_kernel_spmd(nc, in_maps, core_ids=list(range(_cores)))
    R = res.results
    if _debug:
        return R
    st = lambda key: [R[c][key] for c in range(NCORES)]
    y = st('y')
    y_prompt = np.stack([a[:T] for a in y], 0)
    y_sample = np.concatenate([a[T:].reshape(NS_B, NS_T, D) for a in y], 0)
    kv = st('kv')
    kv_p = np.stack([a[:, :T].reshape(DEPTH, T, 4, 2, 64) for a in kv], 1)
    kv_s = np.concatenate([a[:, T:].reshape(DEPTH, NS_B, NS_T, 4, 2, 64) for a in kv], 1)
    win_p = np.stack([a.reshape(DEPTH, WINB, 2, 2, 64) for a in st('winp')], 1)
    win_s = np.concatenate([a.reshape(DEPTH, NS_B, WINB, 2, 2, 64) for a in st('wins')], 1)
    gla_p = np.stack(st('glap'), 1)
    gla_s = np.concatenate(st('glas'), 1)
    rw_p = np.stack(st('rwp'), 1)
    rw_s = np.concatenate(st('rws'), 1)
    sh_p = np.stack(st('shp'), 1)
    sh_s = np.concatenate(st('shs'), 1)
    return (y_prompt, y_sample, kv_p, kv_s, win_p, win_s, gla_p, gla_s, rw_p, rw_s, sh_p, sh_s)
```

```python
import contextlib
import numpy as np
import concourse.bass as bass
import concourse.mybir as mybir
from concourse.bass_utils import run_bass_kernel_spmd

F32 = mybir.dt.float32
BF16 = mybir.dt.bfloat16
I32 = mybir.dt.int32
U32 = mybir.dt.uint32
AF = mybir.ActivationFunctionType
ALU = mybir.AluOpType
AX = mybir.AxisListType

NCORES = 8
D = 1024
DEPTH = 2
T = 2048
NS_B = 4
NS_T = 4
NSAMP = NS_B * NS_T
NTOK = T + NSAMP
INW = 8616
O_GQ, O_GK, O_GV, O_GA, O_GG = 0, 256, 512, 1024, 1040
O_RZ = 1552
RWIN = 2176
O_NQ = 3728
O_NKV = 4240
O_NBG = 5008
O_NG = 5032
O_MG = 5544
WINB = 512
NPAGES = 128
PAGE = 128
NPOOL = 5120


class Sched:
    def __init__(self, nc, n_dma_sems=40, same_eng_wait=True):
        self.nc = nc
        self.e = {'pe': nc.tensor, 'dve': nc.vector, 'act': nc.scalar,
                  'pool': nc.gpsimd, 'sp': nc.sync}
        self.sem = {k: nc.alloc_semaphore(name="s_" + k) for k in self.e}
        self.cnt = {k: 0 for k in self.e}
        self.dsem = [nc.alloc_semaphore(name="d%d" % i) for i in range(n_dma_sems)]
        self.dcnt = [0] * n_dma_sems
        self.dnext = 0
        self.seen = {k: {} for k in self.e}
        self.lastw = {}
        self.readers = {}
        self.same_eng_wait = same_eng_wait
        self.out_events = []
        self.n_inst = 0

    def _wait(self, eng, ev, same_ok=False):
        semkey, semh, val, src = ev
        if src == eng:
            if eng == 'pe' or same_ok or not self.same_eng_wait:
                return
        if self.seen[eng].get(semkey, 0) >= val:
            return
        self.e[eng].wait_ge(semh, val)
        self.seen[eng][semkey] = val
        self.n_inst += 1

    def _deps(self, eng, reads, writes):
        for k in reads:
            ev = self.lastw.get(k)
            if ev is not None:
                self._wait(eng, ev)
        for k in writes:
            ev = self.lastw.get(k)
            if ev is not None:
                self._wait(eng, ev)
            for ev in self.readers.get(k, ()):
                self._wait(eng, ev, same_ok=True)

    def _record(self, ev, reads, writes):
        for k in writes:
            self.lastw[k] = ev
            self.readers[k] = []
        for k in reads:
            if k in writes:
                continue
            lst = self.readers.setdefault(k, [])
            lst.append(ev)
            if len(lst) > 48:
                d = {}
                for e2 in lst:
                    if e2[0] not in d or d[e2[0]][2] < e2[2]:
                        d[e2[0]] = e2
                self.readers[k] = list(d.values())

    def op(self, eng, fn, reads=(), writes=()):
        self._deps(eng, reads, writes)
        inst = fn(self.e[eng])
        self.cnt[eng] += 1
        inst.then_inc(self.sem[eng], 1)
        ev = (eng, self.sem[eng], self.cnt[eng], eng)
        self._record(ev, reads, writes)
        self.n_inst += 1
        return ev

    def dma(self, q, out, in_, reads=(), writes=(), is_output=False, fn=None, **kw):
        self._deps(q, reads, writes)
        i = self.dnext
        self.dnext = (self.dnext + 1) % len(self.dsem)
        if self.dcnt[i] > 0:
            self._wait(q, (('d', i), self.dsem[i], 16 * self.dcnt[i], None))
        if fn is not None:
            inst = fn(self.e[q])
        else:
            inst = self.e[q].dma_start(out=out, in_=in_, **kw)
        inst.then_inc(self.dsem[i], 16)
        self.dcnt[i] += 1
        ev = (('d', i), self.dsem[i], 16 * self.dcnt[i], None)
        self._record(ev, reads, writes)
        if is_output:
            self.out_events.append(ev)
        self.n_inst += 1
        return ev

    def barrier(self):
        evs = [(k, self.sem[k], self.cnt[k], k) for k in self.e if self.cnt[k] > 0]
        evs += [(('d', i), self.dsem[i], 16 * self.dcnt[i], None)
                for i in range(len(self.dsem)) if self.dcnt[i] > 0]
        for eng in self.e:
            for ev in evs:
                if ev[3] == eng:
                    continue
                self._wait(eng, ev)
        self.lastw = {}
        self.readers = {}

    def finish(self):
        for ev in self.out_events:
            self._wait('sp', ev)
        self.barrier()


def token_tiles():
    tl = [(i * 128, 128) for i in range(T // 128)]
    tl.append((T, NSAMP))
    return tl


class Ctx:
    pass


class Phase:
    def __init__(self, C):
        self.C = C
        self.st = contextlib.ExitStack()

    def __enter__(self):
        self.st.__enter__()
        return self

    def __exit__(self, *a):
        if a[0] is None:
            self.C.S.barrier()
        return self.st.__exit__(*a)

    def T(self, name, shape, dt=F32):
        self.C.uid += 1
        return self.st.enter_context(self.C.nc.sbuf_tensor("%s_%d" % (name, self.C.uid), shape, dt))


def MM(S, out, lhsT, rhs, start=True, stop=True, r=(), w=()):
    return S.op('pe', lambda e: e.matmul(out, lhsT=lhsT, rhs=rhs, start=start, stop=stop), r, w)


def TR(S, out, in_, ident, r=(), w=()):
    n = in_.shape[0]
    return S.op('pe', lambda e: e.transpose(out, in_, ident[:n, :n]), list(r) + ['ident'], w)


def TT(S, eng, out, a, b, op, r=(), w=()):
    return S.op(eng, lambda e: e.tensor_tensor(out=out, in0=a, in1=b, op=op), r, w)


def STT(S, out, a, sc, b, op0, op1, r=(), w=()):
    return S.op('dve', lambda e: e.scalar_tensor_tensor(out=out, in0=a, scalar=sc, in1=b, op0=op0, op1=op1), r, w)


def TS(S, eng, out, a, s1, s2, op0, op1, r=(), w=()):
    return S.op(eng, lambda e: e.tensor_scalar(out=out, in0=a, scalar1=s1, scalar2=s2, op0=op0, op1=op1), r, w)


def ACT(S, out, in_, func, r=(), w=(), **kw):
    return S.op('act', lambda e: e.activation(out=out, in_=in_, func=func, **kw), r, w)


def CP(S, eng, out, in_, r=(), w=()):
    if eng == 'act':
        return S.op('act', lambda e: e.copy(out, in_), r, w)
    return S.op(eng, lambda e: e.tensor_copy(out, in_), r, w)


def MEMSET(S, eng, ap, val, w=()):
    return S.op(eng, lambda e: e.memset(ap, val), (), w)


def ASEL(S, ap, pattern, base, cm, w, op=None, fill=0.0):
    op = ALU.is_ge if op is None else op
    return S.op('pool', lambda e: e.affine_select(out=ap, in_=ap, pattern=pattern, compare_op=op, fill=fill,
                                                  base=base, channel_multiplier=cm), w, w)


def RED(S, out, in_, op, r=(), w=()):
    return S.op('dve', lambda e: e.tensor_reduce(out=out, in_=in_, axis=AX.X, op=op), r, w)


def phase_inproj(C, l, xsrc):
    S, nc, I, ps, ident = C.S, C.nc, C.I, C.ps, C.ident
    tiles = token_tiles()
    z = C.z
    with Phase(C) as ph:
        xT = ph.T("xT", [128, 8, NTOK], BF16)
        xin = [ph.T("xin%d" % i, [128, D]) for i in range(2)]
        ones_bf = ph.T("ones_bf", [1, 128], BF16)
        MEMSET(S, 'dve', ones_bf[:], 1.0, ['ones_bf'])
        for ti, (r0, nr) in enumerate(tiles):
            xb = xin[ti % 2]
            kx = 'xin%d' % (ti % 2)
            S.dma('sp', xb[:nr, :], xsrc[r0:r0 + nr, :], reads=['xs'], writes=[kx])
            for half in range(2):
                pi = (ti * 2 + half) % 2
                pb, kp = ps[pi], 'ps%d' % pi
                for c4 in range(4):
                    c = half * 4 + c4
                    TR(S, pb[:, c4 * 128:c4 * 128 + nr], xb[:nr, c * 128:(c + 1) * 128], ident, [kx], [kp])
                src = pb[:].rearrange("p (c t) -> p c t", c=4)[:, :, :nr]
                dst = xT[:, half * 4:half * 4 + 4, r0:r0 + nr]
                CP(S, 'dve' if half == 0 else 'act', dst, src, [kp], [('xT', ti)])
        wbuf = [ph.T("wbuf%d" % i, [128, 8, 512], BF16) for i in range(2)]
        bbuf = [ph.T("bbuf%d" % i, [1, 512], BF16) for i in range(2)]
        ost = [ph.T("ost%d" % i, [128, 512]) for i in range(4)]
        ncol = (INW + 511) // 512
        k_ev = 0
        for j in range(ncol):
            c0 = j * 512
            cw = min(512, INW - c0)
            wb, bb, kw_ = wbuf[j % 2], bbuf[j % 2], 'wbuf%d' % (j % 2)
            S.dma('pool', wb[:, :, :cw], I['w_in'][l, :, c0:c0 + cw].rearrange("(c p) n -> p c n", p=128),
                  writes=[kw_])
            S.dma('pool', bb[:, :cw], I['b_in'][l:l + 1, c0:c0 + cw], writes=[kw_ + 'b'])
            for ti, (r0, nr) in enumerate(tiles):
                pi = 2 + (k_ev % 4)
                pb, kp = ps[pi], 'ps%d' % pi
                for c in range(8):
                    MM(S, pb[:nr, :cw], xT[:, c, r0:r0 + nr], wb[:, c, :cw], c == 0, False, [('xT', ti), kw_], [kp])
                MM(S, pb[:nr, :cw], ones_bf[:, :nr], bb[:, :cw], False, True, ['ones_bf', kw_ + 'b'], [kp])
                ob, ko = ost[k_ev % 4], 'ost%d' % (k_ev % 4)
                CP(S, 'dve' if k_ev % 2 == 0 else 'act', ob[:nr, :cw], pb[:nr, :cw], [kp], [ko])
                S.dma('sp', z[r0:r0 + nr, c0:c0 + cw], ob[:nr, :cw], reads=[ko], writes=['z'])
                k_ev += 1


def direct_outputs(C, l):
    S, I, O, z = C.S, C.I, C.O, C.z
    S.dma('sp', O['kv'][l], z[:, O_NKV:O_NKV + 512], is_output=True)
    S.dma('sp', O['winp'][l], z[T - WINB:T, O_NKV + 512:O_NKV + 768], is_output=True)
    for b in range(NS_B):
        S.dma('sp', O['wins'][l, b, 0:WINB - NS_T, :], I['win'][l, b, NS_T:WINB, :], is_output=True)
        S.dma('sp', O['wins'][l, b, WINB - NS_T:WINB, :],
              z[T + b * NS_T:T + (b + 1) * NS_T, O_NKV + 512:O_NKV + 768], is_output=True)
        S.dma('sp', O['shs'][l, b:b + 1, :], z[T + b * NS_T + NS_T - 1:T + (b + 1) * NS_T, O_RZ:O_RZ + RWIN],
              is_output=True)
    S.dma('sp', O['shp'][l:l + 1, :], z[T - 1:T, O_RZ:O_RZ + RWIN], is_output=True)


def phase_gla(C, l):
    S, nc, I, O, ps, ident, z, br = C.S, C.nc, C.I, C.O, C.ps, C.ident, C.z, C.br
    with Phase(C) as ph:
        tri = ph.T('tri', [64, 64]); slow = ph.T('slow', [64, 64]); cmask = ph.T('cmask', [64, 64])
        MEMSET(S, 'pool', tri[:], -1.0 / 16, ['tri'])
        ASEL(S, tri[:], [[1, 64]], 0, -1, ['tri'])
        MEMSET(S, 'pool', slow[:], -1.0 / 16, ['slow'])
        ASEL(S, slow[:], [[-1, 64]], -1, 1, ['slow'])
        MEMSET(S, 'pool', cmask[:], 1.0, ['cmask'])
        ASEL(S, cmask[:], [[1, 64]], 0, -1, ['cmask'])
        aup = ph.T('aup', [16, 256]); abias = ph.T('abias', [1, 256]); ones1 = ph.T('ones1', [1, 64])
        normg = ph.T('normg', [64, 512])
        S.dma('sp', aup[:], I['gla_a_up'][l], writes=['aup'])
        S.dma('sp', abias[:], I['gla_a_bias'][l:l + 1, :], writes=['abias'])
        MEMSET(S, 'dve', ones1[:], 1.0, ['ones1'])
        S.dma('sp', normg[:], I['gla_norm'][l:l + 1, :].partition_broadcast(64), writes=['normg'])
        Sst = ph.T('Sst', [64, 512])
        zb = [ph.T('gz%d' % i, [64, 1552]) for i in range(2)]
        gaT = ph.T('gaT', [16, 64]); la = ph.T('la', [64, 256])
        ecum = ph.T('ecum', [64, 4, 64]); encum = ph.T('encum', [64, 4, 64]); edl = ph.T('edl', [64, 256])
        qdT = ph.T('qdT', [64, 4, 64]); kdT = ph.T('kdT', [64, 4, 64]); kl = ph.T('kl', [64, 256])
        attT = ph.T('attT', [64, 4, 64])
        ssq = ph.T('ssq', [64, 4]); rstd = ph.T('rstd', [64, 4]); junk = ph.T('junk', [64, 128])
        sg = ph.T('sg', [64, 512]); bro = [ph.T('bro%d' % i, [64, 512]) for i in range(2)]
        seqs = [('p', 0, T, 64, None)] + [('s', T + b * NS_T, NS_T, NS_T, b) for b in range(NS_B)]
        kc = 0
        for (kind, r0, L, n, b) in seqs:
            if kind == 'p':
                MEMSET(S, 'dve', Sst[:], 0.0, ['Sst'])
            else:
                S.dma('sp', Sst[:].rearrange("d (h e) -> d h e", h=4), I['gla_st'][l, b].rearrange("h d e -> d h e"),
                      writes=['Sst'])
            for c0 in range(0, L, n):
                row = r0 + c0
                zt, kz = zb[kc % 2], 'gz%d' % (kc % 2)
                S.dma('sp', zt[:n, :], z[row:row + n, 0:1552], writes=[kz])
                TR(S, ps[0][:16, :n], zt[:n, O_GA:O_GA + 16], ident, [kz], ['ps0'])
                CP(S, 'act', gaT[:, :n], ps[0][:16, :n], ['ps0'], ['gaT'])
                MM(S, ps[1][:n, :256], gaT[:, :n], aup[:], True, False, ['gaT', 'aup'], ['ps1'])
                MM(S, ps[1][:n, :256], ones1[:, :n], abias[:], False, True, ['ones1', 'abias'], ['ps1'])
                ACT(S, la[:n, :], ps[1][:n, :256], AF.Exp, ['ps1'], ['la'], scale=-1.0)
                ACT(S, la[:n, :], la[:n, :], AF.Ln, ['la'], ['la'], bias=1.0)
                for h in range(4):
                    MM(S, ps[2][:64, h * 64:h * 64 + n], la[:n, h * 64:(h + 1) * 64], tri[:n, :n], True, True,
                       ['la', 'tri'], ['ps2'])
                MM(S, ps[3][:n, :256], slow[:n, :n], la[:n, :], True, True, ['la', 'slow'], ['ps3'])
                pc = ps[2][:64, :256].rearrange("p (h t) -> p h t", h=4)[:, :, :n]
                ACT(S, ecum[:, :, :n], pc, AF.Exp, ['ps2'], ['ecum'])
                ACT(S, encum[:, :, :n], pc, AF.Exp, ['ps2'], ['encum'], scale=-1.0)
                ACT(S, edl[:n, :], ps[3][:n, :256], AF.Exp, ['ps3'], ['edl'])
                for hh in range(8):
                    TR(S, ps[4][:64, hh * 64:hh * 64 + n], zt[:n, hh * 64:(hh + 1) * 64], ident, [kz], ['ps4'])
                pq = ps[4][:64, :].rearrange("p (h t) -> p h t", h=8)
                STT(S, qdT[:, :, :n], pq[:, 0:4, :n], 0.125, ecum[:, :, :n], ALU.mult, ALU.mult, ['ps4', 'ecum'], ['qdT'])
                TT(S, 'dve', kdT[:, :, :n], pq[:, 4:8, :n], encum[:, :, :n], ALU.mult, ['ps4', 'encum'], ['kdT'])
                TT(S, 'pool', kl[:n, :], zt[:n, O_GK:O_GK + 256], edl[:n, :], ALU.mult, [kz, 'edl'], ['kl'])
                for h in range(4):
                    MM(S, ps[5][:n, h * 64:h * 64 + n], kdT[:, h, :n], qdT[:, h, :n], True, True, ['kdT', 'qdT'], ['ps5'])
                pe_ = ps[5][:n, :256].rearrange("p (h t) -> p h t", h=4)[:, :, :n]
                TT(S, 'dve', attT[:n, :, :n], pe_, cmask[:n, :n].unsqueeze(1).to_broadcast([n, 4, n]), ALU.mult,
                   ['ps5', 'cmask'], ['attT'])
                for h in range(4):
                    vh = zt[:n, O_GV + h * 128:O_GV + (h + 1) * 128]
                    MM(S, ps[6][:n, h * 128:(h + 1) * 128], attT[:n, h, :n], vh, True, False, ['attT', kz], ['ps6'])
                    MM(S, ps[6][:n, h * 128:(h + 1) * 128], qdT[:, h, :n], Sst[:, h * 128:(h + 1) * 128], False, True,
                       ['qdT', 'Sst'], ['ps6'])
                for h in range(4):
                    vh = zt[:n, O_GV + h * 128:O_GV + (h + 1) * 128]
                    MM(S, ps[7][:64, h * 128:(h + 1) * 128], kl[:n, h * 64:(h + 1) * 64], vh, True, True, ['kl', kz], ['ps7'])
                for h in range(4):
                    STT(S, Sst[:, h * 128:(h + 1) * 128], Sst[:, h * 128:(h + 1) * 128], ecum[:, h, n - 1:n],
                        ps[7][:64, h * 128:(h + 1) * 128], ALU.mult, ALU.add, ['Sst', 'ecum', 'ps7'], ['Sst'])
                for h in range(4):
                    ACT(S, junk[:n, :], ps[6][:n, h * 128:(h + 1) * 128], AF.Square, ['ps6'], ['junk', 'ssq'],
                        accum_out=ssq[:n, h:h + 1])
                ACT(S, rstd[:n, :], ssq[:n, :], AF.Ln, ['ssq'], ['rstd'], scale=1.0 / 128, bias=1e-6)
                ACT(S, rstd[:n, :], rstd[:n, :], AF.Exp, ['rstd'], ['rstd'], scale=-0.5)
                ACT(S, sg[:n, :], zt[:n, O_GG:O_GG + 512], AF.Silu, [kz], ['sg'])
                TT(S, 'pool', sg[:n, :], sg[:n, :], normg[:n, :], ALU.mult, ['sg', 'normg'], ['sg'])
                bo, kb = bro[kc % 2], 'bro%d' % (kc % 2)
                for h in range(4):
                    STT(S, bo[:n, h * 128:(h + 1) * 128], ps[6][:n, h * 128:(h + 1) * 128], rstd[:n, h:h + 1],
                        sg[:n, h * 128:(h + 1) * 128], ALU.mult, ALU.mult, ['ps6', 'rstd', 'sg'], [kb])
                S.dma('sp', br[row:row + n, 0:512], bo[:n, :], reads=[kb], writes=[('br', 0)])
                kc += 1
            dst = O['glap'][l] if kind == 'p' else O['glas'][l, b]
            S.dma('sp', dst.rearrange("h d e -> d h e"), Sst[:].rearrange("d (h e) -> d h e", h=4), reads=['Sst'],
                  is_output=True)
def phase_rwkv_pre(C, l):
    S, nc, I, O, ps, ident, z = C.S, C.nc, C.I, C.O, C.ps, C.ident, C.z
    tiles = token_tiles()
    with Phase(C) as ph:
        def bc(name, src_row, width):
            t = ph.T(name, [128, width])
            S.dma('sp', t[:], src_row.partition_broadcast(128), writes=[name])
            return t
        mu_b = bc('mu_b', I['rwkv_mu'][l:l + 1, :], RWIN)
        kk_b = bc('kk_b', I['rwkv_k_k'][l:l + 1, :], 512)
        ka_b = bc('ka_b', I['rwkv_k_a'][l:l + 1, :], 512)
        rk_b = bc('rk_b', I['rwkv_r_k'][l:l + 1, :], 512)
        wup = ph.T('wup', [64, 512]); aup = ph.T('raup', [64, 512])
        w0 = ph.T('w0', [1, 512]); a0 = ph.T('a0', [1, 512]); ones1 = ph.T('rones', [1, 128])
        S.dma('sp', wup[:], I['rwkv_w_up'][l], writes=['wup'])
        S.dma('sp', aup[:], I['rwkv_a_up'][l], writes=['raup'])
        S.dma('sp', w0[:], I['rwkv_w0'][l:l + 1, :], writes=['w0'])
        S.dma('sp', a0[:], I['rwkv_a0'][l:l + 1, :], writes=['a0'])
        MEMSET(S, 'dve', ones1[:], 1.0, ['rones'])
        zc = ph.T('zc', [128, RWIN]); zp = ph.T('zp', [128, RWIN]); zs = ph.T('zs', [128, RWIN])
        twl = ph.T('twl', [128, 64]); lT = ph.T('lT', [64, 2, 128])
        dec = ph.T('dec', [128, 512]); av = ph.T('av', [128, 512]); kk = ph.T('kk', [128, 512])
        kk2 = ph.T('kk2', [128, 512]); s8 = ph.T('s8', [128, 8]); rn = ph.T('rn', [128, 8])
        nkk = ph.T('nkk', [128, 512]); t1 = ph.T('t1', [128, 512]); kmod = ph.T('kmod', [128, 512])
        rk3 = ph.T('rk3', [128, 3, 512], BF16); rkt = ph.T('rkt', [128, 512]); b8 = ph.T('b8', [128, 8])
        bv = ph.T('bv', [128, 512]); sgt = ph.T('sgt', [128, 512])
        fm = [ph.T('fm%d' % i, [64, 8, 128]) for i in range(3)]
        for ti, (r0, nr) in enumerate(tiles):
            S.dma('sp', zc[:nr, :], z[r0:r0 + nr, O_RZ:O_RZ + RWIN], writes=['zc'])
            if ti == 0:
                MEMSET(S, 'dve', zp[0:1, :], 0.0, ['zp'])
                S.dma('sp', zp[1:nr, :], z[0:nr - 1, O_RZ:O_RZ + RWIN], reads=['zp'], writes=['zp1'])
            elif nr == 128:
                S.dma('sp', zp[:nr, :], z[r0 - 1:r0 + nr - 1, O_RZ:O_RZ + RWIN], writes=['zp', 'zp1'])
            else:
                S.dma('sp', zp[:nr, :], z[r0 - 1:r0 + nr - 1, O_RZ:O_RZ + RWIN], writes=['zp'])
                kws = ['zp']
                for b in range(NS_B):
                    S.dma('sp', zp[b * NS_T:b * NS_T + 1, :], I['shift'][l, b:b + 1, :], reads=kws, writes=['zp1'])
            rd = ['zp', 'zp1', 'zc']
            TT(S, 'dve', zs[:nr, :], zp[:nr, :], zc[:nr, :], ALU.subtract, rd, ['zs'])
            TT(S, 'pool', zs[:nr, :], zs[:nr, :], mu_b[:nr, :], ALU.mult, ['zs', 'mu_b'], ['zs'])
            TT(S, 'dve', zs[:nr, :], zs[:nr, :], zc[:nr, :], ALU.add, ['zs', 'zc'], ['zs'])
            r_, k_, v_ = zs[:nr, 0:512], zs[:nr, 512:1024], zs[:nr, 1024:1536]
            wl_, al_, g_ = zs[:nr, 1536:1600], zs[:nr, 1600:1664], zs[:nr, 1664:2176]
            ACT(S, twl[:nr, :], wl_, AF.Tanh, ['zs'], ['twl'])
            TR(S, ps[0][:64, 0:nr], twl[:nr, :], ident, ['twl'], ['ps0'])
            TR(S, ps[0][:64, 128:128 + nr], al_, ident, ['zs'], ['ps0'])
            CP(S, 'dve', lT[:, :, :nr], ps[0][:64, :256].rearrange("p (a t) -> p a t", a=2)[:, :, :nr], ['ps0'], ['lT'])
            MM(S, ps[1][:nr, :], lT[:, 0, :nr], wup[:], True, False, ['lT', 'wup'], ['ps1'])
            MM(S, ps[1][:nr, :], ones1[:, :nr], w0[:], False, True, ['rones', 'w0'], ['ps1'])
            MM(S, ps[2][:nr, :], lT[:, 1, :nr], aup[:], True, False, ['lT', 'raup'], ['ps2'])
            MM(S, ps[2][:nr, :], ones1[:, :nr], a0[:], False, True, ['rones', 'a0'], ['ps2'])
            ACT(S, dec[:nr, :], ps[1][:nr, :], AF.Sigmoid, ['ps1'], ['dec'])
            ACT(S, av[:nr, :], ps[2][:nr, :], AF.Sigmoid, ['ps2'], ['av'])
            ACT(S, dec[:nr, :], dec[:nr, :], AF.Exp, ['dec'], ['dec'], scale=-float(np.exp(-0.5)))
            ACT(S, sgt[:nr, :], g_, AF.Silu, ['zs'], ['sgt'])
            S.dma('act', C.sg_scr[r0:r0 + nr, :], sgt[:nr, :], reads=['sgt'], writes=[('sgs', ti)])
            TT(S, 'dve', kk[:nr, :], k_, kk_b[:nr, :], ALU.mult, ['zs', 'kk_b'], ['kk'])
            TT(S, 'pool', kk2[:nr, :], kk[:nr, :], kk[:nr, :], ALU.mult, ['kk'], ['kk2'])
            RED(S, s8[:nr, :], kk2[:nr, :].rearrange("p (h j) -> p h j", h=8), ALU.add, ['kk2'], ['s8'])
            ACT(S, rn[:nr, :], s8[:nr, :], AF.Ln, ['s8'], ['rn'], bias=1e-6)
            ACT(S, rn[:nr, :], rn[:nr, :], AF.Exp, ['rn'], ['rn'], scale=-0.5)
            v3 = lambda t: t.rearrange("p (h j) -> p h j", h=8)
            STT(S, v3(nkk[:nr, :]), v3(kk[:nr, :]), -1.0, rn[:nr, :].unsqueeze(2).to_broadcast([nr, 8, 64]),
                ALU.mult, ALU.mult, ['kk', 'rn'], ['nkk'])
            STT(S, t1[:nr, :], av[:nr, :], -1.0, ka_b[:nr, :], ALU.add, ALU.mult, ['av', 'ka_b'], ['t1'])
            STT(S, kmod[:nr, :], t1[:nr, :], 1.0, k_, ALU.add, ALU.mult, ['t1', 'zs'], ['kmod'])
            STT(S, rk3[:nr, 1, :], nkk[:nr, :], -1.0, av[:nr, :], ALU.mult, ALU.mult, ['nkk', 'av'], [('rk3', 1)])
            CP(S, 'act', rk3[:nr, 0, :], kmod[:nr, :], ['kmod'], [('rk3', 0)])
            CP(S, 'act', rk3[:nr, 2, :], v_, ['zs'], [('rk3', 2)])
            S.dma('act', C.rk_scr[r0:r0 + nr, :, :], rk3[:nr, :, :], reads=[('rk3', 0), ('rk3', 1), ('rk3', 2)],
                  writes=[('rks', ti)])
            TT(S, 'pool', rkt[:nr, :], r_, kmod[:nr, :], ALU.mult, ['zs', 'kmod'], ['rkt'])
            TT(S, 'dve', rkt[:nr, :], rkt[:nr, :], rk_b[:nr, :], ALU.mult, ['rkt', 'rk_b'], ['rkt'])
            RED(S, b8[:nr, :], v3(rkt[:nr, :]), ALU.add, ['rkt'], ['b8'])
            TT(S, 'dve', v3(bv[:nr, :]), v3(v_), b8[:nr, :].unsqueeze(2).to_broadcast([nr, 8, 64]), ALU.mult,
               ['zs', 'b8'], ['bv'])
            S.dma('act', C.bv_scr[r0:r0 + nr, :], bv[:nr, :], reads=['bv'], writes=[('bvs', ti)])
            for qi, src in enumerate((nkk[:nr, :], r_, dec[:nr, :])):
                rkey = ['nkk', 'zs', 'dec'][qi]
                for hb in range(2):
                    pb, kp = ps[3 + hb], 'ps%d' % (3 + hb)
                    for h4 in range(4):
                        h = hb * 4 + h4
                        TR(S, pb[:64, h4 * 128:h4 * 128 + nr], src[:, h * 64:(h + 1) * 64], ident, [rkey], [kp])
                    CP(S, 'dve' if hb == 0 else 'act', fm[qi][:, hb * 4:hb * 4 + 4, :nr],
                       pb[:64, :].rearrange("p (h t) -> p h t", h=4)[:, :, :nr], [kp], [('fm', qi)])
                S.dma('sp', C.fm_scr[qi, :, :, r0:r0 + nr], fm[qi][:, :, :nr], reads=[('fm', qi)], writes=[('fms', ti)])


def phase_rwkv_scan(C, l):
    S, nc, I, O, ps, ident = C.S, C.nc, C.I, C.O, C.ps, C.ident
    SUB = 8
    with Phase(C) as ph:
        maskbd = ph.T('maskbd', [8, 512]); maskbf = ph.T('maskbf', [8, 512], BF16)
        MEMSET(S, 'pool', maskbd[:], 1.0, ['maskbd'])
        ASEL(S, maskbd[:], [[1, 512]], 0, -64, ['maskbd'])
        ASEL(S, maskbd[:], [[-1, 512]], 63, 64, ['maskbd'])
        CP(S, 'dve', maskbf[:], maskbd[:], ['maskbd'], ['maskbf'])
        ST = [ph.T('ST%d' % i, [64, 512]) for i in range(2)]
        tmp = ph.T('sttmp', [64, 512])
        STb = [ph.T('STb%d' % i, [64, 512], BF16) for i in range(2)]
        fmb0 = [ph.T('fmb0_%d' % i, [64, 8, 128], BF16) for i in range(2)]
        fmb1 = [ph.T('fmb1_%d' % i, [64, 8, 128], BF16) for i in range(2)]
        fmt = [[ph.T('fmt%d_%d' % (q, i), [64, 8, 128]) for i in range(2)] for q in range(3)]
        kmr = [ph.T('kmr%d' % i, [8, SUB, 64], BF16) for i in range(2)]
        kar = [ph.T('kar%d' % i, [8, SUB, 64], BF16) for i in range(2)]
        vb = [ph.T('vb%d' % i, [8, SUB, 512], BF16) for i in range(2)]
        sab = [ph.T('sab%d' % i, [8, 512], BF16) for i in range(2)]
        yT = ph.T('yT', [128, 4, 128]); ytm = ph.T('ytm', [128, 512])
        sin = ph.T('sin', [64, 8, 64]); sout = ph.T('sout', [64, 8, 64])
        seqs = [('p', 0, T, None)] + [('s', T + b * NS_T, NS_T, b) for b in range(NS_B)]
        cur = 0
        gt = 0
        gs = 0
        for (kind, r0, L, b) in seqs:
            if kind == 'p':
                MEMSET(S, 'dve', ST[cur][:], 0.0, ['ST%d' % cur])
                MEMSET(S, 'dve', STb[cur][:], 0.0, ['STb%d' % cur])
            else:
                S.dma('sp', sin[:], I['rw_st'][l, b].rearrange("h i j -> i h j"), writes=['sin'])
                for h in range(8):
                    TR(S, ps[0][:64, h * 64:(h + 1) * 64], sin[:, h, :], ident, ['sin'], ['ps0'])
                CP(S, 'dve', ST[cur][:], ps[0][:64, :], ['ps0'], ['ST%d' % cur])
                CP(S, 'dve', STb[cur][:], ps[0][:64, :], ['ps0'], ['STb%d' % cur])
            tiles_ = [(t0, min(128, L - t0)) for t0 in range(0, L, 128)]
            subs = []
            for k, (t0, nt) in enumerate(tiles_):
                for s0 in range(0, nt, SUB):
                    subs.append((k, t0, s0, min(SUB, nt - s0)))
            steps = []
            for si, (k, t0, s0, ns) in enumerate(subs):
                for tt_ in range(ns):
                    steps.append((si, k, t0, s0, ns, tt_))

            def load_tile(k):
                t0, nt = tiles_[k]
                fb = (gt + k) % 2
                for q in range(3):
                    S.dma('sp' if q != 1 else 'act', fmt[q][fb][:, :, :nt], C.fm_scr[q, :, :, r0 + t0:r0 + t0 + nt],
                          writes=[('fmt', q, fb)])
                CP(S, 'act', fmb0[fb][:, :, :nt], fmt[0][fb][:, :, :nt], [('fmt', 0, fb)], [('fmb0', fb)])
                CP(S, 'act', fmb1[fb][:, :, :nt], fmt[1][fb][:, :, :nt], [('fmt', 1, fb)], [('fmb1', fb)])

            def load_sub(si):
                k, t0, s0, ns = subs[si]
                sb = (gs + si) % 2
                rows = slice(r0 + t0 + s0, r0 + t0 + s0 + ns)
                S.dma('sp', kmr[sb][:, :ns, :], C.rk_scr[rows, 0, :].rearrange("t (h j) -> h t j", h=8),
                      writes=[('kmr', sb)])
                S.dma('act', kar[sb][:, :ns, :], C.rk_scr[rows, 1, :].rearrange("t (h j) -> h t j", h=8),
                      writes=[('kar', sb)])
                S.dma('sp', vb[sb][:, :ns, :], C.rk_scr[rows, 2, :].partition_broadcast(8), writes=[('vb', sb)])
                TT(S, 'pool', vb[sb][:, :ns, :], vb[sb][:, :ns, :],
                   maskbf[:].unsqueeze(1).to_broadcast([8, ns, 512]), ALU.mult, [('vb', sb), 'maskbf'], [('vb', sb)])

            def frontB(i):
                si, k, t0, s0, ns, tt_ = steps[i]
                sb = (gs + si) % 2
                MM(S, ps[2][:64, :], kmr[sb][:, tt_, :], vb[sb][:, tt_, :], True, False, [('kmr', sb), ('vb', sb)], ['ps2'])

            def frontA(i, cur_):
                si, k, t0, s0, ns, tt_ = steps[i]
                fb = (gt + k) % 2
                t = s0 + tt_
                MM(S, ps[1][:8, :], fmb0[fb][:, :, t], STb[cur_][:], True, True, [('fmb0', fb), 'STb%d' % cur_], ['ps1'])

            load_tile(0)
            load_sub(0)
            frontB(0)
            frontA(0, cur)
            slot = 0
            slot_t0 = 0
            for i, (si, k, t0, s0, ns, tt_) in enumerate(steps):
                sb = (gs + si) % 2
                fb = (gt + k) % 2
                t = s0 + tt_
                nt = tiles_[k][1]
                if tt_ == 0 and si + 1 < len(subs):
                    if subs[si + 1][0] != k:
                        load_tile(subs[si + 1][0])
                    load_sub(si + 1)
                nxt = 1 - cur
                kc_, kn_ = 'ST%d' % cur, 'ST%d' % nxt
                sa, ksa = sab[i % 2], 'sab%d' % (i % 2)
                TT(S, 'dve', sa[:], ps[1][:8, :], maskbd[:], ALU.mult, ['ps1', 'maskbd'], [ksa])
                MM(S, ps[2][:64, :], kar[sb][:, tt_, :], sa[:], False, True, [('kar', sb), ksa], ['ps2'])
                TT(S, 'pool', tmp[:].rearrange("p (h i) -> p h i", h=8), ST[cur][:].rearrange("p (h i) -> p h i", h=8),
                   fmt[2][fb][:, :, t].unsqueeze(2).to_broadcast([64, 8, 64]), ALU.mult, [kc_, ('fmt', 2, fb)], ['sttmp'])
                TT(S, 'dve', STb[nxt][:], tmp[:], ps[2][:64, :], ALU.add, ['sttmp', 'ps2'], ['STb%d' % nxt])
                if i + 1 < len(steps):
                    frontA(i + 1, nxt)
                TT(S, 'dve', ST[nxt][:], tmp[:], ps[2][:64, :], ALU.add, ['sttmp', 'ps2'], [kn_])
                if i + 1 < len(steps):
                    frontB(i + 1)
                for c in range(4):
                    MM(S, ps[3][:, slot * 32 + c * 8:slot * 32 + c * 8 + 8], STb[nxt][:, c * 128:(c + 1) * 128],
                       fmb1[fb][:, :, t], True, True, ['STb%d' % nxt, ('fmb1', fb)], ['ps3'])
                cur = nxt
                slot += 1
                if slot == 16 or t == nt - 1:
                    for hf in range(2):
                        src = ps[3][hf * 64:(hf + 1) * 64, :].rearrange("p (s x) -> p s x", x=32)[:, :slot, hf:hf + 31:10]
                        dst = yT[hf * 64:(hf + 1) * 64, :, slot_t0:slot_t0 + slot].rearrange("p c t -> p t c")
                        CP(S, 'act', dst, src, ['ps3'], [('yT', hf)])
                    slot_t0 += slot
                    slot = 0
                if t == nt - 1:
                    for c in range(4):
                        TR(S, ps[4][:nt, c * 128:(c + 1) * 128], yT[:, c, :nt], ident, [('yT', 0), ('yT', 1)], ['ps4'])
                    CP(S, 'act', ytm[:nt, :], ps[4][:nt, :], ['ps4'], ['ytm'])
                    S.dma('sp', C.y_scr[r0 + t0:r0 + t0 + nt, :], ytm[:nt, :], reads=['ytm'], writes=[('ys', gt + k)])
                    slot_t0 = 0
            gt += len(tiles_)
            gs += len(subs)
            for h in range(8):
                TR(S, ps[0][:64, h * 64:(h + 1) * 64], ST[cur][:, h * 64:(h + 1) * 64], ident, ['ST%d' % cur], ['ps0'])
            CP(S, 'dve', sout[:].rearrange("p h j -> p (h j)"), ps[0][:64, :], ['ps0'], ['sout'])
            dst = O['rwp'][l] if kind == 'p' else O['rws'][l, b]
            S.dma('sp', dst.rearrange("h i j -> i h j"), sout[:], reads=['sout'], is_output=True)


def phase_rwkv_post(C, l):
    S, nc, I, O, ps, ident, br = C.S, C.nc, C.I, C.O, C.ps, C.ident, C.br
    tiles = token_tiles()
    with Phase(C) as ph:
        lnw = ph.T('lnw', [128, 512]); lnb = ph.T('lnb', [128, 512])
        S.dma('sp', lnw[:], I['rwkv_ln_w'][l:l + 1, :].partition_broadcast(128), writes=['lnw'])
        S.dma('sp', lnb[:], I['rwkv_ln_b'][l:l + 1, :].partition_broadcast(128), writes=['lnb'])
        v3 = lambda t: t.rearrange("p (h j) -> p h j", h=8)
        for ti, (r0, nr) in enumerate(tiles):
            i2 = ti % 2
            y = ph.T('py', [128, 512]) if ti < 2 else None
            if ti < 2:
                C._rwp = getattr(C, '_rwp', {})
                C._rwp[i2] = dict(y=y, bvt=ph.T('pbv', [128, 512]), sgt=ph.T('psg', [128, 512]),
                                  yc=ph.T('pyc', [128, 512]), sq=ph.T('psq', [128, 512]),
                                  m8=ph.T('pm8', [128, 8]), v8=ph.T('pv8', [128, 8]), o=ph.T('po', [128, 512]))
            d = C._rwp[i2]
            k = lambda s: '%s%d' % (s, i2)
            S.dma('sp', d['y'][:nr, :], C.y_scr[r0:r0 + nr, :], writes=[k('y')])
            S.dma('act', d['bvt'][:nr, :], C.bv_scr[r0:r0 + nr, :], writes=[k('bv')])
            S.dma('act', d['sgt'][:nr, :], C.sg_scr[r0:r0 + nr, :], writes=[k('sg')])
            RED(S, d['m8'][:nr, :], v3(d['y'][:nr, :]), ALU.add, [k('y')], [k('m8')])
            STT(S, v3(d['yc'][:nr, :]), d['m8'][:nr, :].unsqueeze(2).to_broadcast([nr, 8, 64]), -1.0 / 64,
                v3(d['y'][:nr, :]), ALU.mult, ALU.add, [k('m8'), k('y')], [k('yc')])
            TT(S, 'pool', d['sq'][:nr, :], d['yc'][:nr, :], d['yc'][:nr, :], ALU.mult, [k('yc')], [k('sq')])
            RED(S, d['v8'][:nr, :], v3(d['sq'][:nr, :]), ALU.add, [k('sq')], [k('v8')])
            ACT(S, d['v8'][:nr, :], d['v8'][:nr, :], AF.Ln, [k('v8')], [k('v8')], scale=1.0 / 64, bias=64e-5)
            ACT(S, d['v8'][:nr, :], d['v8'][:nr, :], AF.Exp, [k('v8')], [k('v8')], scale=-0.5)
            TT(S, 'dve', v3(d['yc'][:nr, :]), v3(d['yc'][:nr, :]), d['v8'][:nr, :].unsqueeze(2).to_broadcast([nr, 8, 64]),
               ALU.mult, [k('yc'), k('v8')], [k('yc')])
            TT(S, 'pool', d['yc'][:nr, :], d['yc'][:nr, :], lnw[:nr, :], ALU.mult, [k('yc'), 'lnw'], [k('yc')])
            TT(S, 'dve', d['yc'][:nr, :], d['yc'][:nr, :], lnb[:nr, :], ALU.add, [k('yc'), 'lnb'], [k('yc')])
            TT(S, 'pool', d['yc'][:nr, :], d['yc'][:nr, :], d['bvt'][:nr, :], ALU.add, [k('yc'), k('bv')], [k('yc')])
            TT(S, 'dve', d['o'][:nr, :], d['yc'][:nr, :], d['sgt'][:nr, :], ALU.mult, [k('yc'), k('sg')], [k('o')])
            S.dma('sp', br[r0:r0 + nr, 512:1024], d['o'][:nr, :], reads=[k('o')], writes=[('br', 1, ti)])


def phase_merge(C, l, xsrc, xdst):
    S, nc, I, O, ps, ident, br, z = C.S, C.nc, C.I, C.O, C.ps, C.ident, C.br, C.z
    tiles = token_tiles()
    alpha = float((2 * DEPTH) ** 0.25)
    with Phase(C) as ph:
        wbr = ph.T('wbr', [128, 12, 1024], BF16); wout = ph.T('wout', [128, 8, 1024], BF16)
        S.dma('pool', wbr[:], I['w_br'][l].rearrange("m (c p) n -> p (m c) n", p=128), writes=['wbr'])
        S.dma('pool', wout[:], I['w_out'][l].rearrange("(c p) n -> p c n", p=128), writes=['wout'])
        lng = ph.T('lng', [128, D]); lnb = ph.T('lnbb', [128, D])
        S.dma('sp', lng[:], I['ln_g'][l:l + 1, :].partition_broadcast(128), writes=['lng'])
        S.dma('sp', lnb[:], I['ln_b'][l:l + 1, :].partition_broadcast(128), writes=['lnbb'])
        bt = ph.T('mbt', [128, 1536]); gt = ph.T('mgt', [128, 3072]); xt = ph.T('mxt', [128, D])
        brT = ph.T('brT', [128, 12, 128], BF16); mg = ph.T('mmg', [128, D]); tmp = ph.T('mtmp', [128, 512])
        mT = ph.T('mT', [128, 8, 128], BF16); res = ph.T('mres', [128, D]); st6 = ph.T('mst6', [128, 2, 6])
        mv = ph.T('mmv', [128, 2]); rs = ph.T('mrs', [128, 1]); xo = ph.T('mxo', [128, D])
        for ti, (r0, nr) in enumerate(tiles):
            S.dma('sp', bt[:nr, :], br[r0:r0 + nr, :], writes=['mbt'])
            S.dma('act', gt[:nr, :], z[r0:r0 + nr, O_MG:O_MG + 3072], writes=['mgt'])
            S.dma('sp', xt[:nr, :], xsrc[r0:r0 + nr, :], writes=['mxt'])
            ACT(S, gt[:nr, :], gt[:nr, :], AF.Sigmoid, ['mgt'], ['mgt'])
            for q in range(3):
                pb, kp = ps[q % 2], 'ps%d' % (q % 2)
                for c4 in range(4):
                    TR(S, pb[:, c4 * 128:c4 * 128 + nr], bt[:nr, (q * 4 + c4) * 128:(q * 4 + c4 + 1) * 128], ident,
                       ['mbt'], [kp])
                CP(S, 'dve' if q % 2 == 0 else 'act', brT[:, q * 4:q * 4 + 4, :nr],
                   pb[:].rearrange("p (c t) -> p c t", c=4)[:, :, :nr], [kp], [('brT', q)])
            for hf in range(2):
                for m in range(3):
                    pi = 2 + (hf * 3 + m) % 3
                    pb, kp = ps[pi], 'ps%d' % pi
                    for c in range(4):
                        MM(S, pb[:nr, :], brT[:, m * 4 + c, :nr], wbr[:, m * 4 + c, hf * 512:(hf + 1) * 512], c == 0, c == 3,
                           [('brT', m), 'wbr'], [kp])
                    gsl = gt[:nr, m * 1024 + hf * 512:m * 1024 + (hf + 1) * 512]
                    if m == 0:
                        TT(S, 'dve', mg[:nr, hf * 512:(hf + 1) * 512], pb[:nr, :], gsl, ALU.mult, [kp, 'mgt'], [('mmg', hf)])
                    else:
                        TT(S, 'dve', tmp[:nr, :], pb[:nr, :], gsl, ALU.mult, [kp, 'mgt'], ['mtmp'])
                        TT(S, 'pool', mg[:nr, hf * 512:(hf + 1) * 512], mg[:nr, hf * 512:(hf + 1) * 512], tmp[:nr, :], ALU.add,
                           [('mmg', hf), 'mtmp'], [('mmg', hf)])
            for hf in range(2):
                pb, kp = ps[5 + hf], 'ps%d' % (5 + hf)
                for c4 in range(4):
                    c = hf * 4 + c4
                    TR(S, pb[:, c4 * 128:c4 * 128 + nr], mg[:nr, c * 128:(c + 1) * 128], ident, [('mmg', hf)], [kp])
                CP(S, 'dve' if hf == 0 else 'act', mT[:, hf * 4:hf * 4 + 4, :nr],
                   pb[:].rearrange("p (c t) -> p c t", c=4)[:, :, :nr], [kp], [('mT', hf)])
            for hf in range(2):
                pb, kp = ps[2 + hf], 'ps%d' % (2 + hf)
                for c in range(8):
                    MM(S, pb[:nr, :], mT[:, c, :nr], wout[:, c, hf * 512:(hf + 1) * 512], c == 0, c == 7,
                       [('mT', 0), ('mT', 1), 'wout'], [kp])
                STT(S, res[:nr, hf * 512:(hf + 1) * 512], xt[:nr, hf * 512:(hf + 1) * 512], alpha, pb[:nr, :], ALU.mult, ALU.add,
                    ['mxt', kp], [('mres', hf)])
                S.op('dve', lambda e: e.bn_stats(out=st6[:nr, hf, :], in_=res[:nr, hf * 512:(hf + 1) * 512]),
                     [('mres', hf)], [('mst6', hf)])
            S.op('dve', lambda e: e.bn_aggr(out=mv[:nr, :], in_=st6[:nr, :, :].rearrange("p a s -> p (a s)")),
                 [('mst6', 0), ('mst6', 1)], ['mmv'])
            ACT(S, rs[:nr, :], mv[:nr, 1:2], AF.Ln, ['mmv'], ['mrs'], bias=1e-5)
            ACT(S, rs[:nr, :], rs[:nr, :], AF.Exp, ['mrs'], ['mrs'], scale=-0.5)
            TS(S, 'dve', xo[:nr, :], res[:nr, :], mv[:nr, 0:1], rs[:nr, 0:1], ALU.subtract, ALU.mult,
               [('mres', 0), ('mres', 1), 'mmv', 'mrs'], ['mxo'])
            TT(S, 'pool', xo[:nr, :], xo[:nr, :], lng[:nr, :], ALU.mult, ['mxo', 'lng'], ['mxo'])
            TT(S, 'dve', xo[:nr, :], xo[:nr, :], lnb[:nr, :], ALU.add, ['mxo', 'lnbb'], ['mxo'])
            S.dma('sp', xdst[r0:r0 + nr, :], xo[:nr, :], reads=['mxo'], writes=[('xd', ti)],
                  is_output=(l == DEPTH - 1))
SLOPES = [2.0 ** (-(h + 1)) for h in range(8)]
NEG = -1.0e30


def IOTA(S, ap, pattern, base, cm, w):
    return S.op('pool', lambda e: e.iota(ap, pattern=pattern, base=base, channel_multiplier=cm,
                                        allow_small_or_imprecise_dtypes=True), (), w)


def phase_nsa_prompt(C, l):
    S, nc, I, O, ps, ident, z, br = C.S, C.nc, C.I, C.O, C.ps, C.ident, C.z, C.br
    NTI = T // 128
    with Phase(C) as ph:
        pool4 = ph.T('pool4', [128, 4]); pband = ph.T('pband', [128, 124])
        MEMSET(S, 'pool', pool4[:], 1.0 / 32, ['pool4'])
        ASEL(S, pool4[:], [[-32, 4]], 0, 1, ['pool4'])
        ASEL(S, pool4[:], [[32, 4]], 31, -1, ['pool4'])
        MEMSET(S, 'pool', pband[:], 1.0 / 32, ['pband'])
        ASEL(S, pband[:], [[-32, 124]], 1920, 1, ['pband'])
        ASEL(S, pband[:], [[32, 124]], -1889, -1, ['pband'])
        d0 = ph.T('d0', [128, 64]); B0 = ph.T('B0', [128, 8, 64])
        IOTA(S, d0[:], [[-32, 64]], -31, 1, ['d0'])
        for hd in range(8):
            TS(S, 'dve', B0[:, hd, :], d0[:], -SLOPES[hd], None, ALU.mult, ALU.bypass, ['d0'], ['B0'])
        penband = ph.T('penband', [128, 124]); m01band = ph.T('m01band', [128, 124])
        MEMSET(S, 'pool', penband[:], 0.0, ['penband'])
        ASEL(S, penband[:], [[-32, 124]], 1889, 1, ['penband'], fill=NEG)
        MEMSET(S, 'pool', m01band[:], 1.0, ['m01band'])
        ASEL(S, m01band[:], [[-32, 124]], 1889, 1, ['m01band'], fill=0.0)
        posb = ph.T('posb', [128, T])
        IOTA(S, posb[:], [[1, T]], 0, 0, ['posb'])
        causalpen = ph.T('causalpen', [128, 128])
        MEMSET(S, 'pool', causalpen[:], 0.0, ['causalpen'])
        ASEL(S, causalpen[:], [[-1, 128]], 0, 1, ['causalpen'], fill=NEG)
        penW = ph.T('penW', [128, 640])
        MEMSET(S, 'pool', penW[:], 0.0, ['penW'])
        ASEL(S, penW[:], [[1, 640]], 0, -1, ['penW'], fill=NEG)
        ASEL(S, penW[:], [[-1, 640]], 512, 1, ['penW'], fill=NEG)
        adj = ph.T('adj', [128, 62])
        MEMSET(S, 'pool', adj[:], 0.0, ['adj'])
        for hf in range(2):
            sl = adj[hf * 64:(hf + 1) * 64, :]
            ASEL(S, sl, [[-1, 62]], 30 + hf, 0, ['adj'], fill=-1.0e9)
            ASEL(S, sl, [[1, 62]], -(30 + hf), 0, ['adj'], op=ALU.not_equal, fill=1.0e9)
        KT = ph.T('KT', [64, 4, T], BF16)
        V = ph.T('Vsw', [128, NTI, 2, 128], BF16)
        kcT = ph.T('kcT', [64, 2, 64], BF16); vc = ph.T('vc', [64, 128], BF16)
        kvt = [ph.T('kvt%d' % i, [128, 768]) for i in range(2)]
        for ti in range(NTI):
            kt_, kk_ = kvt[ti % 2], 'kvt%d' % (ti % 2)
            S.dma('sp', kt_[:], z[ti * 128:(ti + 1) * 128, O_NKV:O_NKV + 768], writes=[kk_])
            for kv in range(2):
                MM(S, ps[6][:64, kv * 64 + 4 * ti:kv * 64 + 4 * ti + 4], kt_[:, kv * 64:(kv + 1) * 64], pool4[:], True, True,
                   [kk_, 'pool4'], ['ps6'])
            MM(S, ps[7][:64, :128], pband[:, 60 - 4 * ti:124 - 4 * ti], kt_[:, 128:256], ti == 0, ti == NTI - 1,
               [kk_, 'pband'], ['ps7'])
            pb, kp = ps[ti % 2], 'ps%d' % (ti % 2)
            for a in range(4):
                col = (256 if a < 2 else 512) + (a % 2) * 64
                TR(S, pb[:64, a * 128:(a + 1) * 128], kt_[:, col:col + 64], ident, [kk_], [kp])
            CP(S, 'dve', KT[:, :, ti * 128:(ti + 1) * 128], pb[:64, :].rearrange("p (a t) -> p a t", a=4), [kp], ['KT'])
            CP(S, 'act', V[:, ti, 0, :], kt_[:, 384:512], [kk_], ['Vsw'])
            CP(S, 'pool', V[:, ti, 1, :], kt_[:, 640:768], [kk_], ['Vsw'])
        CP(S, 'dve', kcT[:].rearrange("p k b -> p (k b)"), ps[6][:64, :128], ['ps6'], ['kcT'])
        CP(S, 'act', vc[:], ps[7][:64, :128], ['ps7'], ['vc'])
        qin = [ph.T('qin%d' % i, [128, 512]) for i in range(2)]
        gin = [ph.T('gin%d' % i, [128, 536]) for i in range(2)]
        qT = ph.T('qT', [64, 8, 128], BF16)
        gates = ph.T('gates', [128, 24]); sgate = ph.T('sgate', [128, 512])
        s1 = ph.T('s1', [128, 8, 64]); mx8 = ph.T('mx8', [128, 8]); den8 = ph.T('den8', [128, 8])
        t8 = ph.T('t8', [128, 8, 32]); imp = ph.T('imp', [128, 2, 32]); sc2 = ph.T('sc2', [128, 2, 32])
        m8a = ph.T('m8a', [128, 8]); m8b = ph.T('m8b', [128, 8]); sel01 = ph.T('sel01', [128, 2, 32])
        pcT = ph.T('pcT', [64, 8, 128], BF16)
        pens = [ph.T('pens%d' % i, [128, T]) for i in range(2)]
        ssb = ph.T('ssb', [128, T]); pex = ph.T('pex', [128, T]); pT = ph.T('pT', [128, 16, 128], BF16)
        mx1 = ph.T('mx1', [128, 1]); dens = ph.T('dens', [128, 8]); denw = ph.T('denw', [128, 8])
        g2 = ph.T('g2', [128, 8]); acc = ph.T('nacc', [128, 512]); tmpo = ph.T('ntmpo', [128, 512])
        v8 = lambda t: t.rearrange("p (h d) -> p h d", h=8)

        def attend(hd, kt0, kt1, aidx, vidx, pen_ap, pen_keys, obank, okey, dent):
            kv = hd // 4
            nkt = kt1 - kt0 + 1
            nk = nkt * 128
            k0 = kt0 * 128
            for ci, c in enumerate(range(0, nk, 512)):
                cw = min(512, nk - c)
                pb, kp = ps[ci % 2], 'ps%d' % (ci % 2)
                MM(S, pb[:, :cw], qT[:, hd, :], KT[:, aidx * 2 + kv, k0 + c:k0 + c + cw], True, True, ['qT', 'KT'], [kp])
                STT(S, ssb[:, c:c + cw], posb[:, k0 + c:k0 + c + cw], SLOPES[hd], pb[:, :cw], ALU.mult, ALU.add,
                    ['posb', kp], ['ssb'])
            TT(S, 'pool', ssb[:, :nk], ssb[:, :nk], pen_ap, ALU.add, ['ssb'] + pen_keys, ['ssb'])
            RED(S, mx1[:], ssb[:, :nk], ALU.max, ['ssb'], ['mx1'])
            TS(S, 'dve', mx1[:], mx1[:], -1.0, None, ALU.mult, ALU.bypass, ['mx1'], ['mx1'])
            ACT(S, pex[:, :nk], ssb[:, :nk], AF.Exp, ['ssb', 'mx1'], ['pex', dent[1]], bias=mx1[:, 0:1],
                accum_out=dent[0][:, hd:hd + 1])
            for g0 in range(0, nkt, 4):
                gn = min(4, nkt - g0)
                pi = 2 + (g0 // 4) % 2
                pb, kp = ps[pi], 'ps%d' % pi
                for j in range(gn):
                    TR(S, pb[:, j * 128:(j + 1) * 128], pex[:, (g0 + j) * 128:(g0 + j + 1) * 128], ident, ['pex'], [kp])
                CP(S, 'act' if (g0 // 4) % 2 == 0 else 'dve', pT[:, g0:g0 + gn, :],
                   pb[:, :gn * 128].rearrange("p (a t) -> p a t", a=gn), [kp], [('pT', g0)])
            for j in range(nkt):
                MM(S, obank[:, hd * 64:(hd + 1) * 64], pT[:, j, :], V[:, kt0 + j, vidx, kv * 64:(kv + 1) * 64],
                   j == 0, j == nkt - 1, [('pT', (j // 4) * 4), 'Vsw'], [okey])

        for ti in range(NTI):
            qb, kq = qin[ti % 2], 'qin%d' % (ti % 2)
            gb, kg = gin[ti % 2], 'gin%d' % (ti % 2)
            r0 = ti * 128
            S.dma('sp', qb[:], z[r0:r0 + 128, O_NQ:O_NQ + 512], writes=[kq])
            S.dma('act', gb[:], z[r0:r0 + 128, O_NBG:O_NBG + 536], writes=[kg])
            for hb in range(2):
                pb, kp = ps[2 + hb], 'ps%d' % (2 + hb)
                for h4 in range(4):
                    hd = hb * 4 + h4
                    TR(S, pb[:64, h4 * 128:(h4 + 1) * 128], qb[:, hd * 64:(hd + 1) * 64], ident, [kq], [kp])
                TS(S, 'dve', qT[:, hb * 4:hb * 4 + 4, :], pb[:64, :].rearrange("p (h t) -> p h t", h=4), 0.125, None,
                   ALU.mult, ALU.bypass, [kp], ['qT'])
            ACT(S, gates[:], gb[:, 0:24], AF.Sigmoid, [kg], ['gates'])
            ACT(S, sgate[:], gb[:, 24:536], AF.Silu, [kg], ['sgate'])
            g3 = gates[:].rearrange("p (h j) -> p h j", j=3)
            for hd in range(8):
                MM(S, ps[6][:, hd * 64:(hd + 1) * 64], qT[:, hd, :], kcT[:, hd // 4, :], True, True, ['qT', 'kcT'], ['ps6'])
            TT(S, 'dve', s1[:], ps[6][:].rearrange("p (h b) -> p h b", h=8), B0[:], ALU.add, ['ps6', 'B0'], ['s1'])
            TT(S, 'pool', s1[:], s1[:], penband[:, 60 - 4 * ti:124 - 4 * ti].unsqueeze(1).to_broadcast([128, 8, 64]), ALU.add,
               ['s1', 'penband'], ['s1'])
            RED(S, mx8[:], s1[:], ALU.max, ['s1'], ['mx8'])
            TT(S, 'dve', s1[:], s1[:], mx8[:].unsqueeze(2).to_broadcast([128, 8, 64]), ALU.subtract, ['s1', 'mx8'], ['s1'])
            ACT(S, s1[:], s1[:], AF.Exp, ['s1'], ['s1'])
            TT(S, 'pool', s1[:], s1[:], m01band[:, 60 - 4 * ti:124 - 4 * ti].unsqueeze(1).to_broadcast([128, 8, 64]), ALU.mult,
               ['s1', 'm01band'], ['s1'])
            RED(S, den8[:], s1[:], ALU.add, ['s1'], ['den8'])
            TS(S, 'dve', den8[:], den8[:], 1e-30, None, ALU.max, ALU.bypass, ['den8'], ['den8'])
            S.op('dve', lambda e: e.reciprocal(den8[:], den8[:]), ['den8'], ['den8'])
            TT(S, 'dve', s1[:], s1[:], den8[:].unsqueeze(2).to_broadcast([128, 8, 64]), ALU.mult, ['s1', 'den8'], ['s1'])
            for hb in range(2):
                pb, kp = ps[2 + hb], 'ps%d' % (2 + hb)
                for h4 in range(4):
                    TR(S, pb[:64, h4 * 128:(h4 + 1) * 128], s1[:, hb * 4 + h4, :], ident, ['s1'], [kp])
                CP(S, 'act', pcT[:, hb * 4:hb * 4 + 4, :], pb[:64, :].rearrange("p (h t) -> p h t", h=4), [kp], ['pcT'])
            for hd in range(8):
                kv = hd // 4
                MM(S, ps[7][:, hd * 64:(hd + 1) * 64], pcT[:, hd, :], vc[:, kv * 64:(kv + 1) * 64], True, True,
                   ['pcT', 'vc'], ['ps7'])
            TT(S, 'dve', v8(acc[:]), v8(ps[7][:]), g3[:, :, 0:1].to_broadcast([128, 8, 64]), ALU.mult,
               ['ps7', 'gates'], ['nacc'])
            nk = (ti + 1) * 128
            if ti >= 8:
                RED(S, t8[:], s1[:].rearrange("p h (s w) -> p h s w", w=2), ALU.add, ['s1'], ['t8'])
                RED(S, imp[:], t8[:].rearrange("p (k g) s -> p k s g", g=4), ALU.add, ['t8'], ['imp'])
                TT(S, 'dve', imp[:], imp[:], adj[:, 30 - 2 * ti:62 - 2 * ti].unsqueeze(1).to_broadcast([128, 2, 32]), ALU.add,
                   ['imp', 'adj'], ['imp'])
                MEMSET(S, 'dve', imp[:, :, 0:1], 1.0e9, ['imp'])
                for kv in range(2):
                    S.op('dve', lambda e: e.max(out=m8a[:], in_=imp[:, kv, :]), ['imp'], ['m8a'])
                    S.op('dve', lambda e: e.match_replace(out=sc2[:, kv, :], in_to_replace=m8a[:], in_values=imp[:, kv, :],
                                                          imm_value=-3.0e9), ['imp', 'm8a'], ['sc2'])
                    S.op('dve', lambda e: e.max(out=m8b[:], in_=sc2[:, kv, :]), ['sc2'], ['m8b'])
                    TS(S, 'dve', sel01[:, kv, :], imp[:, kv, :], m8b[:, 7:8], None, ALU.is_ge, ALU.bypass, ['imp', 'm8b'],
                       ['sel01'])
                    pk = pens[kv]
                    nb = 2 * (ti + 1)
                    TS(S, 'pool' if kv == 0 else 'dve', pk[:, :nk].rearrange("p (b s) -> p b s", s=64),
                       sel01[:, kv, :nb].unsqueeze(2).to_broadcast([128, nb, 64]), 1.0e30, NEG, ALU.mult, ALU.add,
                       ['sel01'], ['pens%d' % kv])
                    TT(S, 'dve', pk[:, ti * 128:nk], pk[:, ti * 128:nk], causalpen[:], ALU.add,
                       ['pens%d' % kv, 'causalpen'], ['pens%d' % kv])
                penk = lambda kv: (pens[kv][:, :nk], ['pens%d' % kv])
            else:
                MEMSET(S, 'pool', pens[0][:, :nk], 0.0, ['pens0'])
                CP(S, 'pool', pens[0][:, ti * 128:nk], causalpen[:], ['causalpen', 'pens0'], ['pens0'])
                penk = lambda kv: (pens[0][:, :nk], ['pens0'])
            wt0 = max(0, ti - 4)
            wc0 = (wt0 - (ti - 4)) * 128
            wnk = (ti - wt0 + 1) * 128
            for hd in range(8):
                pa, pkeys = penk(hd // 4)
                attend(hd, 0, ti, 0, 0, pa, pkeys, ps[4], 'ps4', (dens, 'dens'))
                attend(hd, wt0, ti, 1, 1, penW[:, wc0:wc0 + wnk], ['penW'], ps[5], 'ps5', (denw, 'denw'))
            for (dn, kd, bank, kb, j) in ((dens, 'dens', ps[4], 'ps4', 1), (denw, 'denw', ps[5], 'ps5', 2)):
                S.op('dve', lambda e: e.reciprocal(dn[:], dn[:]), [kd], [kd])
                TT(S, 'dve', g2[:], dn[:], g3[:, :, j], ALU.mult, [kd, 'gates'], ['g2'])
                TT(S, 'dve', v8(tmpo[:]), v8(bank[:]), g2[:].unsqueeze(2).to_broadcast([128, 8, 64]), ALU.mult,
                   [kb, 'g2'], ['ntmpo'])
                TT(S, 'pool', acc[:], acc[:], tmpo[:], ALU.add, ['nacc', 'ntmpo'], ['nacc'])
            TT(S, 'dve', acc[:], acc[:], sgate[:], ALU.mult, ['nacc', 'sgate'], ['nacc'])
            S.dma('sp', br[r0:r0 + 128, 1024:1536], acc[:], reads=['nacc'], writes=[('br', 2, ti)])
def phase_nsa_sample(C, l):
    S, nc, I, O, ps, ident, z, br = C.S, C.nc, C.I, C.O, C.ps, C.ident, C.z, C.br
    cache = I['cache%d' % l]
    PAST = NPAGES * PAGE
    with Phase(C) as ph:
        pool4 = ph.T('spool4', [128, 4]); pband = ph.T('spband', [128, 252])
        MEMSET(S, 'pool', pool4[:], 1.0 / 32, ['pool4'])
        ASEL(S, pool4[:], [[-32, 4]], 0, 1, ['pool4'])
        ASEL(S, pool4[:], [[32, 4]], 31, -1, ['pool4'])
        MEMSET(S, 'pool', pband[:], 1.0 / 32, ['pband'])
        ASEL(S, pband[:], [[-32, 252]], 3968, 1, ['pband'])
        ASEL(S, pband[:], [[32, 252]], -3937, -1, ['pband'])
        pidx = ph.T('pidx', [128, 1])
        IOTA(S, pidx[:], [[0, 1]], 0, 1, ['pidx'])
        G = ph.T('Gm', [16, 4]); GT = ph.T('GTm', [4, 16])
        MEMSET(S, 'pool', G[:], 0.0, ['G']); MEMSET(S, 'pool', GT[:], 0.0, ['GT'])
        for g in range(4):
            ASEL(S, G[:], [[-1, 4]], -4 * g, 1, ['G'], op=ALU.not_equal, fill=1.0)
            ASEL(S, GT[:], [[1, 16]], -4 * g, -1, ['GT'], op=ALU.not_equal, fill=1.0)
        slrow = ph.T('slrow', [1, 2, 16]); one11 = ph.T('one11', [1, 1]); slr = ph.T('slr', [16, 2])
        for kv in range(2):
            for g in range(4):
                MEMSET(S, 'dve', slrow[:, kv, g * 4:(g + 1) * 4], SLOPES[kv * 4 + g], ['slrow'])
        MEMSET(S, 'dve', one11[:], 1.0, ['one11'])
        for kv in range(2):
            MM(S, ps[0][:16, kv:kv + 1], slrow[:, kv, :], one11[:], True, True, ['slrow', 'one11'], ['ps0'])
        CP(S, 'dve', slr[:], ps[0][:16, 0:2], ['ps0'], ['slr'])
        CM = ph.T('CMm', [4, 4]); CMW = ph.T('CMW', [4, 512]); penw = ph.T('penw', [16, 516]); pennew = ph.T('pennew', [16, 4])
        MEMSET(S, 'pool', CM[:], 0.0, ['CM']); ASEL(S, CM[:], [[-1, 4]], 0, 1, ['CM'], fill=NEG)
        MEMSET(S, 'pool', CMW[:], 0.0, ['CMW']); ASEL(S, CMW[:], [[1, 512]], 0, -1, ['CMW'], fill=NEG)
        MM(S, ps[1][:16, 0:512], GT[:], CMW[:], True, True, ['GT', 'CMW'], ['ps1'])
        MM(S, ps[2][:16, 0:4], GT[:], CM[:], True, True, ['GT', 'CM'], ['ps2'])
        CP(S, 'dve', penw[:, 0:512], ps[1][:16, 0:512], ['ps1'], ['penw'])
        CP(S, 'dve', penw[:, 512:516], ps[2][:16, 0:4], ['ps2'], ['penw'])
        CP(S, 'dve', pennew[:], ps[2][:16, 0:4], ['ps2'], ['pennew'])
        posl = ph.T('posl', [16, 2048]); blk32 = ph.T('blk32', [16, 512])
        IOTA(S, posl[:], [[1, 2048]], 0, 0, ['posl'])
        IOTA(S, blk32[:], [[32, 512]], 0, 0, ['blk32'])
        KsT = ph.T('KsT', [128, PAST], BF16); Vs = ph.T('Vs', [128, NPAGES, 128], BF16)
        kcT = ph.T('skcT', [128, 512], BF16); vcs = ph.T('svc', [128, 4, 128], BF16)
        pgb = [ph.T('pgb%d' % i, [128, 512]) for i in range(3)]
        pti = ph.T('pti', [128, NPAGES], I32); ptf = ph.T('ptf', [128, NPAGES]); idx = ph.T('idx', [128, NPAGES], I32)
        qz = ph.T('sqz', [4, 512]); gz = ph.T('sgz', [4, 536]); kvz = ph.T('skvz', [4, 768]); q2 = ph.T('sq2', [4, 4, 128])
        qT2 = ph.T('sqT2', [128, 16], BF16); knT = ph.T('sknT', [128, 2, 4], BF16); vn = ph.T('svn', [4, 2, 128], BF16)
        wbuf = ph.T('swbuf', [128, 4, 256]); KwT = ph.T('sKwT', [128, 516], BF16); Vw = ph.T('sVw', [128, 4, 128], BF16)
        gsig = ph.T('sgsig', [4, 24]); sgate = ph.T('ssgate', [4, 512])
        sc = ph.T('ssc', [16, 2048]); bias_c = ph.T('sbias', [16, 2048]); pex = ph.T('spex', [16, 2048])
        mx = ph.T('smx', [16, 1]); rmx = ph.T('srmx', [16, 1]); den = ph.T('sden', [16, 1]); dc = ph.T('sdc', [16, 1])
        pn = ph.T('spn', [16, 512]); imp = ph.T('simp', [4, 256]); isc = ph.T('sisc', [4, 256])
        m8a = ph.T('sm8a', [4, 8]); m8b = ph.T('sm8b', [4, 8]); sel = ph.T('ssel', [4, 256]); sel16 = ph.T('ssel16', [16, 256])
        pT = ph.T('spT', [128, 16, 16], BF16); pnT = ph.T('spnT', [4, 16], BF16)
        osb = [ph.T('sosb%d' % j, [16, 64]) for j in range(3)]
        snew = ph.T('ssnew', [16, 4]); pnew = ph.T('spnew', [16, 4]); biasw = ph.T('sbiasw', [16, 512])
        acc = ph.T('sacc', [4, 512]); tmpo = ph.T('stmpo', [4, 512])
        kpg = 0
        for b in range(NS_B):
            S.dma('sp', pti[:], I['ptab'][b:b + 1, :].partition_broadcast(128), writes=['pti'])
            CP(S, 'dve', ptf[:], pti[:], ['pti'], ['ptf'])
            STT(S, ptf[:], ptf[:], float(PAGE), pidx[:, 0:1].to_broadcast([128, NPAGES]), ALU.mult, ALU.add,
                ['ptf', 'pidx'], ['ptf'])
            CP(S, 'dve', idx[:], ptf[:], ['ptf'], ['idx'])
            for pg in range(NPAGES):
                pb_, kpb = pgb[kpg % 3], 'pgb%d' % (kpg % 3)
                S.dma('pool', None, None, reads=['idx'], writes=[kpb],
                      fn=lambda e: e.indirect_dma_start(out=pb_[:], out_offset=None, in_=cache,
                                                        in_offset=bass.IndirectOffsetOnAxis(ap=idx[:, pg:pg + 1], axis=0)))
                MM(S, ps[0][:, pg * 4:pg * 4 + 4], pb_[:, 0:128], pool4[:], True, True, [kpb, 'pool4'], ['ps0'])
                j32 = pg % 32
                MM(S, ps[1][:, :128], pband[:, 124 - 4 * j32:252 - 4 * j32], pb_[:, 128:256], j32 == 0, j32 == 31,
                   [kpb, 'pband'], ['ps1'])
                if j32 == 31:
                    CP(S, 'act', vcs[:, pg // 32, :], ps[1][:, :128], ['ps1'], ['vcs'])
                pi = 2 + (pg // 4) % 2
                TR(S, ps[pi][:, (pg % 4) * 128:(pg % 4 + 1) * 128], pb_[:, 256:384], ident, [kpb], ['ps%d' % pi])
                if pg % 4 == 3:
                    CP(S, 'dve', KsT[:, (pg - 3) * 128:(pg + 1) * 128], ps[pi][:, :], ['ps%d' % pi], ['KsT'])
                CP(S, 'act' if pg % 2 == 0 else 'pool', Vs[:, pg, :], pb_[:, 384:512], [kpb], ['Vs'])
                kpg += 1
            CP(S, 'dve', kcT[:], ps[0][:, :], ['ps0'], ['kcT'])
            rows = slice(T + b * NS_T, T + (b + 1) * NS_T)
            S.dma('sp', qz[:], z[rows, O_NQ:O_NQ + 512], writes=['qz'])
            S.dma('act', gz[:], z[rows, O_NBG:O_NBG + 536], writes=['gz'])
            S.dma('sp', kvz[:], z[rows, O_NKV:O_NKV + 768], writes=['kvz'])
            S.dma('act', wbuf[:], I['win'][l, b].rearrange("(a p) c -> p a c", p=128), writes=['wbuf'])
            CP(S, 'dve', q2[:].rearrange("p g (k d) -> p g k d", k=2), qz[:].rearrange("p (k g d) -> p g k d", k=2, g=4),
               ['qz'], ['q2'])
            for g in range(4):
                TR(S, ps[2][:, g * 4:(g + 1) * 4], q2[:, g, :], ident, ['q2'], ['ps2'])
            TS(S, 'dve', qT2[:], ps[2][:, 0:16], 0.125, None, ALU.mult, ALU.bypass, ['ps2'], ['qT2'])
            TR(S, ps[3][:, 0:4], kvz[:, 256:384], ident, ['kvz'], ['ps3'])
            TR(S, ps[3][:, 4:8], kvz[:, 512:640], ident, ['kvz'], ['ps3'])
            CP(S, 'dve', knT[:].rearrange("p a t -> p (a t)"), ps[3][:, 0:8], ['ps3'], ['knT'])
            CP(S, 'act', vn[:, 0, :], kvz[:, 384:512], ['kvz'], ['vn'])
            CP(S, 'act', vn[:, 1, :], kvz[:, 640:768], ['kvz'], ['vn'])
            for a in range(4):
                TR(S, ps[2][:, a * 128:(a + 1) * 128], wbuf[:, a, 0:128], ident, ['wbuf', 'qT2'], ['ps2'])
            CP(S, 'dve', KwT[:, 0:512], ps[2][:, :], ['ps2'], ['KwT'])
            CP(S, 'dve', KwT[:, 512:516], knT[:, 1, :], ['knT', 'KwT'], ['KwT'])
            CP(S, 'pool', Vw[:], wbuf[:, :, 128:256], ['wbuf'], ['Vw'])
            ACT(S, gsig[:], gz[:, 0:24], AF.Sigmoid, ['gz'], ['gsig'])
            ACT(S, sgate[:], gz[:, 24:536], AF.Silu, ['gz'], ['sgate'])
            for kv in range(2):
                p0 = kv * 64
                lq = qT2[p0:p0 + 64, :]
                sl = slr[:, kv:kv + 1]
                MM(S, ps[1][:16, :], lq, kcT[p0:p0 + 64, :], True, True, ['qT2', 'kcT'], ['ps1'])
                STT(S, sc[:, :512], blk32[:], sl, ps[1][:16, :], ALU.mult, ALU.add, ['blk32', 'slr', 'ps1'], ['sc'])
                RED(S, mx[:], sc[:, :512], ALU.max, ['sc'], ['mx'])
                TS(S, 'dve', mx[:], mx[:], -1.0, None, ALU.mult, ALU.bypass, ['mx'], ['mx'])
                ACT(S, pn[:], sc[:, :512], AF.Exp, ['sc', 'mx'], ['pn', 'den'], bias=mx[:, 0:1], accum_out=den[:, 0:1])
                S.op('dve', lambda e: e.reciprocal(den[:], den[:]), ['den'], ['den'])
                TS(S, 'dve', pn[:], pn[:], den[:, 0:1], None, ALU.mult, ALU.bypass, ['pn', 'den'], ['pn'])
                MM(S, ps[2][:4, :], G[:], pn[:], True, True, ['G', 'pn'], ['ps2'])
                RED(S, imp[:], ps[2][:4, :].rearrange("p (s w) -> p s w", w=2), ALU.add, ['ps2'], ['imp'])
                for a in range(4):
                    TR(S, ps[3][:, a * 16:(a + 1) * 16], pn[:, a * 128:(a + 1) * 128], ident, ['pn'], ['ps3'])
                CP(S, 'act', pT[:, 0:4, :].rearrange("p a r -> p (a r)"), ps[3][:, 0:64], ['ps3'], ['pT'])
                for a in range(4):
                    MM(S, ps[4][:16, 0:64], pT[:, a, :], vcs[:, a, p0:p0 + 64], a == 0, a == 3, ['pT', 'vcs'], ['ps4'])
                CP(S, 'dve', osb[0][:], ps[4][:16, 0:64], ['ps4'], ['osb0'])
                MEMSET(S, 'dve', imp[:, 0:1], -1.0, ['imp'])
                S.op('dve', lambda e: e.max(out=m8a[:], in_=imp[:]), ['imp'], ['m8a'])
                S.op('dve', lambda e: e.match_replace(out=isc[:], in_to_replace=m8a[:], in_values=imp[:], imm_value=-3.0),
                     ['imp', 'm8a'], ['isc'])
                S.op('dve', lambda e: e.max(out=m8b[:], in_=isc[:]), ['isc'], ['m8b'])
                TS(S, 'dve', sel[:], imp[:], m8b[:, 5:6], None, ALU.is_ge, ALU.bypass, ['imp', 'm8b'], ['sel'])
                MEMSET(S, 'dve', sel[:, 0:1], 1.0, ['sel'])
                MM(S, ps[2][:16, 0:256], GT[:], sel[:], True, True, ['GT', 'sel'], ['ps2'])
                CP(S, 'dve', sel16[:], ps[2][:16, 0:256], ['ps2'], ['sel16'])
                MM(S, ps[1][:16, 0:4], lq, knT[p0:p0 + 64, 0, :], True, True, ['qT2', 'knT'], ['ps1'])
                STT(S, snew[:], posl[:, 0:4], sl, ps[1][:16, 0:4], ALU.mult, ALU.add, ['posl', 'slr', 'ps1'], ['snew'])
                TT(S, 'dve', snew[:], snew[:], pennew[:], ALU.add, ['snew', 'pennew'], ['snew'])
                RED(S, rmx[:], snew[:], ALU.max, ['snew'], ['rmx'])

                def scores(c):
                    TS(S, 'dve', bias_c[:], posl[:], float(2048 * c - PAST), sl, ALU.add, ALU.mult, ['posl', 'slr'], ['bias_c'])
                    TS(S, 'pool', pex[:].rearrange("p (b s) -> p b s", s=64),
                       sel16[:, c * 32:(c + 1) * 32].unsqueeze(2).to_broadcast([16, 32, 64]), 1.0e30, NEG, ALU.mult, ALU.add,
                       ['sel16'], ['pex'])
                    TT(S, 'dve', bias_c[:], bias_c[:], pex[:], ALU.add, ['bias_c', 'pex'], ['bias_c'])
                    for c4 in range(4):
                        pb2, kp2 = ps[1 + c4 % 2], 'ps%d' % (1 + c4 % 2)
                        MM(S, pb2[:16, :], lq, KsT[p0:p0 + 64, c * 2048 + c4 * 512:c * 2048 + (c4 + 1) * 512], True, True,
                           ['qT2', 'KsT'], [kp2])
                        TT(S, 'dve', sc[:, c4 * 512:(c4 + 1) * 512], pb2[:16, :], bias_c[:, c4 * 512:(c4 + 1) * 512], ALU.add,
                           [kp2, 'bias_c'], ['sc'])

                for c in range(8):
                    scores(c)
                    RED(S, mx[:], sc[:], ALU.max, ['sc'], ['mx'])
                    TT(S, 'dve', rmx[:], rmx[:], mx[:], ALU.max, ['rmx', 'mx'], ['rmx'])
                TS(S, 'dve', rmx[:], rmx[:], -1.0, None, ALU.mult, ALU.bypass, ['rmx'], ['rmx'])
                ACT(S, pnew[:], snew[:], AF.Exp, ['snew', 'rmx'], ['pnew', 'den'], bias=rmx[:, 0:1], accum_out=den[:, 0:1])
                for c in range(8):
                    scores(c)
                    ACT(S, pex[:], sc[:], AF.Exp, ['sc', 'rmx'], ['pex', 'dc'], bias=rmx[:, 0:1], accum_out=dc[:, 0:1])
                    TT(S, 'dve', den[:], den[:], dc[:], ALU.add, ['den', 'dc'], ['den'])
                    for j in range(16):
                        TR(S, ps[3][:, j * 16:(j + 1) * 16], pex[:, j * 128:(j + 1) * 128], ident, ['pex'], ['ps3'])
                    CP(S, 'act', pT[:].rearrange("p a r -> p (a r)"), ps[3][:, 0:256], ['ps3'], ['pT'])
                    for j in range(16):
                        MM(S, ps[4][:16, 0:64], pT[:, j, :], Vs[:, c * 16 + j, p0:p0 + 64], c == 0 and j == 0, False,
                           ['pT', 'Vs'], ['ps4'])
                TR(S, ps[3][:4, 0:16], pnew[:], ident, ['pnew'], ['ps3'])
                CP(S, 'act', pnT[:], ps[3][:4, 0:16], ['ps3'], ['pnT'])
                MM(S, ps[4][:16, 0:64], pnT[:], vn[:, 0, p0:p0 + 64], False, True, ['pnT', 'vn'], ['ps4'])
                S.op('dve', lambda e: e.reciprocal(den[:], den[:]), ['den'], ['den'])
                TS(S, 'dve', osb[1][:], ps[4][:16, 0:64], den[:, 0:1], None, ALU.mult, ALU.bypass, ['ps4', 'den'], ['osb1'])
                TS(S, 'dve', biasw[:], posl[:, 0:512], -512.0, sl, ALU.add, ALU.mult, ['posl', 'slr'], ['biasw'])
                MM(S, ps[1][:16, :], lq, KwT[p0:p0 + 64, 0:512], True, True, ['qT2', 'KwT'], ['ps1'])
                MM(S, ps[2][:16, 0:4], lq, KwT[p0:p0 + 64, 512:516], True, True, ['qT2', 'KwT'], ['ps2'])
                TT(S, 'dve', sc[:, 0:512], ps[1][:16, :], biasw[:], ALU.add, ['ps1', 'biasw'], ['sc'])
                STT(S, sc[:, 512:516], posl[:, 0:4], sl, ps[2][:16, 0:4], ALU.mult, ALU.add, ['posl', 'slr', 'ps2', 'sc'], ['sc'])
                TT(S, 'dve', sc[:, 0:516], sc[:, 0:516], penw[:], ALU.add, ['sc', 'penw'], ['sc'])
                RED(S, mx[:], sc[:, 0:516], ALU.max, ['sc'], ['mx'])
                TS(S, 'dve', mx[:], mx[:], -1.0, None, ALU.mult, ALU.bypass, ['mx'], ['mx'])
                ACT(S, pex[:, 0:516], sc[:, 0:516], AF.Exp, ['sc', 'mx'], ['pex', 'den'], bias=mx[:, 0:1], accum_out=den[:, 0:1])
                for a in range(4):
                    TR(S, ps[3][:, a * 16:(a + 1) * 16], pex[:, a * 128:(a + 1) * 128], ident, ['pex'], ['ps3'])
                TR(S, ps[3][:4, 64:80], pex[:, 512:516], ident, ['pex'], ['ps3'])
                CP(S, 'act', pT[:, 0:4, :].rearrange("p a r -> p (a r)"), ps[3][:, 0:64], ['ps3'], ['pT'])
                CP(S, 'act', pnT[:], ps[3][:4, 64:80], ['ps3'], ['pnT'])
                for a in range(4):
                    MM(S, ps[4][:16, 0:64], pT[:, a, :], Vw[:, a, p0:p0 + 64], a == 0, False, ['pT', 'Vw'], ['ps4'])
                MM(S, ps[4][:16, 0:64], pnT[:], vn[:, 1, p0:p0 + 64], False, True, ['pnT', 'vn'], ['ps4'])
                S.op('dve', lambda e: e.reciprocal(den[:], den[:]), ['den'], ['den'])
                TS(S, 'dve', osb[2][:], ps[4][:16, 0:64], den[:, 0:1], None, ALU.mult, ALU.bypass, ['ps4', 'den'], ['osb2'])
                for j in range(3):
                    for g in range(4):
                        hd = kv * 4 + g
                        MM(S, ps[5 + j][:4, hd * 64:(hd + 1) * 64], ident[:16, g * 4:(g + 1) * 4], osb[j][:], True, True,
                           ['ident', 'osb%d' % j], ['ps%d' % (5 + j)])
            g3 = gsig[:].rearrange("p (h j) -> p h j", j=3)
            v8 = lambda t: t.rearrange("p (h d) -> p h d", h=8)
            for j in range(3):
                dst = acc if j == 0 else tmpo
                TT(S, 'dve', v8(dst[:]), v8(ps[5 + j][:4, :]), g3[:, :, j:j + 1].to_broadcast([4, 8, 64]), ALU.mult,
                   ['ps%d' % (5 + j), 'gsig'], ['sacc' if j == 0 else 'stmpo'])
                if j > 0:
                    TT(S, 'dve', acc[:], acc[:], tmpo[:], ALU.add, ['sacc', 'stmpo'], ['sacc'])
            TT(S, 'dve', acc[:], acc[:], sgate[:], ALU.mult, ['sacc', 'sgate'], ['sacc'])
            S.dma('sp', br[rows, 1024:1536], acc[:], reads=['sacc'], writes=[('br', 3, b)])


HAVE_NSA_SAMPLE = True


def build_program(debug=False):
    nc = bass.Bass("TRN2", target_bir_lowering=False)
    C = Ctx()
    C.nc = nc
    C.uid = 0
    dt_in = lambda name, shape, dt=F32: nc.dram_tensor(name, shape, dt, kind="ExternalInput").ap()
    dt_out = lambda name, shape, dt=F32: nc.dram_tensor(name, shape, dt, kind="ExternalOutput").ap()
    dt_scr = lambda name, shape, dt=F32: nc.dram_tensor(name, shape, dt, kind="Internal").ap()
    I = {}
    I['x'] = dt_in("x", [NTOK, D])
    I['w_in'] = dt_in("w_in", [DEPTH, D, INW])
    I['b_in'] = dt_in("b_in", [DEPTH, INW])
    I['win'] = dt_in("win", [DEPTH, NS_B, WINB, 256])
    I['shift'] = dt_in("shift", [DEPTH, NS_B, RWIN])
    I['gla_st'] = dt_in("gla_st", [DEPTH, NS_B, 4, 64, 128])
    I['rw_st'] = dt_in("rw_st", [DEPTH, NS_B, 8, 64, 64])
    I['gla_a_up'] = dt_in("gla_a_up", [DEPTH, 16, 256])
    I['gla_a_bias'] = dt_in("gla_a_bias", [DEPTH, 256])
    I['gla_norm'] = dt_in("gla_norm", [DEPTH, 512])
    I['rwkv_mu'] = dt_in("rwkv_mu", [DEPTH, RWIN])
    for nm in ('rwkv_w0', 'rwkv_a0', 'rwkv_k_k', 'rwkv_k_a', 'rwkv_r_k', 'rwkv_ln_w', 'rwkv_ln_b'):
        I[nm] = dt_in(nm, [DEPTH, 512])
    I['rwkv_w_up'] = dt_in("rwkv_w_up", [DEPTH, 64, 512])
    I['rwkv_a_up'] = dt_in("rwkv_a_up", [DEPTH, 64, 512])
    I['w_br'] = dt_in("w_br", [DEPTH, 3, 512, D])
    I['w_out'] = dt_in("w_out", [DEPTH, D, D])
    I['ln_g'] = dt_in("ln_g", [DEPTH, D])
    I['ln_b'] = dt_in("ln_b", [DEPTH, D])
    I['ptab'] = dt_in("ptab", [NS_B, NPAGES], I32)
    if HAVE_NSA_SAMPLE:
        for ll in range(DEPTH):
            I['cache%d' % ll] = dt_in("cache%d" % ll, [NPOOL * PAGE, 512])
    O = {}
    O['y'] = dt_out("y", [NTOK, D])
    O['kv'] = dt_out("kv", [DEPTH, NTOK, 512])
    O['winp'] = dt_out("winp", [DEPTH, WINB, 256])
    O['wins'] = dt_out("wins", [DEPTH, NS_B, WINB, 256])
    O['shp'] = dt_out("shp", [DEPTH, RWIN])
    O['shs'] = dt_out("shs", [DEPTH, NS_B, RWIN])
    O['glap'] = dt_out("glap", [DEPTH, 4, 64, 128])
    O['glas'] = dt_out("glas", [DEPTH, NS_B, 4, 64, 128])
    O['rwp'] = dt_out("rwp", [DEPTH, 8, 64, 64])
    O['rws'] = dt_out("rws", [DEPTH, NS_B, 8, 64, 64])
    C.I, C.O = I, O
    C.z = dt_scr("z", [NTOK, INW])
    C.xs = dt_scr("xs", [NTOK, D])
    if debug:
        C.br = dt_out("br", [NTOK, 1536])
    else:
        C.br = dt_scr("br", [NTOK, 1536])
    C.sg_scr = dt_scr("sg_scr", [NTOK, 512])
    C.bv_scr = dt_scr("bv_scr", [NTOK, 512])
    C.y_scr = dt_scr("y_scr", [NTOK, 512])
    C.rk_scr = dt_scr("rk_scr", [NTOK, 3, 512], BF16)
    C.fm_scr = dt_scr("fm_scr", [3, 64, 8, NTOK])
    S = Sched(nc)
    C.S = S
    with contextlib.ExitStack() as gst:
        C.ps = [gst.enter_context(nc.psum_tensor("ps%d" % i, [128, 512], F32)) for i in range(8)]
        C.ident = gst.enter_context(nc.sbuf_tensor("ident", [128, 128], F32))
        zt = gst.enter_context(nc.sbuf_tensor("zerot", [128, 512], F32))
        MEMSET(S, 'pool', C.ident[:], 1.0, ['ident'])
        ASEL(S, C.ident[:], [[-1, 128]], 0, 1, ['ident'], op=ALU.is_equal)
        MEMSET(S, 'dve', zt[:], 0.0, ['zerot'])
        nl = 1 if debug else DEPTH
        for l in range(nl):
            xsrc = I['x'] if l == 0 else C.xs
            xdst = O['y'] if l == DEPTH - 1 else C.xs
            phase_inproj(C, l, xsrc)
            direct_outputs(C, l)
            phase_gla(C, l)
            phase_rwkv_pre(C, l)
            phase_rwkv_scan(C, l)
            phase_rwkv_post(C, l)
            phase_nsa_prompt(C, l)
            if not HAVE_NSA_SAMPLE:
                S.dma('sp', C.br[T:NTOK, 1024:1536], zt[:NSAMP, :], reads=['zerot'])
                S.barrier()
            else:
                phase_nsa_sample(C, l)
            phase_merge(C, l, xsrc, xdst)
        S.finish()
    print("program built: n_inst", S.n_inst, flush=True)
    return nc


_NC_CACHE = {}


def kernel(x_prompt, x_sample, cache_nsa_kv, state_nsa_win, state_gla, state_rwkv, state_rwkv_shift,
           page_table, w_in, b_in, gla_a_up, gla_a_bias, gla_norm, rwkv_mu, rwkv_w0, rwkv_w_up,
           rwkv_a0, rwkv_a_up, rwkv_k_k, rwkv_k_a, rwkv_r_k, rwkv_ln_w, rwkv_ln_b, w_br, w_out, ln_g, ln_b,
           _debug=False, _cores=NCORES):
    f = lambda a: np.ascontiguousarray(np.asarray(a, dtype=np.float32))
    key = 'nc_dbg' if _debug else 'nc'
    if key not in _NC_CACHE:
        _NC_CACHE[key] = build_program(debug=_debug)
    nc = _NC_CACHE[key]
    shared = dict(w_in=f(w_in), b_in=f(b_in), gla_a_up=f(gla_a_up), gla_a_bias=f(gla_a_bias), gla_norm=f(gla_norm),
                  rwkv_mu=f(rwkv_mu), rwkv_w0=f(rwkv_w0), rwkv_a0=f(rwkv_a0), rwkv_k_k=f(rwkv_k_k), rwkv_k_a=f(rwkv_k_a),
                  rwkv_r_k=f(rwkv_r_k).reshape(DEPTH, 512), rwkv_ln_w=f(rwkv_ln_w), rwkv_ln_b=f(rwkv_ln_b),
                  rwkv_w_up=f(rwkv_w_up), rwkv_a_up=f(rwkv_a_up), w_br=f(w_br), w_out=f(w_out), ln_g=f(ln_g), ln_b=f(ln_b))
    if HAVE_NSA_SAMPLE:
        cache = np.asarray(cache_nsa_kv)
        for ll in range(DEPTH):
            shared['cache%d' % ll] = np.ascontiguousarray(cache[ll], dtype=np.float32).reshape(NPOOL * PAGE, 512)
    ptab = np.ascontiguousarray(np.asarray(page_table), dtype=np.int32)
    in_maps = []
    for c in range(_cores):
        bs = slice(c * NS_B, (c + 1) * NS_B)
        xx = np.concatenate([f(x_prompt[c]), f(x_sample[bs]).reshape(NSAMP, D)], axis=0)
        m = dict(shared)
        m.update({
            'x': xx,
            'win': f(state_nsa_win[:, bs]).reshape(DEPTH, NS_B, WINB, 256),
            'shift': f(state_rwkv_shift[:, bs]),
            'gla_st': f(state_gla[:, bs]),
            'rw_st': f(state_rwkv[:, bs]),
            'ptab': np.ascontiguousarray(ptab[bs]),
        })
        in_maps.append(m)
    res = run_bass_kernel_spmd(nc, in_maps, core_ids=list(range(_cores)))
    R = res.results
    if _debug:
        return R
    st = lambda key: [R[c][key] for c in range(NCORES)]
    y = st('y')
    y_prompt = np.stack([a[:T] for a in y], 0)
    y_sample = np.concatenate([a[T:].reshape(NS_B, NS_T, D) for a in y], 0)
    kv = st('kv')
    kv_p = np.stack([a[:, :T].reshape(DEPTH, T, 4, 2, 64) for a in kv], 1)
    kv_s = np.concatenate([a[:, T:].reshape(DEPTH, NS_B, NS_T, 4, 2, 64) for a in kv], 1)
    win_p = np.stack([a.reshape(DEPTH, WINB, 2, 2, 64) for a in st('winp')], 1)
    win_s = np.concatenate([a.reshape(DEPTH, NS_B, WINB, 2, 2, 64) for a in st('wins')], 1)
    gla_p = np.stack(st('glap'), 1)
    gla_s = np.concatenate(st('glas'), 1)
    rw_p = np.stack(st('rwp'), 1)
    rw_s = np.concatenate(st('rws'), 1)
    sh_p = np.stack(st('shp'), 1)
    sh_s = np.concatenate(st('shs'), 1)
    return (y_prompt, y_sample, kv_p, kv_s, win_p, win_s, gla_p, gla_s, rw_p, rw_s, sh_p, sh_s)
```

```python
import contextlib
import numpy as np
import concourse.bass as bass
import concourse.mybir as mybir
from concourse.bass_utils import run_bass_kernel_spmd

F32 = mybir.dt.float32
BF16 = mybir.dt.bfloat16
I32 = mybir.dt.int32
U32 = mybir.dt.uint32
AF = mybir.ActivationFunctionType
ALU = mybir.AluOpType
AX = mybir.AxisListType

NCORES = 8
D = 1024
DEPTH = 2
T = 2048
NS_B = 4
NS_T = 4
NSAMP = NS_B * NS_T
NTOK = T + NSAMP
INW = 8616
O_GQ, O_GK, O_GV, O_GA, O_GG = 0, 256, 512, 1024, 1040
O_RZ = 1552
RWIN = 2176
O_NQ = 3728
O_NKV = 4240
O_NBG = 5008
O_NG = 5032
O_MG = 5544
WINB = 512
NPAGES = 128
PAGE = 128
NPOOL = 5120


class Sched:
    def __init__(self, nc, n_dma_sems=40, same_eng_wait=True):
        self.nc = nc
        self.e = {'pe': nc.tensor, 'dve': nc.vector, 'act': nc.scalar,
                  'pool': nc.gpsimd, 'sp': nc.sync}
        self.sem = {k: nc.alloc_semaphore(name="s_" + k) for k in self.e}
        self.cnt = {k: 0 for k in self.e}
        self.dsem = [nc.alloc_semaphore(name="d%d" % i) for i in range(n_dma_sems)]
        self.dcnt = [0] * n_dma_sems
        self.dnext = 0
        self.seen = {k: {} for k in self.e}
        self.lastw = {}
        self.readers = {}
        self.same_eng_wait = same_eng_wait
        self.out_events = []
        self.n_inst = 0

    def _wait(self, eng, ev, same_ok=False):
        semkey, semh, val, src = ev
        if src == eng:
            if eng == 'pe' or same_ok or not self.same_eng_wait:
                return
        if self.seen[eng].get(semkey, 0) >= val:
            return
        self.e[eng].wait_ge(semh, val)
        self.seen[eng][semkey] = val
        self.n_inst += 1

    def _deps(self, eng, reads, writes):
        for k in reads:
            ev = self.lastw.get(k)
            if ev is not None:
                self._wait(eng, ev)
        for k in writes:
            ev = self.lastw.get(k)
            if ev is not None:
                self._wait(eng, ev)
            for ev in self.readers.get(k, ()):
                self._wait(eng, ev, same_ok=True)

    def _record(self, ev, reads, writes):
        for k in writes:
            self.lastw[k] = ev
            self.readers[k] = []
        for k in reads:
            if k in writes:
                continue
            lst = self.readers.setdefault(k, [])
            lst.append(ev)
            if len(lst) > 48:
                d = {}
                for e2 in lst:
                    if e2[0] not in d or d[e2[0]][2] < e2[2]:
                        d[e2[0]] = e2
                self.readers[k] = list(d.values())

    def op(self, eng, fn, reads=(), writes=()):
        self._deps(eng, reads, writes)
        inst = fn(self.e[eng])
        self.cnt[eng] += 1
        inst.then_inc(self.sem[eng], 1)
        ev = (eng, self.sem[eng], self.cnt[eng], eng)
        self._record(ev, reads, writes)
        self.n_inst += 1
        return ev

    def dma(self, q, out, in_, reads=(), writes=(), is_output=False, fn=None, **kw):
        self._deps(q, reads, writes)
        i = self.dnext
        self.dnext = (self.dnext + 1) % len(self.dsem)
        if self.dcnt[i] > 0:
            self._wait(q, (('d', i), self.dsem[i], 16 * self.dcnt[i], None))
        if fn is not None:
            inst = fn(self.e[q])
        else:
            inst = self.e[q].dma_start(out=out, in_=in_, **kw)
        inst.then_inc(self.dsem[i], 16)
        self.dcnt[i] += 1
        ev = (('d', i), self.dsem[i], 16 * self.dcnt[i], None)
        self._record(ev, reads, writes)
        if is_output:
            self.out_events.append(ev)
        self.n_inst += 1
        return ev

    def barrier(self):
        evs = [(k, self.sem[k], self.cnt[k], k) for k in self.e if self.cnt[k] > 0]
        evs += [(('d', i), self.dsem[i], 16 * self.dcnt[i], None)
                for i in range(len(self.dsem)) if self.dcnt[i] > 0]
        for eng in self.e:
            for ev in evs:
                if ev[3] == eng:
                    continue
                self._wait(eng, ev)
        self.lastw = {}
        self.readers = {}

    def finish(self):
        for ev in self.out_events:
            self._wait('sp', ev)
        self.barrier()


def token_tiles():
    tl = [(i * 128, 128) for i in range(T // 128)]
    tl.append((T, NSAMP))
    return tl


class Ctx:
    pass


class Phase:
    def __init__(self, C):
        self.C = C
        self.st = contextlib.ExitStack()

    def __enter__(self):
        self.st.__enter__()
        return self

    def __exit__(self, *a):
        if a[0] is None:
            self.C.S.barrier()
        return self.st.__exit__(*a)

    def T(self, name, shape, dt=F32):
        self.C.uid += 1
        return self.st.enter_context(self.C.nc.sbuf_tensor("%s_%d" % (name, self.C.uid), shape, dt))


def MM(S, out, lhsT, rhs, start=True, stop=True, r=(), w=()):
    return S.op('pe', lambda e: e.matmul(out, lhsT=lhsT, rhs=rhs, start=start, stop=stop), r, w)


def TR(S, out, in_, ident, r=(), w=()):
    n = in_.shape[0]
    return S.op('pe', lambda e: e.transpose(out, in_, ident[:n, :n]), list(r) + ['ident'], w)


def TT(S, eng, out, a, b, op, r=(), w=()):
    return S.op(eng, lambda e: e.tensor_tensor(out=out, in0=a, in1=b, op=op), r, w)


def STT(S, out, a, sc, b, op0, op1, r=(), w=()):
    return S.op('dve', lambda e: e.scalar_tensor_tensor(out=out, in0=a, scalar=sc, in1=b, op0=op0, op1=op1), r, w)


def TS(S, eng, out, a, s1, s2, op0, op1, r=(), w=()):
    return S.op(eng, lambda e: e.tensor_scalar(out=out, in0=a, scalar1=s1, scalar2=s2, op0=op0, op1=op1), r, w)


def ACT(S, out, in_, func, r=(), w=(), **kw):
    return S.op('act', lambda e: e.activation(out=out, in_=in_, func=func, **kw), r, w)


def CP(S, eng, out, in_, r=(), w=()):
    if eng == 'act':
        return S.op('act', lambda e: e.copy(out, in_), r, w)
    return S.op(eng, lambda e: e.tensor_copy(out, in_), r, w)


def MEMSET(S, eng, ap, val, w=()):
    return S.op(eng, lambda e: e.memset(ap, val), (), w)


def ASEL(S, ap, pattern, base, cm, w, op=None, fill=0.0):
    op = ALU.is_ge if op is None else op
    return S.op('pool', lambda e: e.affine_select(out=ap, in_=ap, pattern=pattern, compare_op=op, fill=fill,
                                                  base=base, channel_multiplier=cm), w, w)


def RED(S, out, in_, op, r=(), w=()):
    return S.op('dve', lambda e: e.tensor_reduce(out=out, in_=in_, axis=AX.X, op=op), r, w)


def phase_inproj(C, l, xsrc):
    S, nc, I, ps, ident = C.S, C.nc, C.I, C.ps, C.ident
    tiles = token_tiles()
    z = C.z
    with Phase(C) as ph:
        xT = ph.T("xT", [128, 8, NTOK], BF16)
        xin = [ph.T("xin%d" % i, [128, D]) for i in range(2)]
        ones_bf = ph.T("ones_bf", [1, 128], BF16)
        MEMSET(S, 'dve', ones_bf[:], 1.0, ['ones_bf'])
        for ti, (r0, nr) in enumerate(tiles):
            xb = xin[ti % 2]
            kx = 'xin%d' % (ti % 2)
            S.dma('sp', xb[:nr, :], xsrc[r0:r0 + nr, :], reads=['xs'], writes=[kx])
            for half in range(2):
                pi = (ti * 2 + half) % 2
                pb, kp = ps[pi], 'ps%d' % pi
                for c4 in range(4):
                    c = half * 4 + c4
                    TR(S, pb[:, c4 * 128:c4 * 128 + nr], xb[:nr, c * 128:(c + 1) * 128], ident, [kx], [kp])
                src = pb[:].rearrange("p (c t) -> p c t", c=4)[:, :, :nr]
                dst = xT[:, half * 4:half * 4 + 4, r0:r0 + nr]
                CP(S, 'dve' if half == 0 else 'act', dst, src, [kp], [('xT', ti)])
        wbuf = [ph.T("wbuf%d" % i, [128, 8, 512], BF16) for i in range(2)]
        bbuf = [ph.T("bbuf%d" % i, [1, 512], BF16) for i in range(2)]
        ost = [ph.T("ost%d" % i, [128, 512]) for i in range(4)]
        ncol = (INW + 511) // 512
        k_ev = 0
        for j in range(ncol):
            c0 = j * 512
            cw = min(512, INW - c0)
            wb, bb, kw_ = wbuf[j % 2], bbuf[j % 2], 'wbuf%d' % (j % 2)
            S.dma('pool', wb[:, :, :cw], I['w_in'][l, :, c0:c0 + cw].rearrange("(c p) n -> p c n", p=128),
                  writes=[kw_])
            S.dma('pool', bb[:, :cw], I['b_in'][l:l + 1, c0:c0 + cw], writes=[kw_ + 'b'])
            for ti, (r0, nr) in enumerate(tiles):
                pi = 2 + (k_ev % 4)
                pb, kp = ps[pi], 'ps%d' % pi
                for c in range(8):
                    MM(S, pb[:nr, :cw], xT[:, c, r0:r0 + nr], wb[:, c, :cw], c == 0, False, [('xT', ti), kw_], [kp])
                MM(S, pb[:nr, :cw], ones_bf[:, :nr], bb[:, :cw], False, True, ['ones_bf', kw_ + 'b'], [kp])
                ob, ko = ost[k_ev % 4], 'ost%d' % (k_ev % 4)
                CP(S, 'dve' if k_ev % 2 == 0 else 'act', ob[:nr, :cw], pb[:nr, :cw], [kp], [ko])
                S.dma('sp', z[r0:r0 + nr, c0:c0 + cw], ob[:nr, :cw], reads=[ko], writes=['z'])
                k_ev += 1


def direct_outputs(C, l):
    S, I, O, z = C.S, C.I, C.O, C.z
    S.dma('sp', O['kv'][l], z[:, O_NKV:O_NKV + 512], is_output=True)
    S.dma('sp', O['winp'][l], z[T - WINB:T, O_NKV + 512:O_NKV + 768], is_output=True)
    for b in range(NS_B):
        S.dma('sp', O['wins'][l, b, 0:WINB - NS_T, :], I['win'][l, b, NS_T:WINB, :], is_output=True)
        S.dma('sp', O['wins'][l, b, WINB - NS_T:WINB, :],
              z[T + b * NS_T:T + (b + 1) * NS_T, O_NKV + 512:O_NKV + 768], is_output=True)
        S.dma('sp', O['shs'][l, b:b + 1, :], z[T + b * NS_T + NS_T - 1:T + (b + 1) * NS_T, O_RZ:O_RZ + RWIN],
              is_output=True)
    S.dma('sp', O['shp'][l:l + 1, :], z[T - 1:T, O_RZ:O_RZ + RWIN], is_output=True)


def phase_gla(C, l):
    S, nc, I, O, ps, ident, z, br = C.S, C.nc, C.I, C.O, C.ps, C.ident, C.z, C.br
    with Phase(C) as ph:
        tri = ph.T('tri', [64, 64]); slow = ph.T('slow', [64, 64]); cmask = ph.T('cmask', [64, 64])
        MEMSET(S, 'pool', tri[:], -1.0 / 16, ['tri'])
        ASEL(S, tri[:], [[1, 64]], 0, -1, ['tri'])
        MEMSET(S, 'pool', slow[:], -1.0 / 16, ['slow'])
        ASEL(S, slow[:], [[-1, 64]], -1, 1, ['slow'])
        MEMSET(S, 'pool', cmask[:], 1.0, ['cmask'])
        ASEL(S, cmask[:], [[1, 64]], 0, -1, ['cmask'])
        aup = ph.T('aup', [16, 256]); abias = ph.T('abias', [1, 256]); ones1 = ph.T('ones1', [1, 64])
        normg = ph.T('normg', [64, 512])
        S.dma('sp', aup[:], I['gla_a_up'][l], writes=['aup'])
        S.dma('sp', abias[:], I['gla_a_bias'][l:l + 1, :], writes=['abias'])
        MEMSET(S, 'dve', ones1[:], 1.0, ['ones1'])
        S.dma('sp', normg[:], I['gla_norm'][l:l + 1, :].partition_broadcast(64), writes=['normg'])
        Sst = ph.T('Sst', [64, 512])
        zb = [ph.T('gz%d' % i, [64, 1552]) for i in range(2)]
        gaT = ph.T('gaT', [16, 64]); la = ph.T('la', [64, 256])
        ecum = ph.T('ecum', [64, 4, 64]); encum = ph.T('encum', [64, 4, 64]); edl = ph.T('edl', [64, 256])
        qdT = ph.T('qdT', [64, 4, 64]); kdT = ph.T('kdT', [64, 4, 64]); kl = ph.T('kl', [64, 256])
        attT = ph.T('attT', [64, 4, 64])
        ssq = ph.T('ssq', [64, 4]); rstd = ph.T('rstd', [64, 4]); junk = ph.T('junk', [64, 128])
        sg = ph.T('sg', [64, 512]); bro = [ph.T('bro%d' % i, [64, 512]) for i in range(2)]
        seqs = [('p', 0, T, 64, None)] + [('s', T + b * NS_T, NS_T, NS_T, b) for b in range(NS_B)]
        kc = 0
        for (kind, r0, L, n, b) in seqs:
            if kind == 'p':
                MEMSET(S, 'dve', Sst[:], 0.0, ['Sst'])
            else:
                S.dma('sp', Sst[:].rearrange("d (h e) -> d h e", h=4), I['gla_st'][l, b].rearrange("h d e -> d h e"),
                      writes=['Sst'])
            for c0 in range(0, L, n):
                row = r0 + c0
                zt, kz = zb[kc % 2], 'gz%d' % (kc % 2)
                S.dma('sp', zt[:n, :], z[row:row + n, 0:1552], writes=[kz])
                TR(S, ps[0][:16, :n], zt[:n, O_GA:O_GA + 16], ident, [kz], ['ps0'])
                CP(S, 'act', gaT[:, :n], ps[0][:16, :n], ['ps0'], ['gaT'])
                MM(S, ps[1][:n, :256], gaT[:, :n], aup[:], True, False, ['gaT', 'aup'], ['ps1'])
                MM(S, ps[1][:n, :256], ones1[:, :n], abias[:], False, True, ['ones1', 'abias'], ['ps1'])
                ACT(S, la[:n, :], ps[1][:n, :256], AF.Exp, ['ps1'], ['la'], scale=-1.0)
                ACT(S, la[:n, :], la[:n, :], AF.Ln, ['la'], ['la'], bias=1.0)
                for h in range(4):
                    MM(S, ps[2][:64, h * 64:h * 64 + n], la[:n, h * 64:(h + 1) * 64], tri[:n, :n], True, True,
                       ['la', 'tri'], ['ps2'])
                MM(S, ps[3][:n, :256], slow[:n, :n], la[:n, :], True, True, ['la', 'slow'], ['ps3'])
                pc = ps[2][:64, :256].rearrange("p (h t) -> p h t", h=4)[:, :, :n]
                ACT(S, ecum[:, :, :n], pc, AF.Exp, ['ps2'], ['ecum'])
                ACT(S, encum[:, :, :n], pc, AF.Exp, ['ps2'], ['encum'], scale=-1.0)
                ACT(S, edl[:n, :], ps[3][:n, :256], AF.Exp, ['ps3'], ['edl'])
                for hh in range(8):
                    TR(S, ps[4][:64, hh * 64:hh * 64 + n], zt[:n, hh * 64:(hh + 1) * 64], ident, [kz], ['ps4'])
                pq = ps[4][:64, :].rearrange("p (h t) -> p h t", h=8)
                STT(S, qdT[:, :, :n], pq[:, 0:4, :n], 0.125, ecum[:, :, :n], ALU.mult, ALU.mult, ['ps4', 'ecum'], ['qdT'])
                TT(S, 'dve', kdT[:, :, :n], pq[:, 4:8, :n], encum[:, :, :n], ALU.mult, ['ps4', 'encum'], ['kdT'])
                TT(S, 'pool', kl[:n, :], zt[:n, O_GK:O_GK + 256], edl[:n, :], ALU.mult, [kz, 'edl'], ['kl'])
                for h in range(4):
                    MM(S, ps[5][:n, h * 64:h * 64 + n], kdT[:, h, :n], qdT[:, h, :n], True, True, ['kdT', 'qdT'], ['ps5'])
                pe_ = ps[5][:n, :256].rearrange("p (h t) -> p h t", h=4)[:, :, :n]
                TT(S, 'dve', attT[:n, :, :n], pe_, cmask[:n, :n].unsqueeze(1).to_broadcast([n, 4, n]), ALU.mult,
                   ['ps5', 'cmask'], ['attT'])
                for h in range(4):
                    vh = zt[:n, O_GV + h * 128:O_GV + (h + 1) * 128]
                    MM(S, ps[6][:n, h * 128:(h + 1) * 128], attT[:n, h, :n], vh, True, False, ['attT', kz], ['ps6'])
                    MM(S, ps[6][:n, h * 128:(h + 1) * 128], qdT[:, h, :n], Sst[:, h * 128:(h + 1) * 128], False, True,
                       ['qdT', 'Sst'], ['ps6'])
                for h in range(4):
                    vh = zt[:n, O_GV + h * 128:O_GV + (h + 1) * 128]
                    MM(S, ps[7][:64, h * 128:(h + 1) * 128], kl[:n, h * 64:(h + 1) * 64], vh, True, True, ['kl', kz], ['ps7'])
                for h in range(4):
                    STT(S, Sst[:, h * 128:(h + 1) * 128], Sst[:, h * 128:(h + 1) * 128], ecum[:, h, n - 1:n],
                        ps[7][:64, h * 128:(h + 1) * 128], ALU.mult, ALU.add, ['Sst', 'ecum', 'ps7'], ['Sst'])
                for h in range(4):
                    ACT(S, junk[:n, :], ps[6][:n, h * 128:(h + 1) * 128], AF.Square, ['ps6'], ['junk', 'ssq'],
                        accum_out=ssq[:n, h:h + 1])
                ACT(S, rstd[:n, :], ssq[:n, :], AF.Ln, ['ssq'], ['rstd'], scale=1.0 / 128, bias=1e-6)
                ACT(S, rstd[:n, :], rstd[:n, :], AF.Exp, ['rstd'], ['rstd'], scale=-0.5)
                ACT(S, sg[:n, :], zt[:n, O_GG:O_GG + 512], AF.Silu, [kz], ['sg'])
                TT(S, 'pool', sg[:n, :], sg[:n, :], normg[:n, :], ALU.mult, ['sg', 'normg'], ['sg'])
                bo, kb = bro[kc % 2], 'bro%d' % (kc % 2)
                for h in range(4):
                    STT(S, bo[:n, h * 128:(h + 1) * 128], ps[6][:n, h * 128:(h + 1) * 128], rstd[:n, h:h + 1],
                        sg[:n, h * 128:(h + 1) * 128], ALU.mult, ALU.mult, ['ps6', 'rstd', 'sg'], [kb])
                S.dma('sp', br[row:row + n, 0:512], bo[:n, :], reads=[kb], writes=[('br', 0)])
                kc += 1
            dst = O['glap'][l] if kind == 'p' else O['glas'][l, b]
            S.dma('sp', dst.rearrange("h d e -> d h e"), Sst[:].rearrange("d (h e) -> d h e", h=4), reads=['Sst'],
                  is_output=True)
def phase_rwkv_pre(C, l):
    S, nc, I, O, ps, ident, z = C.S, C.nc, C.I, C.O, C.ps, C.ident, C.z
    tiles = token_tiles()
    with Phase(C) as ph:
        def bc(name, src_row, width):
            t = ph.T(name, [128, width])
            S.dma('sp', t[:], src_row.partition_broadcast(128), writes=[name])
            return t
        mu_b = bc('mu_b', I['rwkv_mu'][l:l + 1, :], RWIN)
        kk_b = bc('kk_b', I['rwkv_k_k'][l:l + 1, :], 512)
        ka_b = bc('ka_b', I['rwkv_k_a'][l:l + 1, :], 512)
        rk_b = bc('rk_b', I['rwkv_r_k'][l:l + 1, :], 512)
        wup = ph.T('wup', [64, 512]); aup = ph.T('raup', [64, 512])
        w0 = ph.T('w0', [1, 512]); a0 = ph.T('a0', [1, 512]); ones1 = ph.T('rones', [1, 128])
        S.dma('sp', wup[:], I['rwkv_w_up'][l], writes=['wup'])
        S.dma('sp', aup[:], I['rwkv_a_up'][l], writes=['raup'])
        S.dma('sp', w0[:], I['rwkv_w0'][l:l + 1, :], writes=['w0'])
        S.dma('sp', a0[:], I['rwkv_a0'][l:l + 1, :], writes=['a0'])
        MEMSET(S, 'dve', ones1[:], 1.0, ['rones'])
        zc = ph.T('zc', [128, RWIN]); zp = ph.T('zp', [128, RWIN]); zs = ph.T('zs', [128, RWIN])
        twl = ph.T('twl', [128, 64]); lT = ph.T('lT', [64, 2, 128])
        dec = ph.T('dec', [128, 512]); av = ph.T('av', [128, 512]); kk = ph.T('kk', [128, 512])
        kk2 = ph.T('kk2', [128, 512]); s8 = ph.T('s8', [128, 8]); rn = ph.T('rn', [128, 8])
        nkk = ph.T('nkk', [128, 512]); t1 = ph.T('t1', [128, 512]); kmod = ph.T('kmod', [128, 512])
        rk3 = ph.T('rk3', [128, 3, 512], BF16); rkt = ph.T('rkt', [128, 512]); b8 = ph.T('b8', [128, 8])
        bv = ph.T('bv', [128, 512]); sgt = ph.T('sgt', [128, 512])
        fm = [ph.T('fm%d' % i, [64, 8, 128]) for i in range(3)]
        for ti, (r0, nr) in enumerate(tiles):
            S.dma('sp', zc[:nr, :], z[r0:r0 + nr, O_RZ:O_RZ + RWIN], writes=['zc'])
            if ti == 0:
                MEMSET(S, 'dve', zp[0:1, :], 0.0, ['zp'])
                S.dma('sp', zp[1:nr, :], z[0:nr - 1, O_RZ:O_RZ + RWIN], reads=['zp'], writes=['zp1'])
            elif nr == 128:
                S.dma('sp', zp[:nr, :], z[r0 - 1:r0 + nr - 1, O_RZ:O_RZ + RWIN], writes=['zp', 'zp1'])
            else:
                S.dma('sp', zp[:nr, :], z[r0 - 1:r0 + nr - 1, O_RZ:O_RZ + RWIN], writes=['zp'])
                kws = ['zp']
                for b in range(NS_B):
                    S.dma('sp', zp[b * NS_T:b * NS_T + 1, :], I['shift'][l, b:b + 1, :], reads=kws, writes=['zp1'])
            rd = ['zp', 'zp1', 'zc']
            TT(S, 'dve', zs[:nr, :], zp[:nr, :], zc[:nr, :], ALU.subtract, rd, ['zs'])
            TT(S, 'pool', zs[:nr, :], zs[:nr, :], mu_b[:nr, :], ALU.mult, ['zs', 'mu_b'], ['zs'])
            TT(S, 'dve', zs[:nr, :], zs[:nr, :], zc[:nr, :], ALU.add, ['zs', 'zc'], ['zs'])
            r_, k_, v_ = zs[:nr, 0:512], zs[:nr, 512:1024], zs[:nr, 1024:1536]
            wl_, al_, g_ = zs[:nr, 1536:1600], zs[:nr, 1600:1664], zs[:nr, 1664:2176]
            ACT(S, twl[:nr, :], wl_, AF.Tanh, ['zs'], ['twl'])
            TR(S, ps[0][:64, 0:nr], twl[:nr, :], ident, ['twl'], ['ps0'])
            TR(S, ps[0][:64, 128:128 + nr], al_, ident, ['zs'], ['ps0'])
            CP(S, 'dve', lT[:, :, :nr], ps[0][:64, :256].rearrange("p (a t) -> p a t", a=2)[:, :, :nr], ['ps0'], ['lT'])
            MM(S, ps[1][:nr, :], lT[:, 0, :nr], wup[:], True, False, ['lT', 'wup'], ['ps1'])
            MM(S, ps[1][:nr, :], ones1[:, :nr], w0[:], False, True, ['rones', 'w0'], ['ps1'])
            MM(S, ps[2][:nr, :], lT[:, 1, :nr], aup[:], True, False, ['lT', 'raup'], ['ps2'])
            MM(S, ps[2][:nr, :], ones1[:, :nr], a0[:], False, True, ['rones', 'a0'], ['ps2'])
            ACT(S, dec[:nr, :], ps[1][:nr, :], AF.Sigmoid, ['ps1'], ['dec'])
            ACT(S, av[:nr, :], ps[2][:nr, :], AF.Sigmoid, ['ps2'], ['av'])
            ACT(S, dec[:nr, :], dec[:nr, :], AF.Exp, ['dec'], ['dec'], scale=-float(np.exp(-0.5)))
            ACT(S, sgt[:nr, :], g_, AF.Silu, ['zs'], ['sgt'])
            S.dma('act', C.sg_scr[r0:r0 + nr, :], sgt[:nr, :], reads=['sgt'], writes=[('sgs', ti)])
            TT(S, 'dve', kk[:nr, :], k_, kk_b[:nr, :], ALU.mult, ['zs', 'kk_b'], ['kk'])
            TT(S, 'pool', kk2[:nr, :], kk[:nr, :], kk[:nr, :], ALU.mult, ['kk'], ['kk2'])
            RED(S, s8[:nr, :], kk2[:nr, :].rearrange("p (h j) -> p h j", h=8), ALU.add, ['kk2'], ['s8'])
            ACT(S, rn[:nr, :], s8[:nr, :], AF.Ln, ['s8'], ['rn'], bias=1e-6)
            ACT(S, rn[:nr, :], rn[:nr, :], AF.Exp, ['rn'], ['rn'], scale=-0.5)
            v3 = lambda t: t.rearrange("p (h j) -> p h j", h=8)
            STT(S, v3(nkk[:nr, :]), v3(kk[:nr, :]), -1.0, rn[:nr, :].unsqueeze(2).to_broadcast([nr, 8, 64]),
                ALU.mult, ALU.mult, ['kk', 'rn'], ['nkk'])
            STT(S, t1[:nr, :], av[:nr, :], -1.0, ka_b[:nr, :], ALU.add, ALU.mult, ['av', 'ka_b'], ['t1'])
            STT(S, kmod[:nr, :], t1[:nr, :], 1.0, k_, ALU.add, ALU.mult, ['t1', 'zs'], ['kmod'])
            STT(S, rk3[:nr, 1, :], nkk[:nr, :], -1.0, av[:nr, :], ALU.mult, ALU.mult, ['nkk', 'av'], [('rk3', 1)])
            CP(S, 'act', rk3[:nr, 0, :], kmod[:nr, :], ['kmod'], [('rk3', 0)])
            CP(S, 'act', rk3[:nr, 2, :], v_, ['zs'], [('rk3', 2)])
            S.dma('act', C.rk_scr[r0:r0 + nr, :, :], rk3[:nr, :, :], reads=[('rk3', 0), ('rk3', 1), ('rk3', 2)],
                  writes=[('rks', ti)])
            TT(S, 'pool', rkt[:nr, :], r_, kmod[:nr, :], ALU.mult, ['zs', 'kmod'], ['rkt'])
            TT(S, 'dve', rkt[:nr, :], rkt[:nr, :], rk_b[:nr, :], ALU.mult, ['rkt', 'rk_b'], ['rkt'])
            RED(S, b8[:nr, :], v3(rkt[:nr, :]), ALU.add, ['rkt'], ['b8'])
            TT(S, 'dve', v3(bv[:nr, :]), v3(v_), b8[:nr, :].unsqueeze(2).to_broadcast([nr, 8, 64]), ALU.mult,
               ['zs', 'b8'], ['bv'])
            S.dma('act', C.bv_scr[r0:r0 + nr, :], bv[:nr, :], reads=['bv'], writes=[('bvs', ti)])
            for qi, src in enumerate((nkk[:nr, :], r_, dec[:nr, :])):
                rkey = ['nkk', 'zs', 'dec'][qi]
                for hb in range(2):
                    pb, kp = ps[3 + hb], 'ps%d' % (3 + hb)
                    for h4 in range(4):
                        h = hb * 4 + h4
                        TR(S, pb[:64, h4 * 128:h4 * 128 + nr], src[:, h * 64:(h + 1) * 64], ident, [rkey], [kp])
                    CP(S, 'dve' if hb == 0 else 'act', fm[qi][:, hb * 4:hb * 4 + 4, :nr],
                       pb[:64, :].rearrange("p (h t) -> p h t", h=4)[:, :, :nr], [kp], [('fm', qi)])
                S.dma('sp', C.fm_scr[qi, :, :, r0:r0 + nr], fm[qi][:, :, :nr], reads=[('fm', qi)], writes=[('fms', ti)])


def phase_rwkv_scan(C, l):
    S, nc, I, O, ps, ident = C.S, C.nc, C.I, C.O, C.ps, C.ident
    SUB = 8
    with Phase(C) as ph:
        maskbd = ph.T('maskbd', [8, 512]); maskbf = ph.T('maskbf', [8, 512], BF16)
        MEMSET(S, 'pool', maskbd[:], 1.0, ['maskbd'])
        ASEL(S, maskbd[:], [[1, 512]], 0, -64, ['maskbd'])
        ASEL(S, maskbd[:], [[-1, 512]], 63, 64, ['maskbd'])
        CP(S, 'dve', maskbf[:], maskbd[:], ['maskbd'], ['maskbf'])
        ST = [ph.T('ST%d' % i, [64, 512]) for i in range(2)]
        tmp = ph.T('sttmp', [64, 512])
        STb = [ph.T('STb%d' % i, [64, 512], BF16) for i in range(2)]
        fmb0 = [ph.T('fmb0_%d' % i, [64, 8, 128], BF16) for i in range(2)]
        fmb1 = [ph.T('fmb1_%d' % i, [64, 8, 128], BF16) for i in range(2)]
        fmt = [[ph.T('fmt%d_%d' % (q, i), [64, 8, 128]) for i in range(2)] for q in range(3)]
        kmr = [ph.T('kmr%d' % i, [8, SUB, 64], BF16) for i in range(2)]
        kar = [ph.T('kar%d' % i, [8, SUB, 64], BF16) for i in range(2)]
        vb = [ph.T('vb%d' % i, [8, SUB, 512], BF16) for i in range(2)]
        sab = [ph.T('sab%d' % i, [8, 512], BF16) for i in range(2)]
        yT = ph.T('yT', [128, 4, 128]); ytm = ph.T('ytm', [128, 512])
        sin = ph.T('sin', [64, 8, 64]); sout = ph.T('sout', [64, 8, 64])
        seqs = [('p', 0, T, None)] + [('s', T + b * NS_T, NS_T, b) for b in range(NS_B)]
        cur = 0
        gt = 0
        gs = 0
        for (kind, r0, L, b) in seqs:
            if kind == 'p':
                MEMSET(S, 'dve', ST[cur][:], 0.0, ['ST%d' % cur])
                MEMSET(S, 'dve', STb[cur][:], 0.0, ['STb%d' % cur])
            else:
                S.dma('sp', sin[:], I['rw_st'][l, b].rearrange("h i j -> i h j"), writes=['sin'])
                for h in range(8):
                    TR(S, ps[0][:64, h * 64:(h + 1) * 64], sin[:, h, :], ident, ['sin'], ['ps0'])
                CP(S, 'dve', ST[cur][:], ps[0][:64, :], ['ps0'], ['ST%d' % cur])
                CP(S, 'dve', STb[cur][:], ps[0][:64, :], ['ps0'], ['STb%d' % cur])
            tiles_ = [(t0, min(128, L - t0)) for t0 in range(0, L, 128)]
            subs = []
            for k, (t0, nt) in enumerate(tiles_):
                for s0 in range(0, nt, SUB):
                    subs.append((k, t0, s0, min(SUB, nt - s0)))
            steps = []
            for si, (k, t0, s0, ns) in enumerate(subs):
                for tt_ in range(ns):
                    steps.append((si, k, t0, s0, ns, tt_))

            def load_tile(k):
                t0, nt = tiles_[k]
                fb = (gt + k) % 2
                for q in range(3):
                    S.dma('sp' if q != 1 else 'act', fmt[q][fb][:, :, :nt], C.fm_scr[q, :, :, r0 + t0:r0 + t0 + nt],
                          writes=[('fmt', q, fb)])
                CP(S, 'act', fmb0[fb][:, :, :nt], fmt[0][fb][:, :, :nt], [('fmt', 0, fb)], [('fmb0', fb)])
                CP(S, 'act', fmb1[fb][:, :, :nt], fmt[1][fb][:, :, :nt], [('fmt', 1, fb)], [('fmb1', fb)])

            def load_sub(si):
                k, t0, s0, ns = subs[si]
                sb = (gs + si) % 2
                rows = slice(r0 + t0 + s0, r0 + t0 + s0 + ns)
                S.dma('sp', kmr[sb][:, :ns, :], C.rk_scr[rows, 0, :].rearrange("t (h j) -> h t j", h=8),
                      writes=[('kmr', sb)])
                S.dma('act', kar[sb][:, :ns, :], C.rk_scr[rows, 1, :].rearrange("t (h j) -> h t j", h=8),
                      writes=[('kar', sb)])
                S.dma('sp', vb[sb][:, :ns, :], C.rk_scr[rows, 2, :].partition_broadcast(8), writes=[('vb', sb)])
                TT(S, 'pool', vb[sb][:, :ns, :], vb[sb][:, :ns, :],
                   maskbf[:].unsqueeze(1).to_broadcast([8, ns, 512]), ALU.mult, [('vb', sb), 'maskbf'], [('vb', sb)])

            def frontB(i):
                si, k, t0, s0, ns, tt_ = steps[i]
                sb = (gs + si) % 2
                MM(S, ps[2][:64, :], kmr[sb][:, tt_, :], vb[sb][:, tt_, :], True, False, [('kmr', sb), ('vb', sb)], ['ps2'])

            def frontA(i, cur_):
                si, k, t0, s0, ns, tt_ = steps[i]
                fb = (gt + k) % 2
                t = s0 + tt_
                MM(S, ps[1][:8, :], fmb0[fb][:, :, t], STb[cur_][:], True, True, [('fmb0', fb), 'STb%d' % cur_], ['ps1'])

            load_tile(0)
            load_sub(0)
            frontB(0)
            frontA(0, cur)
            slot = 0
            slot_t0 = 0
            for i, (si, k, t0, s0, ns, tt_) in enumerate(steps):
                sb = (gs + si) % 2
                fb = (gt + k) % 2
                t = s0 + tt_
                nt = tiles_[k][1]
                if tt_ == 0 and si + 1 < len(subs):
                    if subs[si + 1][0] != k:
                        load_tile(subs[si + 1][0])
                    load_sub(si + 1)
                nxt = 1 - cur
                kc_, kn_ = 'ST%d' % cur, 'ST%d' % nxt
                sa, ksa = sab[i % 2], 'sab%d' % (i % 2)
                TT(S, 'dve', sa[:], ps[1][:8, :], maskbd[:], ALU.mult, ['ps1', 'maskbd'], [ksa])
                MM(S, ps[2][:64, :], kar[sb][:, tt_, :], sa[:], False, True, [('kar', sb), ksa], ['ps2'])
                TT(S, 'pool', tmp[:].rearrange("p (h i) -> p h i", h=8), ST[cur][:].rearrange("p (h i) -> p h i", h=8),
                   fmt[2][fb][:, :, t].unsqueeze(2).to_broadcast([64, 8, 64]), ALU.mult, [kc_, ('fmt', 2, fb)], ['sttmp'])
                TT(S, 'dve', STb[nxt][:], tmp[:], ps[2][:64, :], ALU.add, ['sttmp', 'ps2'], ['STb%d' % nxt])
                if i + 1 < len(steps):
                    frontA(i + 1, nxt)
                TT(S, 'dve', ST[nxt][:], tmp[:], ps[2][:64, :], ALU.add, ['sttmp', 'ps2'], [kn_])
                if i + 1 < len(steps):
                    frontB(i + 1)
                for c in range(4):
                    MM(S, ps[3][:, slot * 32 + c * 8:slot * 32 + c * 8 + 8], STb[nxt][:, c * 128:(c + 1) * 128],
                       fmb1[fb][:, :, t], True, True, ['STb%d' % nxt, ('fmb1', fb)], ['ps3'])
                cur = nxt
                slot += 1
                if slot == 16 or t == nt - 1:
                    for hf in range(2):
                        src = ps[3][hf * 64:(hf + 1) * 64, :].rearrange("p (s x) -> p s x", x=32)[:, :slot, hf:hf + 31:10]
                        dst = yT[hf * 64:(hf + 1) * 64, :, slot_t0:slot_t0 + slot].rearrange("p c t -> p t c")
                        CP(S, 'act', dst, src, ['ps3'], [('yT', hf)])
                    slot_t0 += slot
                    slot = 0
                if t == nt - 1:
                    for c in range(4):
                        TR(S, ps[4][:nt, c * 128:(c + 1) * 128], yT[:, c, :nt], ident, [('yT', 0), ('yT', 1)], ['ps4'])
                    CP(S, 'act', ytm[:nt, :], ps[4][:nt, :], ['ps4'], ['ytm'])
                    S.dma('sp', C.y_scr[r0 + t0:r0 + t0 + nt, :], ytm[:nt, :], reads=['ytm'], writes=[('ys', gt + k)])
                    slot_t0 = 0
            gt += len(tiles_)
            gs += len(subs)
            for h in range(8):
                TR(S, ps[0][:64, h * 64:(h + 1) * 64], ST[cur][:, h * 64:(h + 1) * 64], ident, ['ST%d' % cur], ['ps0'])
            CP(S, 'dve', sout[:].rearrange("p h j -> p (h j)"), ps[0][:64, :], ['ps0'], ['sout'])
            dst = O['rwp'][l] if kind == 'p' else O['rws'][l, b]
            S.dma('sp', dst.rearrange("h i j -> i h j"), sout[:], reads=['sout'], is_output=True)


def phase_rwkv_post(C, l):
    S, nc, I, O, ps, ident, br = C.S, C.nc, C.I, C.O, C.ps, C.ident, C.br
    tiles = token_tiles()
    with Phase(C) as ph:
        lnw = ph.T('lnw', [128, 512]); lnb = ph.T('lnb', [128, 512])
        S.dma('sp', lnw[:], I['rwkv_ln_w'][l:l + 1, :].partition_broadcast(128), writes=['lnw'])
        S.dma('sp', lnb[:], I['rwkv_ln_b'][l:l + 1, :].partition_broadcast(128), writes=['lnb'])
        v3 = lambda t: t.rearrange("p (h j) -> p h j", h=8)
        for ti, (r0, nr) in enumerate(tiles):
            i2 = ti % 2
            y = ph.T('py', [128, 512]) if ti < 2 else None
            if ti < 2:
                C._rwp = getattr(C, '_rwp', {})
                C._rwp[i2] = dict(y=y, bvt=ph.T('pbv', [128, 512]), sgt=ph.T('psg', [128, 512]),
                                  yc=ph.T('pyc', [128, 512]), sq=ph.T('psq', [128, 512]),
                                  m8=ph.T('pm8', [128, 8]), v8=ph.T('pv8', [128, 8]), o=ph.T('po', [128, 512]))
            d = C._rwp[i2]
            k = lambda s: '%s%d' % (s, i2)
            S.dma('sp', d['y'][:nr, :], C.y_scr[r0:r0 + nr, :], writes=[k('y')])
            S.dma('act', d['bvt'][:nr, :], C.bv_scr[r0:r0 + nr, :], writes=[k('bv')])
            S.dma('act', d['sgt'][:nr, :], C.sg_scr[r0:r0 + nr, :], writes=[k('sg')])
            RED(S, d['m8'][:nr, :], v3(d['y'][:nr, :]), ALU.add, [k('y')], [k('m8')])
            STT(S, v3(d['yc'][:nr, :]), d['m8'][:nr, :].unsqueeze(2).to_broadcast([nr, 8, 64]), -1.0 / 64,
                v3(d['y'][:nr, :]), ALU.mult, ALU.add, [k('m8'), k('y')], [k('yc')])
            TT(S, 'pool', d['sq'][:nr, :], d['yc'][:nr, :], d['yc'][:nr, :], ALU.mult, [k('yc')], [k('sq')])
            RED(S, d['v8'][:nr, :], v3(d['sq'][:nr, :]), ALU.add, [k('sq')], [k('v8')])
            ACT(S, d['v8'][:nr, :], d['v8'][:nr, :], AF.Ln, [k('v8')], [k('v8')], scale=1.0 / 64, bias=64e-5)
            ACT(S, d['v8'][:nr, :], d['v8'][:nr, :], AF.Exp, [k('v8')], [k('v8')], scale=-0.5)
            TT(S, 'dve', v3(d['yc'][:nr, :]), v3(d['yc'][:nr, :]), d['v8'][:nr, :].unsqueeze(2).to_broadcast([nr, 8, 64]),
               ALU.mult, [k('yc'), k('v8')], [k('yc')])
            TT(S, 'pool', d['yc'][:nr, :], d['yc'][:nr, :], lnw[:nr, :], ALU.mult, [k('yc'), 'lnw'], [k('yc')])
            TT(S, 'dve', d['yc'][:nr, :], d['yc'][:nr, :], lnb[:nr, :], ALU.add, [k('yc'), 'lnb'], [k('yc')])
            TT(S, 'pool', d['yc'][:nr, :], d['yc'][:nr, :], d['bvt'][:nr, :], ALU.add, [k('yc'), k('bv')], [k('yc')])
            TT(S, 'dve', d['o'][:nr, :], d['yc'][:nr, :], d['sgt'][:nr, :], ALU.mult, [k('yc'), k('sg')], [k('o')])
            S.dma('sp', br[r0:r0 + nr, 512:1024], d['o'][:nr, :], reads=[k('o')], writes=[('br', 1, ti)])


def phase_merge(C, l, xsrc, xdst):
    S, nc, I, O, ps, ident, br, z = C.S, C.nc, C.I, C.O, C.ps, C.ident, C.br, C.z
    tiles = token_tiles()
    alpha = float((2 * DEPTH) ** 0.25)
    with Phase(C) as ph:
        wbr = ph.T('wbr', [128, 12, 1024], BF16); wout = ph.T('wout', [128, 8, 1024], BF16)
        S.dma('pool', wbr[:], I['w_br'][l].rearrange("m (c p) n -> p (m c) n", p=128), writes=['wbr'])
        S.dma('pool', wout[:], I['w_out'][l].rearrange("(c p) n -> p c n", p=128), writes=['wout'])
        lng = ph.T('lng', [128, D]); lnb = ph.T('lnbb', [128, D])
        S.dma('sp', lng[:], I['ln_g'][l:l + 1, :].partition_broadcast(128), writes=['lng'])
        S.dma('sp', lnb[:], I['ln_b'][l:l + 1, :].partition_broadcast(128), writes=['lnbb'])
        bt = ph.T('mbt', [128, 1536]); gt = ph.T('mgt', [128, 3072]); xt = ph.T('mxt', [128, D])
        brT = ph.T('brT', [128, 12, 128], BF16); mg = ph.T('mmg', [128, D]); tmp = ph.T('mtmp', [128, 512])
        mT = ph.T('mT', [128, 8, 128], BF16); res = ph.T('mres', [128, D]); st6 = ph.T('mst6', [128, 2, 6])
        mv = ph.T('mmv', [128, 2]); rs = ph.T('mrs', [128, 1]); xo = ph.T('mxo', [128, D])
        for ti, (r0, nr) in enumerate(tiles):
            S.dma('sp', bt[:nr, :], br[r0:r0 + nr, :], writes=['mbt'])
            S.dma('act', gt[:nr, :], z[r0:r0 + nr, O_MG:O_MG + 3072], writes=['mgt'])
            S.dma('sp', xt[:nr, :], xsrc[r0:r0 + nr, :], writes=['mxt'])
            ACT(S, gt[:nr, :], gt[:nr, :], AF.Sigmoid, ['mgt'], ['mgt'])
            for q in range(3):
                pb, kp = ps[q % 2], 'ps%d' % (q % 2)
                for c4 in range(4):
                    TR(S, pb[:, c4 * 128:c4 * 128 + nr], bt[:nr, (q * 4 + c4) * 128:(q * 4 + c4 + 1) * 128], ident,
                       ['mbt'], [kp])
                CP(S, 'dve' if q % 2 == 0 else 'act', brT[:, q * 4:q * 4 + 4, :nr],
                   pb[:].rearrange("p (c t) -> p c t", c=4)[:, :, :nr], [kp], [('brT', q)])
            for hf in range(2):
                for m in range(3):
                    pi = 2 + (hf * 3 + m) % 3
                    pb, kp = ps[pi], 'ps%d' % pi
                    for c in range(4):
                        MM(S, pb[:nr, :], brT[:, m * 4 + c, :nr], wbr[:, m * 4 + c, hf * 512:(hf + 1) * 512], c == 0, c == 3,
                           [('brT', m), 'wbr'], [kp])
                    gsl = gt[:nr, m * 1024 + hf * 512:m * 1024 + (hf + 1) * 512]
                    if m == 0:
                        TT(S, 'dve', mg[:nr, hf * 512:(hf + 1) * 512], pb[:nr, :], gsl, ALU.mult, [kp, 'mgt'], [('mmg', hf)])
                    else:
                        TT(S, 'dve', tmp[:nr, :], pb[:nr, :], gsl, ALU.mult, [kp, 'mgt'], ['mtmp'])
                        TT(S, 'pool', mg[:nr, hf * 512:(hf + 1) * 512], mg[:nr, hf * 512:(hf + 1) * 512], tmp[:nr, :], ALU.add,
                           [('mmg', hf), 'mtmp'], [('mmg', hf)])
            for hf in range(2):
                pb, kp = ps[5 + hf], 'ps%d' % (5 + hf)
                for c4 in range(4):
                    c = hf * 4 + c4
                    TR(S, pb[:, c4 * 128:c4 * 128 + nr], mg[:nr, c * 128:(c + 1) * 128], ident, [('mmg', hf)], [kp])
                CP(S, 'dve' if hf == 0 else 'act', mT[:, hf * 4:hf * 4 + 4, :nr],
                   pb[:].rearrange("p (c t) -> p c t", c=4)[:, :, :nr], [kp], [('mT', hf)])
            for hf in range(2):
                pb, kp = ps[2 + hf], 'ps%d' % (2 + hf)
                for c in range(8):
                    MM(S, pb[:nr, :], mT[:, c, :nr], wout[:, c, hf * 512:(hf + 1) * 512], c == 0, c == 7,
                       [('mT', 0), ('mT', 1), 'wout'], [kp])
                STT(S, res[:nr, hf * 512:(hf + 1) * 512], xt[:nr, hf * 512:(hf + 1) * 512], alpha, pb[:nr, :], ALU.mult, ALU.add,
                    ['mxt', kp], [('mres', hf)])
                S.op('dve', lambda e: e.bn_stats(out=st6[:nr, hf, :], in_=res[:nr, hf * 512:(hf + 1) * 512]),
                     [('mres', hf)], [('mst6', hf)])
            S.op('dve', lambda e: e.bn_aggr(out=mv[:nr, :], in_=st6[:nr, :, :].rearrange("p a s -> p (a s)")),
                 [('mst6', 0), ('mst6', 1)], ['mmv'])
            ACT(S, rs[:nr, :], mv[:nr, 1:2], AF.Ln, ['mmv'], ['mrs'], bias=1e-5)
            ACT(S, rs[:nr, :], rs[:nr, :], AF.Exp, ['mrs'], ['mrs'], scale=-0.5)
            TS(S, 'dve', xo[:nr, :], res[:nr, :], mv[:nr, 0:1], rs[:nr, 0:1], ALU.subtract, ALU.mult,
               [('mres', 0), ('mres', 1), 'mmv', 'mrs'], ['mxo'])
            TT(S, 'pool', xo[:nr, :], xo[:nr, :], lng[:nr, :], ALU.mult, ['mxo', 'lng'], ['mxo'])
            TT(S, 'dve', xo[:nr, :], xo[:nr, :], lnb[:nr, :], ALU.add, ['mxo', 'lnbb'], ['mxo'])
            S.dma('sp', xdst[r0:r0 + nr, :], xo[:nr, :], reads=['mxo'], writes=[('xd', ti)],
                  is_output=(l == DEPTH - 1))
SLOPES = [2.0 ** (-(h + 1)) for h in range(8)]
NEG = -1.0e30


def IOTA(S, ap, pattern, base, cm, w):
    return S.op('pool', lambda e: e.iota(ap, pattern=pattern, base=base, channel_multiplier=cm,
                                        allow_small_or_imprecise_dtypes=True), (), w)


def phase_nsa_prompt(C, l):
    S, nc, I, O, ps, ident, z, br = C.S, C.nc, C.I, C.O, C.ps, C.ident, C.z, C.br
    NTI = T // 128
    with Phase(C) as ph:
        pool4 = ph.T('pool4', [128, 4]); pband = ph.T('pband', [128, 124])
        MEMSET(S, 'pool', pool4[:], 1.0 / 32, ['pool4'])
        ASEL(S, pool4[:], [[-32, 4]], 0, 1, ['pool4'])
        ASEL(S, pool4[:], [[32, 4]], 31, -1, ['pool4'])
        MEMSET(S, 'pool', pband[:], 1.0 / 32, ['pband'])
        ASEL(S, pband[:], [[-32, 124]], 1920, 1, ['pband'])
        ASEL(S, pband[:], [[32, 124]], -1889, -1, ['pband'])
        d0 = ph.T('d0', [128, 64]); B0 = ph.T('B0', [128, 8, 64])
        IOTA(S, d0[:], [[-32, 64]], -31, 1, ['d0'])
        for hd in range(8):
            TS(S, 'dve', B0[:, hd, :], d0[:], -SLOPES[hd], None, ALU.mult, ALU.bypass, ['d0'], ['B0'])
        penband = ph.T('penband', [128, 124]); m01band = ph.T('m01band', [128, 124])
        MEMSET(S, 'pool', penband[:], 0.0, ['penband'])
        ASEL(S, penband[:], [[-32, 124]], 1889, 1, ['penband'], fill=NEG)
        MEMSET(S, 'pool', m01band[:], 1.0, ['m01band'])
        ASEL(S, m01band[:], [[-32, 124]], 1889, 1, ['m01band'], fill=0.0)
        posb = ph.T('posb', [128, T])
        IOTA(S, posb[:], [[1, T]], 0, 0, ['posb'])
        causalpen = ph.T('causalpen', [128, 128])
        MEMSET(S, 'pool', causalpen[:], 0.0, ['causalpen'])
        ASEL(S, causalpen[:], [[-1, 128]], 0, 1, ['causalpen'], fill=NEG)
        penW = ph.T('penW', [128, 640])
        MEMSET(S, 'pool', penW[:], 0.0, ['penW'])
        ASEL(S, penW[:], [[1, 640]], 0, -1, ['penW'], fill=NEG)
        ASEL(S, penW[:], [[-1, 640]], 512, 1, ['penW'], fill=NEG)
        adj = ph.T('adj', [128, 62])
        MEMSET(S, 'pool', adj[:], 0.0, ['adj'])
        for hf in range(2):
            sl = adj[hf * 64:(hf + 1) * 64, :]
            ASEL(S, sl, [[-1, 62]], 30 + hf, 0, ['adj'], fill=-1.0e9)
            ASEL(S, sl, [[1, 62]], -(30 + hf), 0, ['adj'], op=ALU.not_equal, fill=1.0e9)
        KT = ph.T('KT', [64, 4, T], BF16)
        V = ph.T('Vsw', [128, NTI, 2, 128], BF16)
        kcT = ph.T('kcT', [64, 2, 64], BF16); vc = ph.T('vc', [64, 128], BF16)
        kvt = [ph.T('kvt%d' % i, [128, 768]) for i in range(2)]
        for ti in range(NTI):
            kt_, kk_ = kvt[ti % 2], 'kvt%d' % (ti % 2)
            S.dma('sp', kt_[:], z[ti * 128:(ti + 1) * 128, O_NKV:O_NKV + 768], writes=[kk_])
            for kv in range(2):
                MM(S, ps[6][:64, kv * 64 + 4 * ti:kv * 64 + 4 * ti + 4], kt_[:, kv * 64:(kv + 1) * 64], pool4[:], True, True,
                   [kk_, 'pool4'], ['ps6'])
            MM(S, ps[7][:64, :128], pband[:, 60 - 4 * ti:124 - 4 * ti], kt_[:, 128:256], ti == 0, ti == NTI - 1,
               [kk_, 'pband'], ['ps7'])
            pb, kp = ps[ti % 2], 'ps%d' % (ti % 2)
            for a in range(4):
                col = (256 if a < 2 else 512) + (a % 2) * 64
                TR(S, pb[:64, a * 128:(a + 1) * 128], kt_[:, col:col + 64], ident, [kk_], [kp])
            CP(S, 'dve', KT[:, :, ti * 128:(ti + 1) * 128], pb[:64, :].rearrange("p (a t) -> p a t", a=4), [kp], ['KT'])
            CP(S, 'act', V[:, ti, 0, :], kt_[:, 384:512], [kk_], ['Vsw'])
            CP(S, 'pool', V[:, ti, 1, :], kt_[:, 640:768], [kk_], ['Vsw'])
        CP(S, 'dve', kcT[:].rearrange("p k b -> p (k b)"), ps[6][:64, :128], ['ps6'], ['kcT'])
        CP(S, 'act', vc[:], ps[7][:64, :128], ['ps7'], ['vc'])
        qin = [ph.T('qin%d' % i, [128, 512]) for i in range(2)]
        gin = [ph.T('gin%d' % i, [128, 536]) for i in range(2)]
        qT = ph.T('qT', [64, 8, 128], BF16)
        gates = ph.T('gates', [128, 24]); sgate = ph.T('sgate', [128, 512])
        s1 = ph.T('s1', [128, 8, 64]); mx8 = ph.T('mx8', [128, 8]); den8 = ph.T('den8', [128, 8])
        t8 = ph.T('t8', [128, 8, 32]); imp = ph.T('imp', [128, 2, 32]); sc2 = ph.T('sc2', [128, 2, 32])
        m8a = ph.T('m8a', [128, 8]); m8b = ph.T('m8b', [128, 8]); sel01 = ph.T('sel01', [128, 2, 32])
        pcT = ph.T('pcT', [64, 8, 128], BF16)
        pens = [ph.T('pens%d' % i, [128, T]) for i in range(2)]
        ssb = [ph.T('ssb%d' % i, [128, T]) for i in range(2)]; pex = [ph.T('pex%d' % i, [128, T]) for i in range(2)]
        pT = [ph.T('pT%d' % i, [128, 16, 128], BF16) for i in range(2)]
        mx1 = [ph.T('mx1_%d' % i, [128, 1]) for i in range(2)]; dens = ph.T('dens', [128, 8]); denw = ph.T('denw', [128, 8])
        g2 = ph.T('g2', [128, 8]); acc = ph.T('nacc', [128, 512]); tmpo = ph.T('ntmpo', [128, 512])
        v8 = lambda t: t.rearrange("p (h d) -> p h d", h=8)

        def att_p1(bs, hd, kt0, kt1, aidx, vidx, pen_ap, pen_keys, obank, okey, dent):
            kv = hd // 4
            nkt = kt1 - kt0 + 1
            nk = nkt * 128
            k0 = kt0 * 128
            ks, kx, km = 'ssb%d' % bs, 'pex%d' % bs, 'mx1_%d' % bs
            for ci, c in enumerate(range(0, nk, 512)):
                cw = min(512, nk - c)
                pb, kp = ps[ci % 2], 'ps%d' % (ci % 2)
                MM(S, pb[:, :cw], qT[:, hd, :], KT[:, aidx * 2 + kv, k0 + c:k0 + c + cw], True, True, ['qT', 'KT'], [kp])
                STT(S, ssb[bs][:, c:c + cw], posb[:, k0 + c:k0 + c + cw], SLOPES[hd], pb[:, :cw], ALU.mult, ALU.add,
                    ['posb', kp], [ks])
            TT(S, 'pool', ssb[bs][:, :nk], ssb[bs][:, :nk], pen_ap, ALU.add, [ks] + pen_keys, [ks])
            RED(S, mx1[bs][:], ssb[bs][:, :nk], ALU.max, [ks], [km])
            TS(S, 'dve', mx1[bs][:], mx1[bs][:], -1.0, None, ALU.mult, ALU.bypass, [km], [km])
            ACT(S, pex[bs][:, :nk], ssb[bs][:, :nk], AF.Exp, [ks, km], [kx, dent[1]], bias=mx1[bs][:, 0:1],
                accum_out=dent[0][:, hd:hd + 1])

        def att_p2(bs, hd, kt0, kt1, aidx, vidx, pen_ap, pen_keys, obank, okey, dent):
            kv = hd // 4
            nkt = kt1 - kt0 + 1
            kx = 'pex%d' % bs
            for g0 in range(0, nkt, 4):
                gn = min(4, nkt - g0)
                pi = 2 + (g0 // 4) % 2
                pb, kp = ps[pi], 'ps%d' % pi
                for j in range(gn):
                    TR(S, pb[:, j * 128:(j + 1) * 128], pex[bs][:, (g0 + j) * 128:(g0 + j + 1) * 128], ident, [kx], [kp])
                CP(S, 'act' if (g0 // 4) % 2 == 0 else 'dve', pT[bs][:, g0:g0 + gn, :],
                   pb[:, :gn * 128].rearrange("p (a t) -> p a t", a=gn), [kp], [('pT', bs, g0)])
            for j in range(nkt):
                MM(S, obank[:, hd * 64:(hd + 1) * 64], pT[bs][:, j, :], V[:, kt0 + j, vidx, kv * 64:(kv + 1) * 64],
                   j == 0, j == nkt - 1, [('pT', bs, (j // 4) * 4), 'Vsw'], [okey])

        for ti in range(NTI):
            qb, kq = qin[ti % 2], 'qin%d' % (ti % 2)
            gb, kg = gin[ti % 2], 'gin%d' % (ti % 2)
            r0 = ti * 128
            S.dma('sp', qb[:], z[r0:r0 + 128, O_NQ:O_NQ + 512], writes=[kq])
            S.dma('act', gb[:], z[r0:r0 + 128, O_NBG:O_NBG + 536], writes=[kg])
            for hb in range(2):
                pb, kp = ps[2 + hb], 'ps%d' % (2 + hb)
                for h4 in range(4):
                    hd = hb * 4 + h4
                    TR(S, pb[:64, h4 * 128:(h4 + 1) * 128], qb[:, hd * 64:(hd + 1) * 64], ident, [kq], [kp])
                TS(S, 'dve', qT[:, hb * 4:hb * 4 + 4, :], pb[:64, :].rearrange("p (h t) -> p h t", h=4), 0.125, None,
                   ALU.mult, ALU.bypass, [kp], ['qT'])
            ACT(S, gates[:], gb[:, 0:24], AF.Sigmoid, [kg], ['gates'])
            ACT(S, sgate[:], gb[:, 24:536], AF.Silu, [kg], ['sgate'])
            g3 = gates[:].rearrange("p (h j) -> p h j", j=3)
            for hd in range(8):
                MM(S, ps[6][:, hd * 64:(hd + 1) * 64], qT[:, hd, :], kcT[:, hd // 4, :], True, True, ['qT', 'kcT'], ['ps6'])
            TT(S, 'dve', s1[:], ps[6][:].rearrange("p (h b) -> p h b", h=8), B0[:], ALU.add, ['ps6', 'B0'], ['s1'])
            TT(S, 'pool', s1[:], s1[:], penband[:, 60 - 4 * ti:124 - 4 * ti].unsqueeze(1).to_broadcast([128, 8, 64]), ALU.add,
               ['s1', 'penband'], ['s1'])
            RED(S, mx8[:], s1[:], ALU.max, ['s1'], ['mx8'])
            TT(S, 'dve', s1[:], s1[:], mx8[:].unsqueeze(2).to_broadcast([128, 8, 64]), ALU.subtract, ['s1', 'mx8'], ['s1'])
            ACT(S, s1[:], s1[:], AF.Exp, ['s1'], ['s1'])
            TT(S, 'pool', s1[:], s1[:], m01band[:, 60 - 4 * ti:124 - 4 * ti].unsqueeze(1).to_broadcast([128, 8, 64]), ALU.mult,
               ['s1', 'm01band'], ['s1'])
            RED(S, den8[:], s1[:], ALU.add, ['s1'], ['den8'])
            TS(S, 'dve', den8[:], den8[:], 1e-30, None, ALU.max, ALU.bypass, ['den8'], ['den8'])
            S.op('dve', lambda e: e.reciprocal(den8[:], den8[:]), ['den8'], ['den8'])
            TT(S, 'dve', s1[:], s1[:], den8[:].unsqueeze(2).to_broadcast([128, 8, 64]), ALU.mult, ['s1', 'den8'], ['s1'])
            for hb in range(2):
                pb, kp = ps[2 + hb], 'ps%d' % (2 + hb)
                for h4 in range(4):
                    TR(S, pb[:64, h4 * 128:(h4 + 1) * 128], s1[:, hb * 4 + h4, :], ident, ['s1'], [kp])
                CP(S, 'act', pcT[:, hb * 4:hb * 4 + 4, :], pb[:64, :].rearrange("p (h t) -> p h t", h=4), [kp], ['pcT'])
            for hd in range(8):
                kv = hd // 4
                MM(S, ps[7][:, hd * 64:(hd + 1) * 64], pcT[:, hd, :], vc[:, kv * 64:(kv + 1) * 64], True, True,
                   ['pcT', 'vc'], ['ps7'])
            TT(S, 'dve', v8(acc[:]), v8(ps[7][:]), g3[:, :, 0:1].to_broadcast([128, 8, 64]), ALU.mult,
               ['ps7', 'gates'], ['nacc'])
            nk = (ti + 1) * 128
            if ti >= 8:
                RED(S, t8[:], s1[:].rearrange("p h (s w) -> p h s w", w=2), ALU.add, ['s1'], ['t8'])
                RED(S, imp[:], t8[:].rearrange("p (k g) s -> p k s g", g=4), ALU.add, ['t8'], ['imp'])
                TT(S, 'dve', imp[:], imp[:], adj[:, 30 - 2 * ti:62 - 2 * ti].unsqueeze(1).to_broadcast([128, 2, 32]), ALU.add,
                   ['imp', 'adj'], ['imp'])
                MEMSET(S, 'dve', imp[:, :, 0:1], 1.0e9, ['imp'])
                for kv in range(2):
                    S.op('dve', lambda e: e.max(out=m8a[:], in_=imp[:, kv, :]), ['imp'], ['m8a'])
                    S.op('dve', lambda e: e.match_replace(out=sc2[:, kv, :], in_to_replace=m8a[:], in_values=imp[:, kv, :],
                                                          imm_value=-3.0e9), ['imp', 'm8a'], ['sc2'])
                    S.op('dve', lambda e: e.max(out=m8b[:], in_=sc2[:, kv, :]), ['sc2'], ['m8b'])
                    TS(S, 'dve', sel01[:, kv, :], imp[:, kv, :], m8b[:, 7:8], None, ALU.is_ge, ALU.bypass, ['imp', 'm8b'],
                       ['sel01'])
                    pk = pens[kv]
                    nb = 2 * (ti + 1)
                    TS(S, 'pool' if kv == 0 else 'dve', pk[:, :nk].rearrange("p (b s) -> p b s", s=64),
                       sel01[:, kv, :nb].unsqueeze(2).to_broadcast([128, nb, 64]), 1.0e30, NEG, ALU.mult, ALU.add,
                       ['sel01'], ['pens%d' % kv])
                    TT(S, 'dve', pk[:, ti * 128:nk], pk[:, ti * 128:nk], causalpen[:], ALU.add,
                       ['pens%d' % kv, 'causalpen'], ['pens%d' % kv])
                penk = lambda kv: (pens[kv][:, :nk], ['pens%d' % kv])
            else:
                MEMSET(S, 'pool', pens[0][:, :nk], 0.0, ['pens0'])
                CP(S, 'pool', pens[0][:, ti * 128:nk], causalpen[:], ['causalpen', 'pens0'], ['pens0'])
                penk = lambda kv: (pens[0][:, :nk], ['pens0'])
            wt0 = max(0, ti - 4)
            wc0 = (wt0 - (ti - 4)) * 128
            wnk = (ti - wt0 + 1) * 128
            tasks = []
            for hd in range(8):
                pa, pkeys = penk(hd // 4)
                tasks.append((hd, 0, ti, 0, 0, pa, pkeys, ps[4], 'ps4', (dens, 'dens')))
                tasks.append((hd, wt0, ti, 1, 1, penW[:, wc0:wc0 + wnk], ['penW'], ps[5], 'ps5', (denw, 'denw')))
            att_p1(0, *tasks[0])
            for k_, tk in enumerate(tasks):
                if k_ + 1 < len(tasks):
                    att_p1((k_ + 1) % 2, *tasks[k_ + 1])
                att_p2(k_ % 2, *tk)
            for (dn, kd, bank, kb, j) in ((dens, 'dens', ps[4], 'ps4', 1), (denw, 'denw', ps[5], 'ps5', 2)):
                S.op('dve', lambda e: e.reciprocal(dn[:], dn[:]), [kd], [kd])
                TT(S, 'dve', g2[:], dn[:], g3[:, :, j], ALU.mult, [kd, 'gates'], ['g2'])
                TT(S, 'dve', v8(tmpo[:]), v8(bank[:]), g2[:].unsqueeze(2).to_broadcast([128, 8, 64]), ALU.mult,
                   [kb, 'g2'], ['ntmpo'])
                TT(S, 'pool', acc[:], acc[:], tmpo[:], ALU.add, ['nacc', 'ntmpo'], ['nacc'])
            TT(S, 'dve', acc[:], acc[:], sgate[:], ALU.mult, ['nacc', 'sgate'], ['nacc'])
            S.dma('sp', br[r0:r0 + 128, 1024:1536], acc[:], reads=['nacc'], writes=[('br', 2, ti)])
def phase_nsa_sample(C, l):
    S, nc, I, O, ps, ident, z, br = C.S, C.nc, C.I, C.O, C.ps, C.ident, C.z, C.br
    cache = I['cache%d' % l]
    PAST = NPAGES * PAGE
    with Phase(C) as ph:
        pool4 = ph.T('spool4', [128, 4]); pband = ph.T('spband', [128, 252])
        MEMSET(S, 'pool', pool4[:], 1.0 / 32, ['pool4'])
        ASEL(S, pool4[:], [[-32, 4]], 0, 1, ['pool4'])
        ASEL(S, pool4[:], [[32, 4]], 31, -1, ['pool4'])
        MEMSET(S, 'pool', pband[:], 1.0 / 32, ['pband'])
        ASEL(S, pband[:], [[-32, 252]], 3968, 1, ['pband'])
        ASEL(S, pband[:], [[32, 252]], -3937, -1, ['pband'])
        pidx = ph.T('pidx', [128, 1])
        IOTA(S, pidx[:], [[0, 1]], 0, 1, ['pidx'])
        G = ph.T('Gm', [16, 4]); GT = ph.T('GTm', [4, 16])
        MEMSET(S, 'pool', G[:], 0.0, ['G']); MEMSET(S, 'pool', GT[:], 0.0, ['GT'])
        for g in range(4):
            ASEL(S, G[:], [[-1, 4]], -4 * g, 1, ['G'], op=ALU.not_equal, fill=1.0)
            ASEL(S, GT[:], [[1, 16]], -4 * g, -1, ['GT'], op=ALU.not_equal, fill=1.0)
        slrow = ph.T('slrow', [1, 2, 16]); one11 = ph.T('one11', [1, 1]); slr = ph.T('slr', [16, 2])
        for kv in range(2):
            for g in range(4):
                MEMSET(S, 'dve', slrow[:, kv, g * 4:(g + 1) * 4], SLOPES[kv * 4 + g], ['slrow'])
        MEMSET(S, 'dve', one11[:], 1.0, ['one11'])
        for kv in range(2):
            MM(S, ps[0][:16, kv:kv + 1], slrow[:, kv, :], one11[:], True, True, ['slrow', 'one11'], ['ps0'])
        CP(S, 'dve', slr[:], ps[0][:16, 0:2], ['ps0'], ['slr'])
        CM = ph.T('CMm', [4, 4]); CMW = ph.T('CMW', [4, 512]); penw = ph.T('penw', [16, 516]); pennew = ph.T('pennew', [16, 4])
        MEMSET(S, 'pool', CM[:], 0.0, ['CM']); ASEL(S, CM[:], [[-1, 4]], 0, 1, ['CM'], fill=NEG)
        MEMSET(S, 'pool', CMW[:], 0.0, ['CMW']); ASEL(S, CMW[:], [[1, 512]], 0, -1, ['CMW'], fill=NEG)
        MM(S, ps[1][:16, 0:512], GT[:], CMW[:], True, True, ['GT', 'CMW'], ['ps1'])
        MM(S, ps[2][:16, 0:4], GT[:], CM[:], True, True, ['GT', 'CM'], ['ps2'])
        CP(S, 'dve', penw[:, 0:512], ps[1][:16, 0:512], ['ps1'], ['penw'])
        CP(S, 'dve', penw[:, 512:516], ps[2][:16, 0:4], ['ps2'], ['penw'])
        CP(S, 'dve', pennew[:], ps[2][:16, 0:4], ['ps2'], ['pennew'])
        posl = ph.T('posl', [16, 2048]); blk32 = ph.T('blk32', [16, 512])
        IOTA(S, posl[:], [[1, 2048]], 0, 0, ['posl'])
        IOTA(S, blk32[:], [[32, 512]], 0, 0, ['blk32'])
        KsT = ph.T('KsT', [128, PAST], BF16); Vs = ph.T('Vs', [128, NPAGES, 128], BF16)
        kcT = ph.T('skcT', [128, 512], BF16); vcs = ph.T('svc', [128, 4, 128], BF16)
        pgb = [ph.T('pgb%d' % i, [128, 512]) for i in range(3)]
        pti = ph.T('pti', [128, NPAGES], I32); ptf = ph.T('ptf', [128, NPAGES]); idx = ph.T('idx', [128, NPAGES], I32)
        qz = ph.T('sqz', [4, 512]); gz = ph.T('sgz', [4, 536]); kvz = ph.T('skvz', [4, 768]); q2 = ph.T('sq2', [4, 4, 128])
        qT2 = ph.T('sqT2', [128, 16], BF16); knT = ph.T('sknT', [128, 2, 4], BF16); vn = ph.T('svn', [4, 2, 128], BF16)
        wbuf = ph.T('swbuf', [128, 4, 256]); KwT = ph.T('sKwT', [128, 516], BF16); Vw = ph.T('sVw', [128, 4, 128], BF16)
        gsig = ph.T('sgsig', [4, 24]); sgate = ph.T('ssgate', [4, 512])
        sc = ph.T('ssc', [16, 2048]); bias_c = ph.T('sbias', [16, 2048]); pex = ph.T('spex', [16, 2048])
        mx = ph.T('smx', [16, 1]); rmx = ph.T('srmx', [16, 1]); den = ph.T('sden', [16, 1]); dc = ph.T('sdc', [16, 1])
        pn = ph.T('spn', [16, 512]); imp = ph.T('simp', [4, 256]); isc = ph.T('sisc', [4, 256])
        m8a = ph.T('sm8a', [4, 8]); m8b = ph.T('sm8b', [4, 8]); sel = ph.T('ssel', [4, 256]); sel16 = ph.T('ssel16', [16, 256])
        pT = ph.T('spT', [128, 16, 16], BF16); pnT = ph.T('spnT', [4, 16], BF16)
        osb = [ph.T('sosb%d' % j, [16, 64]) for j in range(3)]
        snew = ph.T('ssnew', [16, 4]); pnew = ph.T('spnew', [16, 4]); biasw = ph.T('sbiasw', [16, 512])
        acc = ph.T('sacc', [4, 512]); tmpo = ph.T('stmpo', [4, 512])
        kpg = 0
        for b in range(NS_B):
            S.dma('sp', pti[:], I['ptab'][b:b + 1, :].partition_broadcast(128), writes=['pti'])
            CP(S, 'dve', ptf[:], pti[:], ['pti'], ['ptf'])
            STT(S, ptf[:], ptf[:], float(PAGE), pidx[:, 0:1].to_broadcast([128, NPAGES]), ALU.mult, ALU.add,
                ['ptf', 'pidx'], ['ptf'])
            CP(S, 'dve', idx[:], ptf[:], ['ptf'], ['idx'])
            for pg in range(NPAGES):
                pb_, kpb = pgb[kpg % 3], 'pgb%d' % (kpg % 3)
                S.dma('pool', None, None, reads=['idx'], writes=[kpb],
                      fn=lambda e: e.indirect_dma_start(out=pb_[:], out_offset=None, in_=cache,
                                                        in_offset=bass.IndirectOffsetOnAxis(ap=idx[:, pg:pg + 1], axis=0)))
                MM(S, ps[0][:, pg * 4:pg * 4 + 4], pb_[:, 0:128], pool4[:], True, True, [kpb, 'pool4'], ['ps0'])
                j32 = pg % 32
                MM(S, ps[1][:, :128], pband[:, 124 - 4 * j32:252 - 4 * j32], pb_[:, 128:256], j32 == 0, j32 == 31,
                   [kpb, 'pband'], ['ps1'])
                if j32 == 31:
                    CP(S, 'act', vcs[:, pg // 32, :], ps[1][:, :128], ['ps1'], ['vcs'])
                pi = 2 + (pg // 4) % 2
                TR(S, ps[pi][:, (pg % 4) * 128:(pg % 4 + 1) * 128], pb_[:, 256:384], ident, [kpb], ['ps%d' % pi])
                if pg % 4 == 3:
                    CP(S, 'dve', KsT[:, (pg - 3) * 128:(pg + 1) * 128], ps[pi][:, :], ['ps%d' % pi], ['KsT'])
                CP(S, 'act' if pg % 2 == 0 else 'pool', Vs[:, pg, :], pb_[:, 384:512], [kpb], ['Vs'])
                kpg += 1
            CP(S, 'dve', kcT[:], ps[0][:, :], ['ps0'], ['kcT'])
            rows = slice(T + b * NS_T, T + (b + 1) * NS_T)
            S.dma('sp', qz[:], z[rows, O_NQ:O_NQ + 512], writes=['qz'])
            S.dma('act', gz[:], z[rows, O_NBG:O_NBG + 536], writes=['gz'])
            S.dma('sp', kvz[:], z[rows, O_NKV:O_NKV + 768], writes=['kvz'])
            S.dma('act', wbuf[:], I['win'][l, b].rearrange("(a p) c -> p a c", p=128), writes=['wbuf'])
            CP(S, 'dve', q2[:].rearrange("p g (k d) -> p g k d", k=2), qz[:].rearrange("p (k g d) -> p g k d", k=2, g=4),
               ['qz'], ['q2'])
            for g in range(4):
                TR(S, ps[2][:, g * 4:(g + 1) * 4], q2[:, g, :], ident, ['q2'], ['ps2'])
            TS(S, 'dve', qT2[:], ps[2][:, 0:16], 0.125, None, ALU.mult, ALU.bypass, ['ps2'], ['qT2'])
            TR(S, ps[3][:, 0:4], kvz[:, 256:384], ident, ['kvz'], ['ps3'])
            TR(S, ps[3][:, 4:8], kvz[:, 512:640], ident, ['kvz'], ['ps3'])
            CP(S, 'dve', knT[:].rearrange("p a t -> p (a t)"), ps[3][:, 0:8], ['ps3'], ['knT'])
            CP(S, 'act', vn[:, 0, :], kvz[:, 384:512], ['kvz'], ['vn'])
            CP(S, 'act', vn[:, 1, :], kvz[:, 640:768], ['kvz'], ['vn'])
            for a in range(4):
                TR(S, ps[2][:, a * 128:(a + 1) * 128], wbuf[:, a, 0:128], ident, ['wbuf', 'qT2'], ['ps2'])
            CP(S, 'dve', KwT[:, 0:512], ps[2][:, :], ['ps2'], ['KwT'])
            CP(S, 'dve', KwT[:, 512:516], knT[:, 1, :], ['knT', 'KwT'], ['KwT'])
            CP(S, 'pool', Vw[:], wbuf[:, :, 128:256], ['wbuf'], ['Vw'])
            ACT(S, gsig[:], gz[:, 0:24], AF.Sigmoid, ['gz'], ['gsig'])
            ACT(S, sgate[:], gz[:, 24:536], AF.Silu, ['gz'], ['sgate'])
            for kv in range(2):
                p0 = kv * 64
                lq = qT2[p0:p0 + 64, :]
                sl = slr[:, kv:kv + 1]
                MM(S, ps[1][:16, :], lq, kcT[p0:p0 + 64, :], True, True, ['qT2', 'kcT'], ['ps1'])
                STT(S, sc[:, :512], blk32[:], sl, ps[1][:16, :], ALU.mult, ALU.add, ['blk32', 'slr', 'ps1'], ['sc'])
                RED(S, mx[:], sc[:, :512], ALU.max, ['sc'], ['mx'])
                TS(S, 'dve', mx[:], mx[:], -1.0, None, ALU.mult, ALU.bypass, ['mx'], ['mx'])
                ACT(S, pn[:], sc[:, :512], AF.Exp, ['sc', 'mx'], ['pn', 'den'], bias=mx[:, 0:1], accum_out=den[:, 0:1])
                S.op('dve', lambda e: e.reciprocal(den[:], den[:]), ['den'], ['den'])
                TS(S, 'dve', pn[:], pn[:], den[:, 0:1], None, ALU.mult, ALU.bypass, ['pn', 'den'], ['pn'])
                MM(S, ps[2][:4, :], G[:], pn[:], True, True, ['G', 'pn'], ['ps2'])
                RED(S, imp[:], ps[2][:4, :].rearrange("p (s w) -> p s w", w=2), ALU.add, ['ps2'], ['imp'])
                for a in range(4):
                    TR(S, ps[3][:, a * 16:(a + 1) * 16], pn[:, a * 128:(a + 1) * 128], ident, ['pn'], ['ps3'])
                CP(S, 'act', pT[:, 0:4, :].rearrange("p a r -> p (a r)"), ps[3][:, 0:64], ['ps3'], ['pT'])
                for a in range(4):
                    MM(S, ps[4][:16, 0:64], pT[:, a, :], vcs[:, a, p0:p0 + 64], a == 0, a == 3, ['pT', 'vcs'], ['ps4'])
                CP(S, 'dve', osb[0][:], ps[4][:16, 0:64], ['ps4'], ['osb0'])
                MEMSET(S, 'dve', imp[:, 0:1], -1.0, ['imp'])
                S.op('dve', lambda e: e.max(out=m8a[:], in_=imp[:]), ['imp'], ['m8a'])
                S.op('dve', lambda e: e.match_replace(out=isc[:], in_to_replace=m8a[:], in_values=imp[:], imm_value=-3.0),
                     ['imp', 'm8a'], ['isc'])
                S.op('dve', lambda e: e.max(out=m8b[:], in_=isc[:]), ['isc'], ['m8b'])
                TS(S, 'dve', sel[:], imp[:], m8b[:, 5:6], None, ALU.is_ge, ALU.bypass, ['imp', 'm8b'], ['sel'])
                MEMSET(S, 'dve', sel[:, 0:1], 1.0, ['sel'])
                MM(S, ps[2][:16, 0:256], GT[:], sel[:], True, True, ['GT', 'sel'], ['ps2'])
                CP(S, 'dve', sel16[:], ps[2][:16, 0:256], ['ps2'], ['sel16'])
                MM(S, ps[1][:16, 0:4], lq, knT[p0:p0 + 64, 0, :], True, True, ['qT2', 'knT'], ['ps1'])
                STT(S, snew[:], posl[:, 0:4], sl, ps[1][:16, 0:4], ALU.mult, ALU.add, ['posl', 'slr', 'ps1'], ['snew'])
                TT(S, 'dve', snew[:], snew[:], pennew[:], ALU.add, ['snew', 'pennew'], ['snew'])
                RED(S, rmx[:], snew[:], ALU.max, ['snew'], ['rmx'])

                def scores(c):
                    TS(S, 'dve', bias_c[:], posl[:], float(2048 * c - PAST), sl, ALU.add, ALU.mult, ['posl', 'slr'], ['bias_c'])
                    TS(S, 'pool', pex[:].rearrange("p (b s) -> p b s", s=64),
                       sel16[:, c * 32:(c + 1) * 32].unsqueeze(2).to_broadcast([16, 32, 64]), 1.0e30, NEG, ALU.mult, ALU.add,
                       ['sel16'], ['pex'])
                    TT(S, 'dve', bias_c[:], bias_c[:], pex[:], ALU.add, ['bias_c', 'pex'], ['bias_c'])
                    for c4 in range(4):
                        pb2, kp2 = ps[1 + c4 % 2], 'ps%d' % (1 + c4 % 2)
                        MM(S, pb2[:16, :], lq, KsT[p0:p0 + 64, c * 2048 + c4 * 512:c * 2048 + (c4 + 1) * 512], True, True,
                           ['qT2', 'KsT'], [kp2])
                        TT(S, 'dve', sc[:, c4 * 512:(c4 + 1) * 512], pb2[:16, :], bias_c[:, c4 * 512:(c4 + 1) * 512], ALU.add,
                           [kp2, 'bias_c'], ['sc'])

                for c in range(8):
                    scores(c)
                    RED(S, mx[:], sc[:], ALU.max, ['sc'], ['mx'])
                    TT(S, 'dve', rmx[:], rmx[:], mx[:], ALU.max, ['rmx', 'mx'], ['rmx'])
                TS(S, 'dve', rmx[:], rmx[:], -1.0, None, ALU.mult, ALU.bypass, ['rmx'], ['rmx'])
                ACT(S, pnew[:], snew[:], AF.Exp, ['snew', 'rmx'], ['pnew', 'den'], bias=rmx[:, 0:1], accum_out=den[:, 0:1])
                for c in range(8):
                    scores(c)
                    ACT(S, pex[:], sc[:], AF.Exp, ['sc', 'rmx'], ['pex', 'dc'], bias=rmx[:, 0:1], accum_out=dc[:, 0:1])
                    TT(S, 'dve', den[:], den[:], dc[:], ALU.add, ['den', 'dc'], ['den'])
                    for j in range(16):
                        TR(S, ps[3][:, j * 16:(j + 1) * 16], pex[:, j * 128:(j + 1) * 128], ident, ['pex'], ['ps3'])
                    CP(S, 'act', pT[:].rearrange("p a r -> p (a r)"), ps[3][:, 0:256], ['ps3'], ['pT'])
                    for j in range(16):
                        MM(S, ps[4][:16, 0:64], pT[:, j, :], Vs[:, c * 16 + j, p0:p0 + 64], c == 0 and j == 0, False,
                           ['pT', 'Vs'], ['ps4'])
                TR(S, ps[3][:4, 0:16], pnew[:], ident, ['pnew'], ['ps3'])
                CP(S, 'act', pnT[:], ps[3][:4, 0:16], ['ps3'], ['pnT'])
                MM(S, ps[4][:16, 0:64], pnT[:], vn[:, 0, p0:p0 + 64], False, True, ['pnT', 'vn'], ['ps4'])
                S.op('dve', lambda e: e.reciprocal(den[:], den[:]), ['den'], ['den'])
                TS(S, 'dve', osb[1][:], ps[4][:16, 0:64], den[:, 0:1], None, ALU.mult, ALU.bypass, ['ps4', 'den'], ['osb1'])
                TS(S, 'dve', biasw[:], posl[:, 0:512], -512.0, sl, ALU.add, ALU.mult, ['posl', 'slr'], ['biasw'])
                MM(S, ps[1][:16, :], lq, KwT[p0:p0 + 64, 0:512], True, True, ['qT2', 'KwT'], ['ps1'])
                MM(S, ps[2][:16, 0:4], lq, KwT[p0:p0 + 64, 512:516], True, True, ['qT2', 'KwT'], ['ps2'])
                TT(S, 'dve', sc[:, 0:512], ps[1][:16, :], biasw[:], ALU.add, ['ps1', 'biasw'], ['sc'])
                STT(S, sc[:, 512:516], posl[:, 0:4], sl, ps[2][:16, 0:4], ALU.mult, ALU.add, ['posl', 'slr', 'ps2', 'sc'], ['sc'])
                TT(S, 'dve', sc[:, 0:516], sc[:, 0:516], penw[:], ALU.add, ['sc', 'penw'], ['sc'])
                RED(S, mx[:], sc[:, 0:516], ALU.max, ['sc'], ['mx'])
                TS(S, 'dve', mx[:], mx[:], -1.0, None, ALU.mult, ALU.bypass, ['mx'], ['mx'])
                ACT(S, pex[:, 0:516], sc[:, 0:516], AF.Exp, ['sc', 'mx'], ['pex', 'den'], bias=mx[:, 0:1], accum_out=den[:, 0:1])
                for a in range(4):
                    TR(S, ps[3][:, a * 16:(a + 1) * 16], pex[:, a * 128:(a + 1) * 128], ident, ['pex'], ['ps3'])
                TR(S, ps[3][:4, 64:80], pex[:, 512:516], ident, ['pex'], ['ps3'])
                CP(S, 'act', pT[:, 0:4, :].rearrange("p a r -> p (a r)"), ps[3][:, 0:64], ['ps3'], ['pT'])
                CP(S, 'act', pnT[:], ps[3][:4, 64:80], ['ps3'], ['pnT'])
                for a in range(4):
                    MM(S, ps[4][:16, 0:64], pT[:, a, :], Vw[:, a, p0:p0 + 64], a == 0, False, ['pT', 'Vw'], ['ps4'])
                MM(S, ps[4][:16, 0:64], pnT[:], vn[:, 1, p0:p0 + 64], False, True, ['pnT', 'vn'], ['ps4'])
                S.op('dve', lambda e: e.reciprocal(den[:], den[:]), ['den'], ['den'])
                TS(S, 'dve', osb[2][:], ps[4][:16, 0:64], den[:, 0:1], None, ALU.mult, ALU.bypass, ['ps4', 'den'], ['osb2'])
                for j in range(3):
                    for g in range(4):
                        hd = kv * 4 + g
                        MM(S, ps[5 + j][:4, hd * 64:(hd + 1) * 64], ident[:16, g * 4:(g + 1) * 4], osb[j][:], True, True,
                           ['ident', 'osb%d' % j], ['ps%d' % (5 + j)])
            g3 = gsig[:].rearrange("p (h j) -> p h j", j=3)
            v8 = lambda t: t.rearrange("p (h d) -> p h d", h=8)
            for j in range(3):
                dst = acc if j == 0 else tmpo
                TT(S, 'dve', v8(dst[:]), v8(ps[5 + j][:4, :]), g3[:, :, j:j + 1].to_broadcast([4, 8, 64]), ALU.mult,
                   ['ps%d' % (5 + j), 'gsig'], ['sacc' if j == 0 else 'stmpo'])
                if j > 0:
                    TT(S, 'dve', acc[:], acc[:], tmpo[:], ALU.add, ['sacc', 'stmpo'], ['sacc'])
            TT(S, 'dve', acc[:], acc[:], sgate[:], ALU.mult, ['sacc', 'sgate'], ['sacc'])
            S.dma('sp', br[rows, 1024:1536], acc[:], reads=['sacc'], writes=[('br', 3, b)])


HAVE_NSA_SAMPLE = True


def build_program(debug=False):
    nc = bass.Bass("TRN2", target_bir_lowering=False)
    C = Ctx()
    C.nc = nc
    C.uid = 0
    dt_in = lambda name, shape, dt=F32: nc.dram_tensor(name, shape, dt, kind="ExternalInput").ap()
    dt_out = lambda name, shape, dt=F32: nc.dram_tensor(name, shape, dt, kind="ExternalOutput").ap()
    dt_scr = lambda name, shape, dt=F32: nc.dram_tensor(name, shape, dt, kind="Internal").ap()
    I = {}
    I['x'] = dt_in("x", [NTOK, D])
    I['w_in'] = dt_in("w_in", [DEPTH, D, INW])
    I['b_in'] = dt_in("b_in", [DEPTH, INW])
    I['win'] = dt_in("win", [DEPTH, NS_B, WINB, 256])
    I['shift'] = dt_in("shift", [DEPTH, NS_B, RWIN])
    I['gla_st'] = dt_in("gla_st", [DEPTH, NS_B, 4, 64, 128])
    I['rw_st'] = dt_in("rw_st", [DEPTH, NS_B, 8, 64, 64])
    I['gla_a_up'] = dt_in("gla_a_up", [DEPTH, 16, 256])
    I['gla_a_bias'] = dt_in("gla_a_bias", [DEPTH, 256])
    I['gla_norm'] = dt_in("gla_norm", [DEPTH, 512])
    I['rwkv_mu'] = dt_in("rwkv_mu", [DEPTH, RWIN])
    for nm in ('rwkv_w0', 'rwkv_a0', 'rwkv_k_k', 'rwkv_k_a', 'rwkv_r_k', 'rwkv_ln_w', 'rwkv_ln_b'):
        I[nm] = dt_in(nm, [DEPTH, 512])
    I['rwkv_w_up'] = dt_in("rwkv_w_up", [DEPTH, 64, 512])
    I['rwkv_a_up'] = dt_in("rwkv_a_up", [DEPTH, 64, 512])
    I['w_br'] = dt_in("w_br", [DEPTH, 3, 512, D])
    I['w_out'] = dt_in("w_out", [DEPTH, D, D])
    I['ln_g'] = dt_in("ln_g", [DEPTH, D])
    I['ln_b'] = dt_in("ln_b", [DEPTH, D])
    I['ptab'] = dt_in("ptab", [NS_B, NPAGES], I32)
    if HAVE_NSA_SAMPLE:
        for ll in range(DEPTH):
            I['cache%d' % ll] = dt_in("cache%d" % ll, [NPOOL * PAGE, 512])
    O = {}
    O['y'] = dt_out("y", [NTOK, D])
    O['kv'] = dt_out("kv", [DEPTH, NTOK, 512])
    O['winp'] = dt_out("winp", [DEPTH, WINB, 256])
    O['wins'] = dt_out("wins", [DEPTH, NS_B, WINB, 256])
    O['shp'] = dt_out("shp", [DEPTH, RWIN])
    O['shs'] = dt_out("shs", [DEPTH, NS_B, RWIN])
    O['glap'] = dt_out("glap", [DEPTH, 4, 64, 128])
    O['glas'] = dt_out("glas", [DEPTH, NS_B, 4, 64, 128])
    O['rwp'] = dt_out("rwp", [DEPTH, 8, 64, 64])
    O['rws'] = dt_out("rws", [DEPTH, NS_B, 8, 64, 64])
    C.I, C.O = I, O
    C.z = dt_scr("z", [NTOK, INW])
    C.xs = dt_scr("xs", [NTOK, D])
    if debug:
        C.br = dt_out("br", [NTOK, 1536])
    else:
        C.br = dt_scr("br", [NTOK, 1536])
    C.sg_scr = dt_scr("sg_scr", [NTOK, 512])
    C.bv_scr = dt_scr("bv_scr", [NTOK, 512])
    C.y_scr = dt_scr("y_scr", [NTOK, 512])
    C.rk_scr = dt_scr("rk_scr", [NTOK, 3, 512], BF16)
    C.fm_scr = dt_scr("fm_scr", [3, 64, 8, NTOK])
    S = Sched(nc)
    C.S = S
    with contextlib.ExitStack() as gst:
        C.ps = [gst.enter_context(nc.psum_tensor("ps%d" % i, [128, 512], F32)) for i in range(8)]
        C.ident = gst.enter_context(nc.sbuf_tensor("ident", [128, 128], F32))
        zt = gst.enter_context(nc.sbuf_tensor("zerot", [128, 512], F32))
        MEMSET(S, 'pool', C.ident[:], 1.0, ['ident'])
        ASEL(S, C.ident[:], [[-1, 128]], 0, 1, ['ident'], op=ALU.is_equal)
        MEMSET(S, 'dve', zt[:], 0.0, ['zerot'])
        nl = 1 if debug else DEPTH
        for l in range(nl):
            xsrc = I['x'] if l == 0 else C.xs
            xdst = O['y'] if l == DEPTH - 1 else C.xs
            phase_inproj(C, l, xsrc)
            direct_outputs(C, l)
            phase_gla(C, l)
            phase_rwkv_pre(C, l)
            phase_rwkv_scan(C, l)
            phase_rwkv_post(C, l)
            phase_nsa_prompt(C, l)
            if not HAVE_NSA_SAMPLE:
                S.dma('sp', C.br[T:NTOK, 1024:1536], zt[:NSAMP, :], reads=['zerot'])
                S.barrier()
            else:
                phase_nsa_sample(C, l)
            phase_merge(C, l, xsrc, xdst)
        S.finish()
    print("program built: n_inst", S.n_inst, flush=True)
    return nc


_NC_CACHE = {}


def kernel(x_prompt, x_sample, cache_nsa_kv, state_nsa_win, state_gla, state_rwkv, state_rwkv_shift,
           page_table, w_in, b_in, gla_a_up, gla_a_bias, gla_norm, rwkv_mu, rwkv_w0, rwkv_w_up,
           rwkv_a0, rwkv_a_up, rwkv_k_k, rwkv_k_a, rwkv_r_k, rwkv_ln_w, rwkv_ln_b, w_br, w_out, ln_g, ln_b,
           _debug=False, _cores=NCORES):
    f = lambda a: np.ascontiguousarray(np.asarray(a, dtype=np.float32))
    key = 'nc_dbg' if _debug else 'nc'
    if key not in _NC_CACHE:
        _NC_CACHE[key] = build_program(debug=_debug)
    nc = _NC_CACHE[key]
    shared = dict(w_in=f(w_in), b_in=f(b_in), gla_a_up=f(gla_a_up), gla_a_bias=f(gla_a_bias), gla_norm=f(gla_norm),
                  rwkv_mu=f(rwkv_mu), rwkv_w0=f(rwkv_w0), rwkv_a0=f(rwkv_a0), rwkv_k_k=f(rwkv_k_k), rwkv_k_a=f(rwkv_k_a),
                  rwkv_r_k=f(rwkv_r_k).reshape(DEPTH, 512), rwkv_ln_w=f(rwkv_ln_w), rwkv_ln_b=f(rwkv_ln_b),
                  rwkv_w_up=f(rwkv_w_up), rwkv_a_up=f(rwkv_a_up), w_br=f(w_br), w_out=f(w_out), ln_g=f(ln_g), ln_b=f(ln_b))
    if HAVE_NSA_SAMPLE:
        cache = np.asarray(cache_nsa_kv)
        for ll in range(DEPTH):
            shared['cache%d' % ll] = np.ascontiguousarray(cache[ll], dtype=np.float32).reshape(NPOOL * PAGE, 512)
    ptab = np.ascontiguousarray(np.asarray(page_table), dtype=np.int32)
    in_maps = []
    for c in range(_cores):
        bs = slice(c * NS_B, (c + 1) * NS_B)
        xx = np.concatenate([f(x_prompt[c]), f(x_sample[bs]).reshape(NSAMP, D)], axis=0)
        m = dict(shared)
        m.update({
            'x': xx,
            'win': f(state_nsa_win[:, bs]).reshape(DEPTH, NS_B, WINB, 256),
            'shift': f(state_rwkv_shift[:, bs]),
            'gla_st': f(state_gla[:, bs]),
            'rw_st': f(state_rwkv[:, bs]),
            'ptab': np.ascontiguousarray(ptab[bs]),
        })
        in_maps.append(m)
    res = run_bass_kernel_spmd(nc, in_maps, core_ids=list(range(_cores)))
    R = res.results
    if _debug:
        return R
    st = lambda key: [R[c][key] for c in range(NCORES)]
    y = st('y')
    y_prompt = np.stack([a[:T] for a in y], 0)
    y_sample = np.concatenate([a[T:].reshape(NS_B, NS_T, D) for a in y], 0)
    kv = st('kv')
    kv_p = np.stack([a[:, :T].reshape(DEPTH, T, 4, 2, 64) for a in kv], 1)
    kv_s = np.concatenate([a[:, T:].reshape(DEPTH, NS_B, NS_T, 4, 2, 64) for a in kv], 1)
    win_p = np.stack([a.reshape(DEPTH, WINB, 2, 2, 64) for a in st('winp')], 1)
    win_s = np.concatenate([a.reshape(DEPTH, NS_B, WINB, 2, 2, 64) for a in st('wins')], 1)
    gla_p = np.stack(st('glap'), 1)
    gla_s = np.concatenate(st('glas'), 1)
    rw_p = np.stack(st('rwp'), 1)
    rw_s = np.concatenate(st('rws'), 1)
    sh_p = np.stack(st('shp'), 1)
    sh_s = np.concatenate(st('shs'), 1)
    return (y_prompt, y_sample, kv_p, kv_s, win_p, win_s, gla_p, gla_s, rw_p, rw_s, sh_p, sh_s)
```
